# Optimizing a Trainium2 kernel written in Bass

```python
import jax, jax.numpy as jnp
from jax import lax
import numpy as np

D_MODEL = 2048
BATCH = 4
SEQ = 2048
DEPTH = 1

SSM_EXPAND = 2
SSM_D_INNER = SSM_EXPAND * D_MODEL
SSM_HEAD_DIM = 64
SSM_N_HEADS = SSM_D_INNER // SSM_HEAD_DIM
SSM_N_GROUPS = 8
SSM_D_STATE = 128
SSM_CONV = 4
SSM_CHUNK = 256
SSM_CONV_DIM = SSM_D_INNER + 2 * SSM_N_GROUPS * SSM_D_STATE
ATTN_HEAD_DIM = 64
ATTN_N_HEADS = D_MODEL // ATTN_HEAD_DIM
ATTN_N_KV = ATTN_N_HEADS // 8
ATTN_Q_PER_KV = ATTN_N_HEADS // ATTN_N_KV
WINDOW = 128
ATTN_BLOCK = 128
D_FF = 5632
EPS = 1e-6

IN_SIZES = (
    SSM_D_INNER,
    SSM_CONV_DIM,
    SSM_N_HEADS,
    ATTN_N_HEADS * ATTN_HEAD_DIM,
    ATTN_N_KV * ATTN_HEAD_DIM,
    ATTN_N_KV * ATTN_HEAD_DIM,
    D_MODEL,
    D_MODEL,
)
IN_COLS = int(sum(IN_SIZES))
IN_SPLITS = tuple(int(s) for s in np.cumsum(IN_SIZES)[:-1])

kernel_name = "hybrid_ssd_swa_macaron_block"


def rms_norm(x, w):
    xf = x.astype(jnp.float32)
    y = xf * lax.rsqrt(jnp.mean(xf * xf, axis=-1, keepdims=True) + EPS)
    return (y * w.astype(jnp.float32)).astype(x.dtype)


def swiglu(h, w_gate, w_up, w_down):
    return (jax.nn.silu(h @ w_gate) * (h @ w_up)) @ w_down


def causal_depthwise_conv(u, w, b):
    k = w.shape[0]
    out = lax.conv_general_dilated(
        u, w[:, None, :].astype(u.dtype), window_strides=(1,), padding=[(k - 1, 0)],
        dimension_numbers=("NWC", "WIO", "NWC"), feature_group_count=u.shape[-1])
    return out + b


def ssd_chunked(xs, dt, a, bmat, cmat):
    bsz, t_len, n_h, p_dim = xs.shape
    g, n = bmat.shape[2], bmat.shape[3]
    r = n_h // g
    L = SSM_CHUNK
    nc = -(-t_len // L)
    pad = nc * L - t_len
    xs, dt, bmat, cmat = (v.astype(jnp.float32) for v in (xs, dt, bmat, cmat))
    if pad:
        xs = jnp.pad(xs, ((0, 0), (0, pad), (0, 0), (0, 0)))
        dt = jnp.pad(dt, ((0, 0), (0, pad), (0, 0)))
        bmat = jnp.pad(bmat, ((0, 0), (0, pad), (0, 0), (0, 0)))
        cmat = jnp.pad(cmat, ((0, 0), (0, pad), (0, 0), (0, 0)))
    xdt = (xs * dt[..., None]).reshape(bsz, nc, L, g, r, p_dim)
    adt = (dt * a.astype(jnp.float32)).reshape(bsz, nc, L, g, r)
    a_cum = jnp.cumsum(jnp.transpose(adt, (0, 1, 3, 4, 2)), axis=-1)
    bc = bmat.reshape(bsz, nc, L, g, n)
    cc = cmat.reshape(bsz, nc, L, g, n)
    causal = jnp.asarray(np.tril(np.ones((L, L), dtype=bool)))
    seg = a_cum[..., :, None] - a_cum[..., None, :]
    cb = jnp.einsum("bclgn,bcsgn->bcgls", cc, bc)
    w_ls = cb[:, :, :, None] * jnp.exp(jnp.where(causal, seg, -jnp.inf))
    y_diag = jnp.einsum("bcgrls,bcsgrp->bclgrp", w_ls, xdt)
    decay_to_end = jnp.exp(a_cum[..., -1:] - a_cum)
    chunk_states = jnp.einsum("bclgn,bcgrl,bclgrp->bcgrpn", bc, decay_to_end, xdt)
    chunk_decay = jnp.exp(a_cum[..., -1])

    def step(state, inp):
        s_c, d_c = inp
        return state * d_c[..., None, None] + s_c, state

    init = jnp.zeros((bsz, g, r, p_dim, n), jnp.float32)
    _, prev = lax.scan(step, init, (jnp.moveaxis(chunk_states, 1, 0), jnp.moveaxis(chunk_decay, 1, 0)))
    prev = jnp.moveaxis(prev, 0, 1)
    y_off = jnp.einsum("bclgn,bcgrpn,bcgrl->bclgrp", cc, prev, jnp.exp(a_cum))
    y = (y_diag + y_off).reshape(bsz, nc * L, n_h, p_dim)
    return y[:, :t_len]


def ssd_branch(z, xbc, dt_raw, conv_w, conv_b, dt_bias, a_log, d_skip, ssm_norm):
    bsz, t_len, _ = z.shape
    xbc = jax.nn.silu(causal_depthwise_conv(xbc, conv_w, conv_b))
    gn = SSM_N_GROUPS * SSM_D_STATE
    xs = xbc[..., :SSM_D_INNER].reshape(bsz, t_len, SSM_N_HEADS, SSM_HEAD_DIM)
    bm = xbc[..., SSM_D_INNER:SSM_D_INNER + gn].reshape(bsz, t_len, SSM_N_GROUPS, SSM_D_STATE)
    cm = xbc[..., SSM_D_INNER + gn:].reshape(bsz, t_len, SSM_N_GROUPS, SSM_D_STATE)
    dt = jax.nn.softplus(dt_raw.astype(jnp.float32) + dt_bias.astype(jnp.float32))
    a = -jnp.exp(a_log.astype(jnp.float32))
    y = ssd_chunked(xs, dt, a, bm, cm) + d_skip.astype(jnp.float32)[:, None] * xs.astype(jnp.float32)
    y = y.reshape(bsz, t_len, SSM_D_INNER)
    yg = (y * jax.nn.silu(z.astype(jnp.float32))).reshape(bsz, t_len, SSM_N_GROUPS, SSM_D_INNER // SSM_N_GROUPS)
    yg = yg * lax.rsqrt(jnp.mean(yg * yg, axis=-1, keepdims=True) + EPS)
    return (yg.reshape(bsz, t_len, SSM_D_INNER) * ssm_norm.astype(jnp.float32)).astype(z.dtype)


def alibi_slopes(n_heads):
    return np.array([2.0 ** (-8.0 * (h + 1) / n_heads) for h in range(n_heads)], dtype=np.float32)


def swa_branch(q, k, v, q_norm, k_norm, sinks):
    bsz, t_len, _ = q.shape
    blk = ATTN_BLOCK
    nb = t_len // blk
    q = rms_norm(q.reshape(bsz, t_len, ATTN_N_KV, ATTN_Q_PER_KV, ATTN_HEAD_DIM), q_norm)
    k = rms_norm(k.reshape(bsz, t_len, ATTN_N_KV, ATTN_HEAD_DIM), k_norm)
    v = v.reshape(bsz, t_len, ATTN_N_KV, ATTN_HEAD_DIM)
    qb = q.reshape(bsz, nb, blk, ATTN_N_KV, ATTN_Q_PER_KV, ATTN_HEAD_DIM)
    zpad = ((0, 0), (blk, 0), (0, 0), (0, 0))
    kp = jnp.pad(k, zpad).reshape(bsz, nb + 1, blk, ATTN_N_KV, ATTN_HEAD_DIM)
    vp = jnp.pad(v, zpad).reshape(bsz, nb + 1, blk, ATTN_N_KV, ATTN_HEAD_DIM)
    kband = jnp.concatenate([kp[:, :-1], kp[:, 1:]], axis=2)
    vband = jnp.concatenate([vp[:, :-1], vp[:, 1:]], axis=2)
    scale = ATTN_HEAD_DIM ** -0.5
    s = jnp.einsum("bnqkgd,bnskd->bnkgqs", qb, kband).astype(jnp.float32) * scale
    qi = np.arange(blk)[:, None]
    sj = np.arange(2 * blk)[None, :]
    dist = (qi + blk - sj).astype(np.float32)
    key_pos = np.arange(nb)[:, None, None] * blk - blk + sj[None]
    valid = (dist >= 0) & (dist < WINDOW) & (key_pos >= 0)
    slopes = jnp.asarray(alibi_slopes(ATTN_N_HEADS)).reshape(ATTN_N_KV, ATTN_Q_PER_KV)
    s = s - slopes[:, :, None, None] * jnp.asarray(dist)
    s = jnp.where(jnp.asarray(valid)[None, :, None, None], s, -jnp.inf)
    sink = sinks.astype(jnp.float32).reshape(ATTN_N_KV, ATTN_Q_PER_KV)[None, None, :, :, None]
    m = jnp.maximum(jnp.max(s, axis=-1), sink)
    p = jnp.exp(s - m[..., None])
    denom = jnp.sum(p, axis=-1) + jnp.exp(sink - m)
    p = (p / denom[..., None]).astype(v.dtype)
    o = jnp.einsum("bnkgqs,bnskd->bnqkgd", p, vband)
    return o.reshape(bsz, t_len, ATTN_N_HEADS * ATTN_HEAD_DIM)


def hybrid_mixer(h, w_in, conv_w, conv_b, dt_bias, a_log, d_skip, ssm_norm,
                 q_norm, k_norm, sinks, w_o_ssm, w_o_attn, w_out):
    proj = h @ w_in
    z, xbc, dt_raw, q, k, v, g_ssm, g_attn = jnp.split(proj, IN_SPLITS, axis=-1)
    y_ssm = ssd_branch(z, xbc, dt_raw, conv_w, conv_b, dt_bias, a_log, d_skip, ssm_norm)
    y_attn = swa_branch(q, k, v, q_norm, k_norm, sinks)
    merged = jax.nn.sigmoid(g_ssm) * (y_ssm @ w_o_ssm) + jax.nn.sigmoid(g_attn) * (y_attn @ w_o_attn)
    return merged @ w_out


def setup_inputs(seed: int = 0) -> dict:
    key = jax.random.key(seed)
    ks = jax.random.split(key, 24)
    f32 = jnp.float32

    def normal(k, shape, fan_in):
        return jax.random.normal(k, shape, f32) * fan_in ** -0.5

    def gain(k, shape):
        return 1.0 + 0.02 * jax.random.normal(k, shape, f32)

    dL = DEPTH
    dt_init = jnp.exp(jax.random.uniform(ks[9], (dL, SSM_N_HEADS), f32, np.log(1e-3), np.log(1e-1)))
    return {
        "x": jax.random.normal(ks[0], (BATCH, SEQ, D_MODEL), f32),
        "ffn1_norm": gain(ks[1], (dL, D_MODEL)),
        "ffn1_w_gate": normal(ks[2], (dL, D_MODEL, D_FF), D_MODEL),
        "ffn1_w_up": normal(ks[3], (dL, D_MODEL, D_FF), D_MODEL),
        "ffn1_w_down": normal(ks[4], (dL, D_FF, D_MODEL), D_FF),
        "mix_norm": gain(ks[5], (dL, D_MODEL)),
        "w_in": normal(ks[6], (dL, D_MODEL, IN_COLS), D_MODEL),
        "conv_w": normal(ks[7], (dL, SSM_CONV, SSM_CONV_DIM), SSM_CONV),
        "conv_b": 0.02 * jax.random.normal(ks[8], (dL, SSM_CONV_DIM), f32),
        "dt_bias": dt_init + jnp.log(-jnp.expm1(-dt_init)),
        "a_log": jnp.log(jax.random.uniform(ks[10], (dL, SSM_N_HEADS), f32, 1.0, 16.0)),
        "d_skip": gain(ks[11], (dL, SSM_N_HEADS)),
        "ssm_norm": gain(ks[12], (dL, SSM_D_INNER)),
        "q_norm": gain(ks[13], (dL, ATTN_HEAD_DIM)),
        "k_norm": gain(ks[14], (dL, ATTN_HEAD_DIM)),
        "sinks": 0.5 * jax.random.normal(ks[15], (dL, ATTN_N_HEADS), f32),
        "w_o_ssm": normal(ks[16], (dL, SSM_D_INNER, D_MODEL), SSM_D_INNER),
        "w_o_attn": normal(ks[17], (dL, ATTN_N_HEADS * ATTN_HEAD_DIM, D_MODEL), ATTN_N_HEADS * ATTN_HEAD_DIM),
        "w_out": normal(ks[18], (dL, D_MODEL, D_MODEL), D_MODEL),
        "ffn2_norm": gain(ks[19], (dL, D_MODEL)),
        "ffn2_w_gate": normal(ks[20], (dL, D_MODEL, D_FF), D_MODEL),
        "ffn2_w_up": normal(ks[21], (dL, D_MODEL, D_FF), D_MODEL),
        "ffn2_w_down": normal(ks[22], (dL, D_FF, D_MODEL), D_FF),
    }


def reference(x, ffn1_norm, ffn1_w_gate, ffn1_w_up, ffn1_w_down, mix_norm, w_in, conv_w, conv_b,
              dt_bias, a_log, d_skip, ssm_norm, q_norm, k_norm, sinks, w_o_ssm, w_o_attn, w_out,
              ffn2_norm, ffn2_w_gate, ffn2_w_up, ffn2_w_down):
    for l in range(DEPTH):
        x = x + 0.5 * swiglu(rms_norm(x, ffn1_norm[l]), ffn1_w_gate[l], ffn1_w_up[l], ffn1_w_down[l])
        x = x + hybrid_mixer(rms_norm(x, mix_norm[l]), w_in[l], conv_w[l], conv_b[l], dt_bias[l],
                             a_log[l], d_skip[l], ssm_norm[l], q_norm[l], k_norm[l], sinks[l],
                             w_o_ssm[l], w_o_attn[l], w_out[l])
        x = x + 0.5 * swiglu(rms_norm(x, ffn2_norm[l]), ffn2_w_gate[l], ffn2_w_up[l], ffn2_w_down[l])
    return x
```

```python
import concourse.bass as bass
import concourse.mybir as mybir

ENGS = ("pe", "act", "dve", "pool", "sp")


class Res:
    __slots__ = ("name", "excl", "last_w", "readers")

    def __init__(self, name, excl=False):
        self.name = name
        self.excl = excl
        self.last_w = None
        self.readers = {}


class Op:
    __slots__ = ("eng", "idx", "fn", "deps", "dma_key", "dma_n", "signal", "name")


class Prog:
    def __init__(self, nc):
        self.nc = nc
        self.ops = []
        self.eng_ops = {e: [] for e in ENGS}
        self.dma_cnt = {}
        self.dma_last = {}

    def res(self, name, excl=False):
        return Res(name, excl)

    def _add(self, eng, fn, reads, writes, dma_key=None, name=None):
        op = Op()
        op.eng = eng
        op.fn = fn
        op.name = name
        op.dma_key = dma_key
        op.signal = False
        op.idx = len(self.eng_ops[eng])
        deps = set()
        is_dma = dma_key is not None
        for r in reads:
            if r.excl:
                continue
            if r.last_w is not None:
                deps.add(r.last_w)
        for w in list(writes) + [r for r in reads if r.excl]:
            if w.last_w is not None:
                deps.add(w.last_w)
            for e, rid in w.readers.items():
                deps.add(rid)
        gid = len(self.ops)
        final = set()
        raw_src = set()
        for r in reads:
            if r.last_w is not None:
                raw_src.add(r.last_w)
        for d in deps:
            dop = self.ops[d]
            if dop.dma_key is None and dop.eng == eng and not is_dma:
                if d in raw_src:
                    final.add(d)
            else:
                final.add(d)
        if is_dma:
            prev = self.dma_last.get(dma_key)
            if prev is not None:
                final.add(prev)
            n = self.dma_cnt.get(dma_key, 0) + 1
            self.dma_cnt[dma_key] = n
            op.dma_n = n
            self.dma_last[dma_key] = gid
        else:
            op.dma_n = 0
        op.deps = final
        self.ops.append(op)
        self.eng_ops[eng].append(gid)
        for r in reads:
            if r.excl:
                r.last_w = gid
                r.readers = {}
            else:
                r.readers[("dma", dma_key) if is_dma else eng] = gid
        for w in writes:
            w.last_w = gid
            w.readers = {}
        return gid

    def op(self, eng, fn, reads=(), writes=(), name=None):
        return self._add(eng, fn, list(reads), list(writes), None, name)

    def dma(self, queue, key, fn, reads=(), writes=(), name=None):
        return self._add(queue, fn, list(reads), list(writes), key, name)

    def emit(self, final_wait_ops=()):
        nc = self.nc
        ops = self.ops
        waits = {}
        for e in ENGS:
            known_c = {}
            known_d = {}
            for gid in self.eng_ops[e]:
                op = ops[gid]
                wl = []
                for d in sorted(op.deps):
                    dop = ops[d]
                    if dop.dma_key is not None:
                        if known_d.get(dop.dma_key, 0) >= dop.dma_n:
                            continue
                        known_d[dop.dma_key] = dop.dma_n
                        wl.append(d)
                    else:
                        if known_c.get(dop.eng, -1) >= dop.idx:
                            continue
                        known_c[dop.eng] = dop.idx
                        wl.append(d)
                        dop.signal = True
                waits[gid] = wl
        sig_rank = {}
        for e in ENGS:
            c = 0
            for gid in self.eng_ops[e]:
                op = ops[gid]
                if op.dma_key is None and op.signal:
                    c += 1
                    sig_rank[gid] = c
        self.n_waits = sum(len(v) for v in waits.values())
        self.n_sig = len(sig_rank)
        import contextlib
        with contextlib.ExitStack() as st:
            esem = {e: st.enter_context(nc.semaphore("s_" + e)) for e in ENGS}
            dsem = {k: st.enter_context(nc.semaphore("d_%s" % (k,))) for k in self.dma_cnt}
            block = st.enter_context(nc.Block())
            engobj = {"pe": "tensor", "act": "scalar", "dve": "vector", "pool": "gpsimd", "sp": "sync"}

            def run_engine(e):
                def body(eng):
                    for gid in self.eng_ops[e]:
                        op = ops[gid]
                        for d in waits[gid]:
                            dop = ops[d]
                            if dop.dma_key is not None:
                                eng.wait_ge(dsem[dop.dma_key], 16 * dop.dma_n)
                            else:
                                eng.wait_ge(esem[dop.eng], sig_rank[d])
                        ins = op.fn(eng)
                        if ins is None:
                            continue
                        if op.dma_key is not None:
                            ins.then_inc(dsem[op.dma_key], 16)
                        elif op.signal:
                            ins.then_inc(esem[e], 1)
                return body

            block.tensor(run_engine("pe"))
            block.scalar(run_engine("act"))
            block.vector(run_engine("dve"))
            block.gpsimd(run_engine("pool"))
            block.sync(run_engine("sp"))

import numpy as np
import contextlib
from concourse.bass_utils import run_bass_kernel_spmd

F32 = mybir.dt.float32
BF16 = mybir.dt.bfloat16
AF = mybir.ActivationFunctionType
ALU = mybir.AluOpType
T = 1024
NT = 8
EPS = 1e-6
NEG = -30000.0


def build_nc():
    nc = bass.Bass("TRN2", target_bir_lowering=False)
    di = {}

    def din(name, shape):
        di[name] = nc.dram_tensor(name, list(shape), F32, kind="ExternalInput").ap()
        return di[name]

    xT_own = din("xT_own", [2048, T]); xT_pre = din("xT_pre", [2048, T]); mflag_d = din("mflag", [128, 1])
    f_gu = [din("f1_gu", [44, 128, 2, 16, 128]), din("f2_gu", [44, 128, 2, 16, 128])]
    f_d = [din("f1_d", [11, 4, 128, 4, 4, 128]), din("f2_d", [11, 4, 128, 4, 4, 128])]
    n_f = [din("n_f1", [128, 16]), din("n_f2", [128, 16])]
    n_mix_d = din("n_mix", [128, 16])
    w_z = din("w_z", [32, 128, 16, 128]); w_xbc = din("w_xbc", [48, 128, 16, 128]); w_dt_d = din("w_dt", [128, 16, 64])
    w_q = din("w_q", [16, 128, 16, 128]); w_kv = din("w_kv", [4, 128, 16, 128])
    w_gs = din("w_gs", [16, 128, 16, 128]); w_ga = din("w_ga", [16, 128, 16, 128])
    conv_w_d = din("conv_w", [128, 48, 4]); conv_b_d = din("conv_b", [128, 48])
    dtb_d = din("dtb_bc", [128, 64]); alog_d = din("alog_bc", [128, 64]); dsk_d = din("dsk_bc", [128, 64])
    ssmn_d = din("ssm_n", [128, 32]); qn_d = din("qn", [128, 1]); kn_d = din("kn", [128, 1]); sinks_d = din("sinks_bc", [128, 32])
    w_os = din("w_os", [16, 128, 32, 128]); w_oa = din("w_oa", [16, 128, 16, 128]); w_out = din("w_out", [16, 128, 16, 128])
    c_ident = din("c_ident", [128, 128]); c_tri = din("c_tri", [128, 128]); c_sel127 = din("c_sel127", [128, 128])
    c_negm = din("c_negm", [128, 512]); c_selh = din("c_selh", [64, 64 * 128]); c_ones = din("c_ones", [128, 128])
    c_bd = din("c_bd", [128, 128]); c_ebo = din("c_ebo", [128, 4096]); c_ebp = din("c_ebp", [128, 4096])
    oT = nc.dram_tensor("oT", [2048, T], F32, kind="ExternalOutput").ap()

    def dscr(name, shape, dt=BF16):
        return nc.dram_tensor(name, list(shape), dt).ap()

    szT = dscr("szT", [32, 128, T]); bcT = dscr("bcT", [16, 128, T])
    xs_tm = [dscr("xs_tm0", [NT, 128, 4096]), dscr("xs_tm1", [NT, 128, 4096])]
    b_tm = [dscr("b_tm0", [NT, 128, 1024]), dscr("b_tm1", [NT, 128, 1024])]
    qT_d = dscr("qT_d", [16, 128, T]); kT_d = dscr("kT_d", [2, 128, T + 128]); v_tm = dscr("v_tm", [NT + 1, 128, 256])
    sgs = dscr("sgs", [16, 128, T]); sga = dscr("sga", [16, 128, T])
    ysT_d = dscr("ysT_d", [32, 128, T]); yaT_d = dscr("yaT_d", [16, 128, T])

    st = contextlib.ExitStack()
    with st:
        P = Prog(nc)

        def sb(name, shape, dt=F32):
            return st.enter_context(nc.sbuf_tensor(name, list(shape), dt))

        xT = sb("xT", [128, 16, T]); hT = sb("hT", [128, 16, T], BF16)
        R_xT = [[P.res("xT%d_%d" % (k, h)) for h in range(2)] for k in range(16)]
        R_hT = [P.res("hT%d" % h) for h in range(2)]
        NWB = 5
        wbuf = [sb("wbuf%d" % i, [128, 2048], BF16) for i in range(NWB)]
        R_wb = [P.res("wb%d" % i) for i in range(NWB)]
        wb_i = [0]
        SCR = sb("SCR", [128, 10880])
        R_SCR = P.res("SCRALL")
        ident_bf = sb("ident_bf", [128, 128], BF16); ident_f = sb("ident_f", [128, 128]); tri_f = sb("tri_f", [128, 128])
        sel127_f = sb("sel127_f", [128, 128]); negm_bf = sb("negm_bf", [128, 512], BF16); selh_bf = sb("selh_bf", [64, 64 * 128], BF16)
        ones_bf = sb("ones_bf", [128, 128], BF16); bd_bf = sb("bd_bf", [128, 128], BF16)
        nf_sb = [sb("nf1", [128, 16]), sb("nf2", [128, 16])]; nmix_sb = sb("nmix", [128, 16])
        wdt_sb = sb("wdt", [128, 16, 64], BF16); convw_sb = sb("convw", [128, 48, 4]); convb_sb = sb("convb", [128, 48])
        dtb_sb = sb("dtb", [128, 64]); a_sb = sb("a_sb", [128, 64]); dsk_sb = sb("dsk", [128, 64]); ssmn_sb = sb("ssmn", [128, 32])
        eps_sb = sb("eps_sb", [128, 1]); one_sb = sb("one_sb", [128, 1]); qn_sb = sb("qn_sb", [128, 1]); kn_sb = sb("kn_sb", [128, 1]); esink_sb = sb("esink", [128, 32]); mflag_sb = sb("mflag_sb", [128, 1])
        utail = sb("utail", [128, 48, 3]); S_run = sb("S_run", [128, 4096]); S_bf = sb("S_bf", [128, 4096], BF16)
        R_out = P.res("out"); R_const = P.res("const"); R_utail = P.res("utail"); R_S = P.res("S"); R_Sbf = P.res("Sbf")
        banks = [st.enter_context(nc.psum_tensor("bank%d" % i, [128, 512], F32)) for i in range(8)]
        R_bank = [P.res("bank%d" % i, excl=True) for i in range(8)]
        bk_i = [0]

        def nb():
            i = bk_i[0] % 8
            bk_i[0] += 1
            return banks[i], R_bank[i]

        pa_i = [0]; pb_i = [0]

        def nbA():
            i = pa_i[0] % 4
            pa_i[0] += 1
            return banks[i], R_bank[i]

        def nbB():
            i = 4 + pb_i[0] % 4
            pb_i[0] += 1
            return banks[i], R_bank[i]

        cst_n = [0]

        def cload(dst, src, cast):
            cst_n[0] += 1
            q = "pool" if cast else "sp"
            P.dma(q, "cst%d" % cst_n[0], lambda e: e.dma_start(out=dst, in_=src), writes=[R_const])

        cload(ident_bf[:], c_ident, True); cload(ident_f[:], c_ident, False); cload(tri_f[:], c_tri, False)
        cload(sel127_f[:], c_sel127, False); cload(negm_bf[:], c_negm, True); cload(selh_bf[:], c_selh, True)
        cload(ones_bf[:], c_ones, True); cload(bd_bf[:], c_bd, True)
        cload(nf_sb[0][:], n_f[0], False); cload(nf_sb[1][:], n_f[1], False); cload(nmix_sb[:], n_mix_d, False)
        cload(wdt_sb[:], w_dt_d, True); cload(convw_sb[:], conv_w_d, False); cload(convb_sb[:], conv_b_d, False)
        cload(dtb_sb[:], dtb_d, False); cload(a_sb[:], alog_d, False); cload(dsk_sb[:], dsk_d, False); cload(ssmn_sb[:], ssmn_d, False)
        cload(qn_sb[:], qn_d, False); cload(kn_sb[:], kn_d, False); cload(esink_sb[:], sinks_d, False); cload(mflag_sb[:], mflag_d, False)
        P.op("act", lambda e: e.activation(a_sb[:], a_sb[:], AF.Exp), reads=[R_const], writes=[R_const])
        P.op("dve", lambda e: e.tensor_scalar(a_sb[:], a_sb[:], -1.0, None, ALU.mult), reads=[R_const], writes=[R_const])
        P.op("act", lambda e: e.activation(esink_sb[:], esink_sb[:], AF.Exp), reads=[R_const], writes=[R_const])
        P.op("dve", lambda e: e.memset(utail[:], 0.0), writes=[R_utail])
        P.op("dve", lambda e: e.memset(eps_sb[:], EPS), reads=[R_const], writes=[R_const])
        P.op("dve", lambda e: e.memset(one_sb[:], 1.0), reads=[R_const], writes=[R_const])
        P.op("dve", lambda e: e.memset(S_run[:], 0.0), writes=[R_S])

        class Carver:
            def __init__(self):
                self.off = 0

            def f32(self, n, shape=None):
                v = SCR[:, self.off:self.off + n]
                self.off += n
                assert self.off <= 10880, self.off
                return v

            def bf(self, n):
                assert n % 2 == 0
                v = SCR[:, self.off:self.off + n // 2].bitcast(BF16)
                self.off += n // 2
                assert self.off <= 10880, self.off
                return v

        R_dr = {n: P.res("dr_" + n) for n in ["szT", "bcT", "xs_tm", "b_tm", "qT_d", "kT_d", "v_tm", "sgs", "sga", "ysT_d", "yaT_d"]}
        last_bar = [None]

        def sdma(key, fn, reads=(), writes=()):
            return P.dma("sp", key, fn, reads=list(reads) + list(R_dr.values()), writes=list(writes))

        def phase_barrier(wb=False):
            last_bar[0] = P.op("dve", lambda e: e.memset(SCR[:, 0:1], 0.0), writes=[R_SCR] + phase_res[0] + list(R_dr.values()) + (list(R_wb) if wb else []))
            phase_res[0] = []

        phase_res = [[]]

        def pres(name):
            r = P.res(name)
            r.last_w = last_bar[0]
            phase_res[0].append(r)
            return r

        def wload(src_ap, nelem, view):
            i = wb_i[0] % NWB
            wb_i[0] += 1
            buf = wbuf[i][:, 0:nelem]
            P.dma("pool", "wb%d" % i, lambda e: e.dma_start(out=view(buf), in_=src_ap), writes=[R_wb[i]])
            return view(buf), R_wb[i]

        def mm_group(bank, rb, n, pairs, reads):
            def fn(e):
                ins = None
                L = len(pairs)
                for i, (l, r) in enumerate(pairs):
                    ins = e.matmul(bank[:, 0:n], l, r, start=(i == 0), stop=(i == L - 1))
                return ins
            P.op("pe", fn, reads=reads, writes=[rb])

        def rmsnorm(gain_sb):
            c = Carver()
            sq = [c.bf(4 * 512) for _ in range(2)]
            R_sq = [pres("sq0"), pres("sq1")]
            rstd = c.f32(512); R_rstd = pres("rstd")
            for h in range(2):
                bank, rb = nb()
                for k4 in range(4):
                    i = k4 % 2
                    sqv = sq[i].rearrange("p (k t) -> p k t", k=4)
                    P.op("act", lambda e, sqv=sqv, k4=k4, h=h: e.activation(sqv, xT[:, 4 * k4:4 * k4 + 4, h * 512:(h + 1) * 512], AF.Square),
                         reads=[R_xT[k][h] for k in range(4 * k4, 4 * k4 + 4)], writes=[R_sq[i]])

                    def fn(e, sqv=sqv, k4=k4, bank=bank):
                        ins = None
                        for kk in range(4):
                            ins = e.matmul(bank[:, :], ones_bf[:], sqv[:, kk, :], start=(k4 == 0 and kk == 0), stop=(k4 == 3 and kk == 3))
                        return ins
                    P.op("pe", fn, reads=[R_sq[i], R_const], writes=[rb])
                P.op("act", lambda e, bank=bank: e.activation(rstd, bank[:, :], AF.Ln, bias=eps_sb[:, 0:1], scale=1.0 / 2048), reads=[rb, R_const], writes=[R_rstd])
                P.op("act", lambda e: e.activation(rstd, rstd, AF.Exp, scale=-0.5), reads=[R_rstd], writes=[R_rstd])
                for k in range(16):
                    P.op("dve", lambda e, k=k, h=h: e.scalar_tensor_tensor(hT[:, k, h * 512:(h + 1) * 512], xT[:, k, h * 512:(h + 1) * 512],
                                                                        gain_sb[:, k:k + 1], rstd, ALU.mult, ALU.mult),
                         reads=[R_xT[k][h], R_rstd, R_const], writes=[R_hT[h]])
            phase_barrier()

        def ffn(fi):
            rmsnorm(nf_sb[fi])
            c = Carver()
            actb = [c.bf(4 * T).rearrange("p (k t) -> p k t", k=4) for _ in range(2)]
            R_act = [pres("act0"), pres("act1")]
            sg = [c.f32(512) for _ in range(2)]; R_sg = [pres("sg0"), pres("sg1")]
            sgi = 0
            for q in range(11):
                ab = actb[q % 2]; rab = R_act[q % 2]
                for j in range(4):
                    jj = q * 4 + j
                    wg, rwg = wload(f_gu[fi][jj][:, 0, :, :], 16 * 128, lambda b: b.rearrange("p (k c) -> p k c", k=16))
                    wu, rwu = wload(f_gu[fi][jj][:, 1, :, :], 16 * 128, lambda b: b.rearrange("p (k c) -> p k c", k=16))
                    for h in range(2):
                        bg, rbg = nb(); bu, rbu = nb()
                        mm_group(bg, rbg, 512, [(wg[:, k, :], hT[:, k, h * 512:(h + 1) * 512]) for k in range(16)], [rwg, R_hT[h]])
                        mm_group(bu, rbu, 512, [(wu[:, k, :], hT[:, k, h * 512:(h + 1) * 512]) for k in range(16)], [rwu, R_hT[h]])
                        s = sg[sgi % 2]; rs = R_sg[sgi % 2]; sgi += 1
                        P.op("act", lambda e, s=s, bg=bg: e.activation(s, bg[:, :], AF.Silu), reads=[rbg], writes=[rs])
                        P.op("dve", lambda e, s=s, bu=bu, ab=ab, j=j, h=h: e.tensor_tensor(ab[:, j, h * 512:(h + 1) * 512], bu[:, :], s, ALU.mult),
                             reads=[rbu, rs], writes=[rab])
                for cg in range(4):
                    wv, rw = wload(f_d[fi][q, cg], 4 * 4 * 128, lambda b: b.rearrange("p (g k c) -> p g k c", g=4, k=4))
                    for gi in range(4):
                        cc = 4 * cg + gi
                        for h in range(2):
                            bk, rb = nb()
                            mm_group(bk, rb, 512, [(wv[:, gi, k, :], ab[:, k, h * 512:(h + 1) * 512]) for k in range(4)], [rw, rab])
                            P.op("dve", lambda e, bk=bk, cc=cc, h=h: e.scalar_tensor_tensor(xT[:, cc, h * 512:(h + 1) * 512], bk[:, :], 0.5,
                                                                                         xT[:, cc, h * 512:(h + 1) * 512], ALU.mult, ALU.add),
                                 reads=[rb, R_xT[cc][h]], writes=[R_xT[cc][h]])
            phase_barrier()

        def linear(w_ap, NJ, KCn, rhs_fn, rhs_res, evac, G=1, halves=(0, 1), defer=0):
            pend = []
            for j0 in range(0, NJ, G):
                g = min(G, NJ - j0)
                wv, rw = wload(w_ap[j0:j0 + g].rearrange("g p k c -> p g k c"), g * KCn * 128,
                               lambda b, g=g: b.rearrange("p (g k c) -> p g k c", g=g, k=KCn))
                for gi in range(g):
                    for h in halves:
                        bk, rb = nb()
                        mm_group(bk, rb, 512, [(wv[:, gi, k, :], rhs_fn(k, h)) for k in range(KCn)], [rw] + rhs_res(h))
                        pend.append(evac(j0 + gi, h, bk, rb))
                        if len(pend) > defer:
                            d0 = pend.pop(0)
                            if d0 is not None:
                                d0()
            for d0 in pend:
                if d0 is not None:
                    d0()

        def pipeline(items, stages, extras=(), rev=False):
            n = len(items); S_ = len(stages)
            nsteps = n + S_ - 1
            extras = list(extras)
            per = -(-len(extras) // nsteps) if extras else 0
            for step in range(nsteps):
                order = range(S_ - 1, -1, -1) if rev else range(S_)
                for si in order:
                    k = step - si
                    if 0 <= k < n:
                        stages[si](items[k])
                for _ in range(per):
                    if extras:
                        extras.pop(0)()
            for o in extras:
                o()

        def hT_rhs(k, h):
            return hT[:, k, h * 512:(h + 1) * 512]

        def hT_res(h):
            return [R_hT[h]]

        def inproj(own):
            pz = 1 if own else 0
            c = Carver()
            stg = [c.bf(T) for _ in range(3)]; R_stg = [pres("stg%d" % i) for i in range(3)]
            stg_i = [0]
            U = [c.f32(T + 4) for _ in range(2)]; R_U = [pres("U0"), pres("U1")]
            acc = [c.f32(T) for _ in range(2)]; R_acc = [pres("acc0"), pres("acc1")]
            tst = [c.bf(NT * 128).rearrange("p (t c) -> p t c", t=NT) for _ in range(2)]; R_tst = [pres("tst0"), pres("tst1")]
            tst_i = [0]
            qraw = [c.f32(512) for _ in range(2)]; R_qraw = [pres("qraw0"), pres("qraw1")]
            sqq = [c.bf(512) for _ in range(2)]; R_sqq = [pres("sqq0"), pres("sqq1")]
            rq = [c.f32(512) for _ in range(2)]; R_rq = [pres("rq0"), pres("rq1")]
            qi = [0]
            dcnt = [0]

            def dkey():
                dcnt[0] += 1
                return "ip%d" % (dcnt[0] % 6)

            def getstg():
                i = stg_i[0] % 3
                stg_i[0] += 1
                return stg[i], R_stg[i]

            def act_store(func, dst_d):
                def evac(j, h, bk, rb):
                    s, rs = getstg()
                    P.op("act", lambda e: e.activation(s[:, 0:512], bk[:, :], func), reads=[rb], writes=[rs])
                    sdma(dkey(), lambda e: e.dma_start(out=dst_d[j][:, h * 512:(h + 1) * 512], in_=s[:, 0:512]), reads=[rs])
                    return None
                return evac

            def transpose_store(src_bf, rsrc, dst_ap_fn):
                bk, rb = nb()
                bkb = bk[:, :].bitcast(BF16)

                def fn(e):
                    ins = None
                    for t in range(NT):
                        ins = e.transpose(bkb[:, t * 128:(t + 1) * 128], src_bf[:, t * 128:(t + 1) * 128], ident_bf[:])
                    return ins
                P.op("pe", fn, reads=[rsrc, R_const], writes=[rb])
                i = tst_i[0] % 2
                tst_i[0] += 1
                P.op("act", lambda e: e.activation(tst[i].rearrange("p t c -> p (t c)"), bkb, AF.Copy), reads=[rb], writes=[R_tst[i]])
                sdma(dkey(), lambda e: dst_ap_fn(e, tst[i]), reads=[R_tst[i]])

            def qk_evac(gain_sb, store):
                def evac(j, h, bk, rb):
                    i = qi[0] % 2
                    qi[0] += 1
                    P.op("act", lambda e: e.activation(qraw[i], bk[:, :], AF.Copy), reads=[rb], writes=[R_qraw[i]])
                    P.op("act", lambda e: e.activation(sqq[i], bk[:, :], AF.Square), reads=[rb], writes=[R_sqq[i]])

                    def later():
                        b2, rb2 = nb()
                        P.op("pe", lambda e: e.matmul(b2[:, :], bd_bf[:], sqq[i], start=True, stop=True), reads=[R_sqq[i], R_const], writes=[rb2])
                        P.op("act", lambda e: e.activation(rq[i], b2[:, :], AF.Ln, bias=eps_sb[:, 0:1], scale=1.0 / 64), reads=[rb2, R_const], writes=[R_rq[i]])
                        P.op("act", lambda e: e.activation(rq[i], rq[i], AF.Exp, scale=-0.5), reads=[R_rq[i]], writes=[R_rq[i]])
                        s, rs = getstg()
                        P.op("dve", lambda e: e.scalar_tensor_tensor(s[:, 0:512], qraw[i], gain_sb[:, 0:1], rq[i], ALU.mult, ALU.mult),
                             reads=[R_qraw[i], R_rq[i], R_const], writes=[rs])
                        store(j, h, s, rs)
                    return later
                return evac

            def xbc_evac(j, h, bk, rb):
                ub = U[j % 2]; rub = R_U[j % 2]; ab = acc[j % 2]; rab = R_acc[j % 2]
                P.op("act", lambda e: e.activation(ub[:, 3 + h * 512:3 + (h + 1) * 512], bk[:, :], AF.Copy), reads=[rb], writes=[rub])
                if h == 0:
                    return None
                if (not own) and j >= 40:
                    P.op("dve", lambda e: e.tensor_copy(utail[:, j, :], ub[:, T:T + 3]), reads=[rub], writes=[R_utail])
                    return None
                P.op("dve", lambda e: e.tensor_copy(ub[:, 0:3], utail[:, j, :]), reads=[R_utail], writes=[rub])
                P.op("dve", lambda e: e.tensor_scalar(ab, ub[:, 0:T], convw_sb[:, j, 0:1], None, ALU.mult), reads=[rub, R_const], writes=[rab])
                for kk in range(1, 4):
                    P.op("dve", lambda e, kk=kk: e.scalar_tensor_tensor(ab, ub[:, kk:kk + T], convw_sb[:, j, kk:kk + 1], ab, ALU.mult, ALU.add),
                         reads=[rub, rab, R_const], writes=[rab])
                if not own:
                    P.op("dve", lambda e: e.tensor_copy(utail[:, j, :], ub[:, T:T + 3]), reads=[rub], writes=[R_utail])

                def later():
                    s, rs = getstg()
                    P.op("act", lambda e: e.activation(s, ab, AF.Silu, bias=convb_sb[:, j:j + 1]), reads=[rab, R_const], writes=[rs])
                    if j < 32:
                        transpose_store(s, rs, lambda e, stt: e.dma_start(out=xs_tm[pz][:, :, j * 128:(j + 1) * 128].rearrange("t p c -> p t c"), in_=stt))
                    elif j < 40:
                        transpose_store(s, rs, lambda e, stt: e.dma_start(out=b_tm[pz][:, :, (j - 32) * 128:(j - 31) * 128].rearrange("t p c -> p t c"), in_=stt))
                    if own and j >= 32:
                        sdma(dkey(), lambda e: e.dma_start(out=bcT[j - 32], in_=s), reads=[rs])
                return later

            def k_store(j, h, s, rs):
                if own:
                    sdma(dkey(), lambda e: e.dma_start(out=kT_d[j][:, 128 + h * 512:128 + (h + 1) * 512], in_=s[:, 0:512]), reads=[rs])
                elif h == 1:
                    sdma(dkey(), lambda e: e.dma_start(out=kT_d[j][:, 0:128], in_=s[:, 384:512]), reads=[rs])

            def q_store(j, h, s, rs):
                sdma(dkey(), lambda e: e.dma_start(out=qT_d[j][:, h * 512:(h + 1) * 512], in_=s[:, 0:512]), reads=[rs])

            vfull = [None]

            def kv_evac(j, h, bk, rb):
                if j < 2:
                    return qk_evac(kn_sb, k_store)(j, h, bk, rb)
                s = acc[j % 2].bitcast(BF16)[:, 0:T]; rs = R_acc[j % 2]
                P.op("act", lambda e: e.activation(s[:, h * 512:(h + 1) * 512], bk[:, :], AF.Copy), reads=[rb], writes=[rs])
                if h == 0:
                    return None
                jv = j - 2

                def later():
                    if own:
                        transpose_store(s, rs, lambda e, stt: e.dma_start(out=v_tm[1:NT + 1, :, jv * 128:(jv + 1) * 128].rearrange("t p c -> p t c"), in_=stt))
                    else:
                        transpose_store(s, rs, lambda e, stt: e.dma_start(out=v_tm[0][:, jv * 128:(jv + 1) * 128], in_=stt[:, NT - 1, :]))
                return later

            if own:
                linear(w_z, 32, 16, hT_rhs, hT_res, act_store(AF.Silu, szT))
                linear(w_gs, 16, 16, hT_rhs, hT_res, act_store(AF.Sigmoid, sgs))
                linear(w_ga, 16, 16, hT_rhs, hT_res, act_store(AF.Sigmoid, sga))
                linear(w_q, 16, 16, hT_rhs, hT_res, qk_evac(qn_sb, q_store), defer=1)
            linear(w_kv, 4, 16, hT_rhs, hT_res, kv_evac, defer=1)
            linear(w_xbc, 48 if own else 40, 16, hT_rhs, hT_res, xbc_evac, defer=2)
            if not own:
                linear(w_xbc[40:48], 8, 16, hT_rhs, hT_res, lambda j, h, bk, rb: xbc_evac(j + 40, h, bk, rb), halves=(1,))
            phase_barrier(wb=True)

        def ssd(own):
            pz = 1 if own else 0
            c = Carver()
            xs_b = [c.bf(4096) for _ in range(2)]; R_xs = [pres("xs0"), pres("xs1")]
            xsc = c.bf(512); R_xsc = pres("xsc")
            BTs = [wbuf[i][:, 0:1024].rearrange("p (g n) -> p g n", g=8) for i in range(2)]
            CTs = [wbuf[i][:, 1024:2048].rearrange("p (g n) -> p g n", g=8) for i in range(2)]
            R_BC = [pres("BC0"), pres("BC1")]
            NMS = ["dtx", "ax", "ee", "dt", "adt", "acum", "lndt", "bias", "expA", "dd", "dtdec", "cdec"]
            sms = [{nm: wbuf[2 + i][:, k * 128:(k + 1) * 128].bitcast(F32) for k, nm in enumerate(NMS)} for i in range(2)]
            R_sms = [pres("sm0"), pres("sm1")]
            AT_his = [wbuf[2 + i][:, 1536:1664] for i in range(2)]; AT_los = [wbuf[2 + i][:, 1664:1792] for i in range(2)]
            R_ATs = [pres("AT0"), pres("AT1")]
            b_t = wbuf[4][:, 0:1024]; R_bt = pres("b_t")
            sz_t = [wbuf[4][:, 1024:1536], wbuf[4][:, 1536:2048]]; R_sz = [pres("sz_t0"), pres("sz_t1")]
            if own:
                xsd = [c.bf(512) for _ in range(2)]; R_xsd = [pres("xsd0"), pres("xsd1")]
                cbT = [c.bf(128) for _ in range(2)]; R_cb = [pres("cbT0"), pres("cbT1")]
                E = [[c.bf(512) for _ in range(2)] for _ in range(2)]; R_E = [[pres("E%d%d" % (a_, b_)) for b_ in range(2)] for a_ in range(2)]
                WT = E; R_WT = R_E
                t1 = [c.f32(512) for _ in range(2)]; R_t1 = [pres("t10"), pres("t11")]
                ytm = [c.f32(512) for _ in range(2)]; R_ytm = [pres("ytm0"), pres("ytm1")]
                yg = [c.f32(512) for _ in range(2)]; R_yg = [pres("yg0"), pres("yg1")]
                sqy = [c.bf(512) for _ in range(2)]; R_sqy = [pres("sqy0"), pres("sqy1")]
                rs_y = c.f32(128); R_rsy = pres("rs_y")
                ysn = [c.bf(512) for _ in range(2)]; R_ysn = [pres("ysn0"), pres("ysn1")]

            def prep_ops(t):
                q_ = t % 2
                sm = sms[q_]; R_sm = R_sms[q_]; xs_t = xs_b[q_]
                ts = slice(t * 128, (t + 1) * 128)
                ops = []
                ops.append(lambda: sdma("ss0%d" % q_, lambda e: e.dma_start(out=xs_t, in_=xs_tm[pz][t]), writes=[R_xs[q_]]))
                if own:
                    ops.append(lambda: sdma("ss2%d" % q_, lambda e: e.dma_start(out=BTs[q_], in_=bcT[0:8, :, ts].rearrange("g p n -> p g n")), writes=[R_BC[q_]]))
                    ops.append(lambda: sdma("ss3%d" % q_, lambda e: e.dma_start(out=CTs[q_], in_=bcT[8:16, :, ts].rearrange("g p n -> p g n")), writes=[R_BC[q_]]))
                st_ = {}

                def o1():
                    st_["bk"], st_["rb"] = nbB()
                    mm_group(st_["bk"], st_["rb"], 64, [(hT[:, k, ts], wdt_sb[:, k, :]) for k in range(16)], [R_hT[t // 4], R_const])
                    P.op("dve", lambda e: e.tensor_tensor(sm["dtx"], st_["bk"][:, 0:64], dtb_sb[:], ALU.add), reads=[st_["rb"], R_const], writes=[R_sm])
                ops.append(o1)
                ops.append(lambda: P.op("act", lambda e: e.activation(sm["ax"], sm["dtx"], AF.Abs), reads=[R_sm], writes=[R_sm]))
                ops.append(lambda: P.op("act", lambda e: e.activation(sm["ee"], sm["ax"], AF.Exp, scale=-1.0), reads=[R_sm], writes=[R_sm]))
                ops.append(lambda: P.op("act", lambda e: e.activation(sm["ee"], sm["ee"], AF.Ln, bias=one_sb[:, 0:1]), reads=[R_sm, R_const], writes=[R_sm]))
                ops.append(lambda: P.op("dve", lambda e: e.scalar_tensor_tensor(sm["dt"], sm["dtx"], 0.0, sm["ee"], ALU.max, ALU.add), reads=[R_sm], writes=[R_sm]))
                ops.append(lambda: P.op("dve", lambda e: e.tensor_tensor(sm["adt"], sm["dt"], a_sb[:], ALU.mult), reads=[R_sm, R_const], writes=[R_sm]))

                def o2():
                    bk2, rb2 = nbB()
                    P.op("pe", lambda e: e.matmul(bk2[:, 0:64], tri_f[:], sm["adt"], start=True, stop=True), reads=[R_sm, R_const], writes=[rb2])
                    P.op("act", lambda e: e.activation(sm["acum"], bk2[:, 0:64], AF.Copy), reads=[rb2], writes=[R_sm])
                ops.append(o2)

                def o3():
                    bk4, rb4 = nbB()
                    P.op("pe", lambda e: e.matmul(bk4[:, 0:64], sel127_f[:], sm["acum"], start=True, stop=True), reads=[R_sm, R_const], writes=[rb4])
                    P.op("dve", lambda e: e.tensor_tensor(sm["dd"], bk4[:, 0:64], sm["acum"], ALU.subtract), reads=[rb4, R_sm], writes=[R_sm])
                    P.op("act", lambda e: e.activation(sm["cdec"], bk4[:, 0:64], AF.Exp), reads=[rb4], writes=[R_sm])
                ops.append(o3)
                ops.append(lambda: P.op("act", lambda e: e.activation(sm["dd"], sm["dd"], AF.Exp), reads=[R_sm], writes=[R_sm]))
                ops.append(lambda: P.op("dve", lambda e: e.tensor_tensor(sm["dtdec"], sm["dd"], sm["dt"], ALU.mult), reads=[R_sm], writes=[R_sm]))
                if own:
                    def o4():
                        bk3, rb3 = nbB()
                        P.op("pe", lambda e: e.matmul(bk3[0:64, 0:128], sm["adt"], tri_f[:], start=True, stop=True), reads=[R_sm, R_const], writes=[rb3])
                        P.op("dve", lambda e: e.tensor_copy(AT_his[q_][0:64, :], bk3[0:64, 0:128]), reads=[rb3], writes=[R_ATs[q_]])
                        P.op("dve", lambda e: e.tensor_tensor(AT_los[q_][0:64, :], bk3[0:64, 0:128], AT_his[q_][0:64, :], ALU.subtract),
                             reads=[rb3, R_ATs[q_]], writes=[R_ATs[q_]])
                    ops.append(o4)
                    ops.append(lambda: P.op("act", lambda e: e.activation(sm["lndt"], sm["dt"], AF.Ln), reads=[R_sm], writes=[R_sm]))
                    ops.append(lambda: P.op("dve", lambda e: e.tensor_tensor(sm["bias"], sm["lndt"], sm["acum"], ALU.subtract), reads=[R_sm], writes=[R_sm]))
                    ops.append(lambda: P.op("act", lambda e: e.activation(sm["expA"], sm["acum"], AF.Exp), reads=[R_sm], writes=[R_sm]))
                return ops

            def state_ops(t):
                q_ = t % 2
                sm = sms[q_]; R_sm = R_sms[q_]; xs_t = xs_b[q_]
                ops = [lambda: sdma("ss1", lambda e: e.dma_start(out=b_t, in_=b_tm[pz][t]), writes=[R_bt])]
                for g in range(8):
                    def og(g=g):
                        P.op("dve", lambda e: e.tensor_tensor(xsc.rearrange("p (h d) -> p h d", h=8), xs_t[:, g * 512:(g + 1) * 512].rearrange("p (h d) -> p h d", h=8),
                                                              sm["dtdec"][:, g * 8:(g + 1) * 8].unsqueeze(2).to_broadcast([128, 8, 64]), ALU.mult),
                             reads=[R_xs[q_], R_sm], writes=[R_xsc])
                        bS, rbS = nbB()
                        P.op("pe", lambda e: e.matmul(bS[:, :], b_t[:, g * 128:(g + 1) * 128], xsc, start=True, stop=True), reads=[R_bt, R_xsc], writes=[rbS])
                        Sg = S_run[:, g * 512:(g + 1) * 512].rearrange("p (h d) -> p h d", h=8)
                        P.op("dve", lambda e: e.tensor_tensor(Sg, Sg, sm["cdec"][:, g * 8:(g + 1) * 8].unsqueeze(2).to_broadcast([128, 8, 64]), ALU.mult),
                             reads=[R_S, R_sm], writes=[R_S])
                        P.op("dve", lambda e: e.tensor_tensor(S_run[:, g * 512:(g + 1) * 512], S_run[:, g * 512:(g + 1) * 512], bS[:, :], ALU.add),
                             reads=[R_S, rbS], writes=[R_S])
                    ops.append(og)
                return ops

            def y_stages(t):
                q_ = t % 2
                sm = sms[q_]; R_sm = R_sms[q_]; xs_t = xs_b[q_]; BT_t = BTs[q_]; CT_t = CTs[q_]
                AT_hi = AT_his[q_]; AT_lo = AT_los[q_]; R_AT = R_ATs[q_]
                ts = slice(t * 128, (t + 1) * 128)
                ctx = [dict() for _ in range(8)]

                def s0(g):
                    d = ctx[g]
                    d["R"] = []
                    for hh in range(2):
                        bR, rbR = nbA()
                        d["R"].append((bR, rbR))

                        def fnR(e, bR=bR, hh=hh):
                            e.matmul(bR[:, :], ident_bf[:], negm_bf[:], start=True, stop=False)
                            ins = None
                            for i in range(4):
                                hd = g * 8 + hh * 4 + i
                                e.matmul(bR[:, i * 128:(i + 1) * 128], selh_bf[:, hd * 128:(hd + 1) * 128], AT_hi[0:64, :], start=False, stop=False)
                                ins = e.matmul(bR[:, i * 128:(i + 1) * 128], selh_bf[:, hd * 128:(hd + 1) * 128], AT_lo[0:64, :], start=False, stop=(i == 3))
                            return ins
                        P.op("pe", fnR, reads=[R_AT, R_const], writes=[rbR])
                    bkc, rbc = nbA(); d["cb"] = (bkc, rbc)
                    P.op("pe", lambda e: e.matmul(bkc[:, 0:128], BT_t[:, g, :], CT_t[:, g, :], start=True, stop=True), reads=[R_BC[q_]], writes=[rbc])
                    bO, rbO = nbA(); d["O"] = (bO, rbO)
                    P.op("pe", lambda e: e.matmul(bO[:, :], CT_t[:, g, :], S_bf[:, g * 512:(g + 1) * 512], start=True, stop=True),
                         reads=[R_BC[q_], R_Sbf], writes=[rbO])

                def s1(g):
                    d = ctx[g]; st_ = g % 2
                    bkc, rbc = d["cb"]
                    P.op("act", lambda e: e.activation(cbT[st_], bkc[:, 0:128], AF.Copy), reads=[rbc], writes=[R_cb[st_]])
                    for hh in range(2):
                        bR, rbR = d["R"][hh]
                        Eb = E[st_][hh]
                        for i in range(4):
                            hd = g * 8 + hh * 4 + i
                            P.op("act", lambda e, bR=bR, i=i, hd=hd, Eb=Eb: e.activation(Eb[:, i * 128:(i + 1) * 128], bR[:, i * 128:(i + 1) * 128], AF.Exp,
                                                                                       bias=sm["bias"][:, hd:hd + 1]),
                                 reads=[rbR, R_sm], writes=[R_E[st_][hh]])
                    bO, rbO = d["O"]
                    P.op("dve", lambda e: e.tensor_tensor(xsd[st_].rearrange("p (h d) -> p h d", h=8), xs_t[:, g * 512:(g + 1) * 512].rearrange("p (h d) -> p h d", h=8),
                                                          dsk_sb[:, g * 8:(g + 1) * 8].unsqueeze(2).to_broadcast([128, 8, 64]), ALU.mult), reads=[R_xs[q_], R_const], writes=[R_xsd[st_]])
                    P.op("dve", lambda e: e.tensor_tensor(t1[st_].rearrange("p (h d) -> p h d", h=8), bO[:, :].rearrange("p (h d) -> p h d", h=8),
                                                          sm["expA"][:, g * 8:(g + 1) * 8].unsqueeze(2).to_broadcast([128, 8, 64]), ALU.mult),
                         reads=[rbO, R_sm], writes=[R_t1[st_]])

                def s2(g):
                    st_ = g % 2
                    for hh in range(2):
                        Eb = E[st_][hh]; Wb = WT[st_][hh]
                        P.op("dve", lambda e, Eb=Eb, Wb=Wb: e.tensor_tensor(Wb.rearrange("p (i l) -> p i l", i=4), Eb.rearrange("p (i l) -> p i l", i=4),
                                                                              cbT[st_].unsqueeze(1).to_broadcast([128, 4, 128]), ALU.mult),
                             reads=[R_E[st_][hh], R_cb[st_]], writes=[R_WT[st_][hh]])

                def s3(g):
                    st_ = g % 2
                    bY, rbY = nbB()

                    def fnY(e):
                        e.matmul(bY[:, :], ident_bf[:], xsd[st_], start=True, stop=False)
                        ins = None
                        for hl in range(8):
                            hd = g * 8 + hl
                            Wb = WT[st_][hl // 4]
                            ins = e.matmul(bY[:, hl * 64:(hl + 1) * 64], Wb[:, (hl % 4) * 128:(hl % 4 + 1) * 128], xs_t[:, hd * 64:(hd + 1) * 64],
                                           start=False, stop=(hl == 7))
                        return ins
                    P.op("pe", fnY, reads=[R_xsd[st_], R_WT[st_][0], R_WT[st_][1], R_xs[q_], R_const], writes=[rbY])
                    P.op("dve", lambda e: e.tensor_tensor(ytm[st_], bY[:, :], t1[st_], ALU.add), reads=[rbY, R_t1[st_]], writes=[R_ytm[st_]])
                    sdma("ss4%d" % st_, lambda e: e.dma_start(out=sz_t[st_].rearrange("p (i l) -> p i l", i=4),
                                                            in_=szT[g * 4:(g + 1) * 4, :, ts].rearrange("i p l -> p i l")), writes=[R_sz[st_]])

                def s4(g):
                    st_ = g % 2
                    bT, rbT = nbB()

                    def fnT(e):
                        ins = None
                        for i in range(4):
                            ins = e.transpose(bT[:, i * 128:(i + 1) * 128], ytm[st_][:, i * 128:(i + 1) * 128], ident_f[:])
                        return ins
                    P.op("pe", fnT, reads=[R_ytm[st_], R_const], writes=[rbT])
                    P.op("dve", lambda e: e.tensor_tensor(yg[st_], bT[:, :], sz_t[st_], ALU.mult), reads=[rbT, R_sz[st_]], writes=[R_yg[st_]])
                    P.op("act", lambda e: e.activation(sqy[st_], yg[st_], AF.Square), reads=[R_yg[st_]], writes=[R_sqy[st_]])

                def s5(g):
                    st_ = g % 2
                    bN, rbN = nbB()
                    mm_group(bN, rbN, 128, [(ones_bf[:], sqy[st_][:, i * 128:(i + 1) * 128]) for i in range(4)], [R_sqy[st_], R_const])
                    P.op("act", lambda e: e.activation(rs_y, bN[:, 0:128], AF.Ln, bias=eps_sb[:, 0:1], scale=1.0 / 512), reads=[rbN, R_const], writes=[R_rsy])
                    P.op("act", lambda e: e.activation(rs_y, rs_y, AF.Exp, scale=-0.5), reads=[R_rsy], writes=[R_rsy])
                    yb = ysn[st_]; ryb = R_ysn[st_]
                    for i in range(4):
                        jj = g * 4 + i
                        P.op("dve", lambda e, i=i, jj=jj: e.scalar_tensor_tensor(yb[:, i * 128:(i + 1) * 128], yg[st_][:, i * 128:(i + 1) * 128],
                                                                               ssmn_sb[:, jj:jj + 1], rs_y, ALU.mult, ALU.mult),
                             reads=[R_yg[st_], R_rsy, R_const], writes=[ryb])
                    sdma("ss5%d" % st_, lambda e: e.dma_start(out=ysT_d[g * 4:(g + 1) * 4, :, ts].rearrange("i p l -> p i l"),
                                                            in_=yb.rearrange("p (i l) -> p i l", i=4)), reads=[ryb])
                return [s0, s1, s2, s3, s4, s5]

            for o in prep_ops(0):
                o()
            for t in range(NT):
                extras = state_ops(t) + (prep_ops(t + 1) if t + 1 < NT else [])
                if own:
                    pipeline(list(range(8)), y_stages(t), extras=extras, rev=True)
                else:
                    for o in extras:
                        o()
                if own and t < NT - 1:
                    P.op("act", lambda e: e.activation(S_bf[:], S_run[:], AF.Copy), reads=[R_S], writes=[R_Sbf])
            if not own:
                P.op("dve", lambda e: e.tensor_scalar(S_run[:], S_run[:], mflag_sb[:, 0:1], None, ALU.mult), reads=[R_S, R_const], writes=[R_S])
                P.op("act", lambda e: e.activation(S_bf[:], S_run[:], AF.Copy), reads=[R_S], writes=[R_Sbf])
            phase_barrier(wb=True)

        def attention():
            c = Carver()
            ebo = c.bf(4096); ebp = c.bf(4096); R_eb = pres("eb")
            qT_t = c.bf(16 * 128).rearrange("p (j q) -> p j q", j=16); R_q = pres("qT_t")
            kT_t = c.bf(4 * 256).rearrange("p (k s) -> p k s", k=4); R_k = pres("kT_t")
            v1 = [c.f32(2 * 4 * 65 // 2 + 2) for _ in range(2)]
            v1 = [v[:, 0:260].bitcast(BF16).rearrange("p (k g d) -> p k g d", k=2, g=4) for v in v1]; R_v = [pres("v0"), pres("v1")]
            Eb = [[c.bf(512) for _ in range(2)] for _ in range(2)]; R_Eb = [[pres("aE%d%d" % (a_, b_)) for b_ in range(2)] for a_ in range(2)]
            PT = [[c.bf(512) for _ in range(2)] for _ in range(2)]; R_PT = [[pres("PT%d%d" % (a, b)) for b in range(2)] for a in range(2)]
            den = c.f32(4); R_den = pres("den")
            ya = c.bf(2048); R_ya = pres("ya")
            yst = c.bf(2048); R_yst = pres("yst")
            P.dma("pool", "at0", lambda e: e.dma_start(out=ebo, in_=c_ebo), writes=[R_eb])
            P.dma("pool", "at1", lambda e: e.dma_start(out=ebp, in_=c_ebp), writes=[R_eb])
            for i in range(2):
                P.op("dve", lambda e, i=i: e.memset(v1[i], 1.0), writes=[R_v[i]])
            for t in range(NT):
                vb = v1[t % 2]; rv = R_v[t % 2]
                sdma("at2", lambda e, t=t: e.dma_start(out=qT_t, in_=qT_d[:, :, t * 128:(t + 1) * 128].rearrange("j p q -> p j q")), writes=[R_q])
                for par in range(2):
                    sdma("at3%d" % par, lambda e, t=t, par=par: e.dma_start(
                        out=kT_t[par * 64:(par + 1) * 64, :, :],
                        in_=kT_d[:, :, t * 128:t * 128 + 256].rearrange("j (u d) s -> d (j u) s", u=2)), writes=[R_k])
                for k2 in range(2):
                    sdma("at4%d%d" % (t % 2, k2), lambda e, t=t, vb=vb, k2=k2: e.dma_start(out=vb[:, k2, :, 0:64],
                                                                                       in_=v_tm[t + k2].rearrange("p (g d) -> p g d", g=4)), writes=[rv])
                actx = {}

                def a0(u, t=t):
                    kv, par = u
                    actx[u] = []
                    for kt in range(2):
                        bS, rbS = nbA()
                        actx[u].append((bS, rbS))
                        P.op("pe", lambda e, bS=bS, kt=kt: e.matmul(
                            bS[:, :], kT_t[par * 64:(par + 1) * 64, kv, kt * 128:(kt + 1) * 128],
                            qT_t[par * 64:(par + 1) * 64, kv * 4:(kv + 1) * 4, :], start=True, stop=True), reads=[R_k, R_q], writes=[rbS])

                def a1(u, t=t):
                    kv, par = u
                    for kt in range(2):
                        bS, rbS = actx[u][kt]
                        Ei = Eb[par][kt]
                        P.op("act", lambda e, bS=bS, Ei=Ei: e.activation(Ei, bS[:, :], AF.Exp, scale=0.125), reads=[rbS], writes=[R_Eb[par][kt]])

                def a2(u, t=t):
                    kv, par = u
                    for kt in range(2):
                        Ei = Eb[par][kt]
                        ebsrc = ebo if kt == 1 else ebp
                        ebv = ebsrc.rearrange("p (h q) -> p h q", h=32)[:, kv * 8 + par:kv * 8 + 8:2, :]
                        P.op("dve", lambda e, Ei=Ei, ebv=ebv, kt=kt: e.tensor_tensor(PT[par][kt].rearrange("p (i q) -> p i q", i=4),
                                                                                  Ei.rearrange("p (i q) -> p i q", i=4), ebv, ALU.mult),
                             reads=[R_Eb[par][kt], R_eb], writes=[R_PT[par][kt]])
                        if t == 0 and kt == 0:
                            P.op("dve", lambda e, kt=kt: e.tensor_scalar(PT[par][kt], PT[par][kt], mflag_sb[:, 0:1], None, ALU.mult),
                                 reads=[R_PT[par][kt], R_const], writes=[R_PT[par][kt]])

                def a3(u, t=t, vb=vb, rv=rv):
                    kv, par = u
                    bO, rbO = nbB()

                    def fnO(e, bO=bO):
                        ins = None
                        for i in range(4):
                            for kt in range(2):
                                ins = e.matmul(bO[:, i * 128:i * 128 + 65], PT[par][kt][:, i * 128:(i + 1) * 128], vb[:, kt, kv, :],
                                               start=(kt == 0), stop=(kt == 1))
                        return ins
                    P.op("pe", fnO, reads=[R_PT[par][0], R_PT[par][1], rv], writes=[rbO])
                    bOv = bO[:, :].rearrange("p (i c) -> p i c", i=4)
                    esv = esink_sb[:, kv * 8 + par:kv * 8 + 8:2]
                    P.op("dve", lambda e: e.tensor_tensor(den.unsqueeze(2), bOv[:, :, 64:65], esv.unsqueeze(2), ALU.add),
                         reads=[rbO, R_const], writes=[R_den])
                    P.op("dve", lambda e: e.reciprocal(den, den), reads=[R_den], writes=[R_den])
                    yav = ya.rearrange("p (h d) -> p h d", h=32)[:, kv * 8 + par:kv * 8 + 8:2, :]
                    P.op("dve", lambda e: e.tensor_tensor(yav, bOv[:, :, 0:64], den.unsqueeze(2).to_broadcast([128, 4, 64]), ALU.mult),
                         reads=[rbO, R_den], writes=[R_ya])
                pipeline([(kv, par) for kv in range(4) for par in range(2)], [a0, a1, a2, a3], rev=True)
                for half in range(2):
                    bT, rbT = nbB()
                    bTb = bT[:, :].bitcast(BF16)

                    def fnT(e, bTb=bTb, half=half):
                        ins = None
                        for i in range(8):
                            cch = half * 8 + i
                            ins = e.transpose(bTb[:, i * 128:(i + 1) * 128], ya[:, cch * 128:(cch + 1) * 128], ident_bf[:])
                        return ins
                    P.op("pe", fnT, reads=[R_ya, R_const], writes=[rbT])
                    P.op("act", lambda e, bTb=bTb, half=half: e.activation(yst[:, half * 1024:(half + 1) * 1024], bTb, AF.Copy), reads=[rbT], writes=[R_yst])
                sdma("at5", lambda e, t=t: e.dma_start(out=yaT_d[:, :, t * 128:(t + 1) * 128].rearrange("j p q -> p j q"),
                                                              in_=yst.rearrange("p (j q) -> p j q", j=16)), reads=[R_yst])
            phase_barrier()

        def outproj():
            for h in range(2):
                hs = slice(h * 512, (h + 1) * 512)
                c = Carver()
                ysT_h = hT[:, :, :].rearrange("p k t -> p (k t)").rearrange("p (k t) -> p k t", k=32)
                yaT_h = c.bf(16 * 512).rearrange("p (k t) -> p k t", k=16); R_ya = pres("yaT_h")
                mT_h = c.bf(16 * 512).rearrange("p (k t) -> p k t", k=16); R_m = pres("mT_h")
                gsb = [c.bf(512) for _ in range(2)]; gab = [c.bf(512) for _ in range(2)]
                R_g = [pres("g0"), pres("g1")]
                m1 = c.f32(512); R_m1 = pres("m1")
                t2 = c.f32(512); R_t2 = pres("t2")
                R_ys = R_hT[0]
                sdma("op0", lambda e, hs=hs: e.dma_start(out=ysT_h, in_=ysT_d[:, :, hs].rearrange("k p t -> p k t")), writes=[R_hT[0], R_hT[1]])
                sdma("op1", lambda e, hs=hs: e.dma_start(out=yaT_h, in_=yaT_d[:, :, hs].rearrange("k p t -> p k t")), writes=[R_ya])
                for j in range(16):
                    wv1a, rw1a = wload(w_os[j][:, 0:16, :], 16 * 128, lambda b: b.rearrange("p (k c) -> p k c", k=16))
                    wv1b, rw1b = wload(w_os[j][:, 16:32, :], 16 * 128, lambda b: b.rearrange("p (k c) -> p k c", k=16))
                    wv2, rw2 = wload(w_oa[j], 16 * 128, lambda b: b.rearrange("p (k c) -> p k c", k=16))
                    gi = j % 2
                    sdma("op2%d" % gi, lambda e, j=j, hs=hs, gi=gi: e.dma_start(out=gsb[gi], in_=sgs[j][:, hs]), writes=[R_g[gi]])
                    sdma("op3%d" % gi, lambda e, j=j, hs=hs, gi=gi: e.dma_start(out=gab[gi], in_=sga[j][:, hs]), writes=[R_g[gi]])
                    bA, rbA = nb(); bB, rbB = nb()
                    mm_group(bA, rbA, 512, [(wv1a[:, k, :], ysT_h[:, k, :]) for k in range(16)] + [(wv1b[:, k, :], ysT_h[:, 16 + k, :]) for k in range(16)], [rw1a, rw1b, R_hT[0], R_hT[1]])
                    mm_group(bB, rbB, 512, [(wv2[:, k, :], yaT_h[:, k, :]) for k in range(16)], [rw2, R_ya])
                    P.op("dve", lambda e, bA=bA, gi=gi: e.tensor_tensor(m1, bA[:, :], gsb[gi], ALU.mult), reads=[rbA, R_g[gi]], writes=[R_m1])
                    P.op("dve", lambda e, bB=bB, gi=gi: e.tensor_tensor(t2, bB[:, :], gab[gi], ALU.mult), reads=[rbB, R_g[gi]], writes=[R_t2])
                    P.op("dve", lambda e, j=j: e.tensor_tensor(mT_h[:, j, :], m1, t2, ALU.add), reads=[R_m1, R_t2], writes=[R_m])

                def evac(j, hh, bk, rb, h=h):
                    P.op("dve", lambda e: e.tensor_tensor(xT[:, j, h * 512:(h + 1) * 512], bk[:, :], xT[:, j, h * 512:(h + 1) * 512], ALU.add),
                         reads=[rb, R_xT[j][h]], writes=[R_xT[j][h]])
                linear(w_out, 16, 16, lambda k, hh: mT_h[:, k, :], lambda hh: [R_m], evac, halves=(0,))
                phase_barrier()

        def load_x(src):
            for k in range(16):
                for h in range(2):
                    sdma("lx%d" % ((2 * k + h) % 4), lambda e, k=k, h=h: e.dma_start(out=xT[:, k, h * 512:(h + 1) * 512],
                                                                                           in_=src[k * 128:(k + 1) * 128, h * 512:(h + 1) * 512]),
                          writes=[R_xT[k][h]])

        load_x(xT_pre)
        ffn(0)
        rmsnorm(nmix_sb)
        inproj(False)
        ssd(False)
        load_x(xT_own)
        ffn(0)
        rmsnorm(nmix_sb)
        inproj(True)
        ssd(True)
        attention()
        outproj()
        ffn(1)
        outs = []
        for k in range(16):
            sdma("ox%d" % (k % 4), lambda e, k=k: e.dma_start(out=oT[k * 128:(k + 1) * 128, :], in_=xT[:, k, :]),
                  reads=[R_xT[k][0], R_xT[k][1]], writes=[R_out])
        P.op("sp", lambda e: None, reads=[R_out])
        P.emit()
    return nc


def _tile_w(W, KCn):
    K, N = W.shape
    return np.ascontiguousarray(W.reshape(KCn, 128, N // 128, 128).transpose(2, 1, 0, 3))


def _vec_pk(v, n):
    return np.ascontiguousarray(v.reshape(n, 128).T)


def _consts():
    c = {}
    c["c_ident"] = np.eye(128, dtype=np.float32)
    s = np.arange(128)
    c["c_tri"] = (s[:, None] <= s[None, :]).astype(np.float32)
    sel = np.zeros((128, 128), np.float32); sel[127, :] = 1.0
    c["c_sel127"] = sel
    negm = np.where(s[None, :] < s[:, None], NEG, 0.0).astype(np.float32)
    c["c_negm"] = np.ascontiguousarray(np.tile(negm, (1, 4)))
    selh = np.zeros((64, 64, 128), np.float32)
    for h in range(64):
        selh[h, h, :] = 1.0
    c["c_selh"] = selh.reshape(64, 64 * 128)
    c["c_ones"] = np.ones((128, 128), np.float32)
    bd = np.zeros((128, 128), np.float32); bd[:64, :64] = 1.0; bd[64:, 64:] = 1.0
    c["c_bd"] = bd
    slopes = np.array([2.0 ** (-8.0 * (h + 1) / 32) for h in range(32)], dtype=np.float64)
    key = np.arange(128)[:, None, None]; q = np.arange(128)[None, None, :]
    dist_o = (q - key).astype(np.float64)
    ebo = np.where(dist_o >= 0, np.exp(-slopes[None, :, None] * dist_o), 0.0)
    dist_p = (q + 128 - key).astype(np.float64)
    ebp = np.where(dist_p < 128, np.exp(-slopes[None, :, None] * dist_p), 0.0)
    c["c_ebo"] = np.ascontiguousarray(ebo.reshape(128, 4096).astype(np.float32))
    c["c_ebp"] = np.ascontiguousarray(ebp.reshape(128, 4096).astype(np.float32))
    return c


_NC_CACHE = {}


def kernel(x, ffn1_norm, ffn1_w_gate, ffn1_w_up, ffn1_w_down, mix_norm, w_in, conv_w, conv_b,
           dt_bias, a_log, d_skip, ssm_norm, q_norm, k_norm, sinks, w_o_ssm, w_o_attn, w_out,
           ffn2_norm, ffn2_w_gate, ffn2_w_up, ffn2_w_down):
    f = lambda a: np.asarray(a, dtype=np.float32)
    x = f(x)
    shared = dict(_consts())

    def ffn_w(wg, wu, wd, pre):
        g = _tile_w(f(wg)[0], 16); u = _tile_w(f(wu)[0], 16)
        shared[pre + "_gu"] = np.ascontiguousarray(np.stack([g, u], axis=2))
        d = f(wd)[0].reshape(11, 4, 128, 4, 4, 128).transpose(0, 3, 2, 4, 1, 5)
        shared[pre + "_d"] = np.ascontiguousarray(d)
    ffn_w(ffn1_w_gate, ffn1_w_up, ffn1_w_down, "f1")
    ffn_w(ffn2_w_gate, ffn2_w_up, ffn2_w_down, "f2")
    shared["n_f1"] = _vec_pk(f(ffn1_norm)[0], 16); shared["n_f2"] = _vec_pk(f(ffn2_norm)[0], 16); shared["n_mix"] = _vec_pk(f(mix_norm)[0], 16)
    W = f(w_in)[0]
    o = 0
    seg = {}
    for nm, sz in [("z", 4096), ("xbc", 6144), ("dt", 64), ("q", 2048), ("k", 256), ("v", 256), ("gs", 2048), ("ga", 2048)]:
        seg[nm] = W[:, o:o + sz]; o += sz
    shared["w_z"] = _tile_w(seg["z"], 16); shared["w_xbc"] = _tile_w(seg["xbc"], 16)
    shared["w_dt"] = np.ascontiguousarray(seg["dt"].reshape(16, 128, 64).transpose(1, 0, 2))
    shared["w_q"] = _tile_w(seg["q"], 16)
    shared["w_kv"] = _tile_w(np.concatenate([seg["k"], seg["v"]], axis=1), 16)
    shared["w_gs"] = _tile_w(seg["gs"], 16); shared["w_ga"] = _tile_w(seg["ga"], 16)
    shared["conv_w"] = np.ascontiguousarray(f(conv_w)[0].reshape(4, 48, 128).transpose(2, 1, 0))
    shared["conv_b"] = _vec_pk(f(conv_b)[0], 48)
    bc = lambda v: np.ascontiguousarray(np.broadcast_to(f(v)[0][None, :], (128, f(v).shape[1])))
    shared["dtb_bc"] = bc(dt_bias); shared["alog_bc"] = bc(a_log); shared["dsk_bc"] = bc(d_skip); shared["sinks_bc"] = bc(sinks)
    shared["ssm_n"] = _vec_pk(f(ssm_norm)[0], 32)
    shared["qn"] = np.ascontiguousarray(np.tile(f(q_norm)[0], 2)[:, None]); shared["kn"] = np.ascontiguousarray(np.tile(f(k_norm)[0], 2)[:, None])
    shared["w_os"] = _tile_w(f(w_o_ssm)[0], 32); shared["w_oa"] = _tile_w(f(w_o_attn)[0], 16); shared["w_out"] = _tile_w(f(w_out)[0], 16)
    in_maps = []
    for c in range(8):
        b, hf = c // 2, c % 2
        m = dict(shared)
        m["xT_own"] = np.ascontiguousarray(x[b, hf * T:(hf + 1) * T, :].T)
        m["xT_pre"] = np.ascontiguousarray(x[b, 0:T, :].T) if hf == 1 else np.zeros((2048, T), np.float32)
        m["mflag"] = np.full((128, 1), float(hf), np.float32)
        in_maps.append(m)
    if "nc" not in _NC_CACHE:
        _NC_CACHE["nc"] = build_nc()
    res = run_bass_kernel_spmd(_NC_CACHE["nc"], in_maps, core_ids=list(range(8)))
    out = np.empty((4, 2048, 2048), np.float32)
    for c in range(8):
        b, hf = c // 2, c % 2
        out[b, hf * T:(hf + 1) * T, :] = res.results[c]["oT"].T
    return out
```

```python
import concourse.bass as bass
import concourse.mybir as mybir

ENGS = ("pe", "act", "dve", "pool", "sp")


class Res:
    __slots__ = ("name", "excl", "last_w", "readers")

    def __init__(self, name, excl=False):
        self.name = name
        self.excl = excl
        self.last_w = None
        self.readers = {}


class Op:
    __slots__ = ("eng", "idx", "fn", "deps", "dma_key", "dma_n", "signal", "name")


class Prog:
    def __init__(self, nc):
        self.nc = nc
        self.ops = []
        self.eng_ops = {e: [] for e in ENGS}
        self.dma_cnt = {}
        self.dma_last = {}

    def res(self, name, excl=False):
        return Res(name, excl)

    def _add(self, eng, fn, reads, writes, dma_key=None, name=None):
        op = Op()
        op.eng = eng
        op.fn = fn
        op.name = name
        op.dma_key = dma_key
        op.signal = False
        op.idx = len(self.eng_ops[eng])
        deps = set()
        is_dma = dma_key is not None
        for r in reads:
            if r.excl:
                continue
            if r.last_w is not None:
                deps.add(r.last_w)
        for w in list(writes) + [r for r in reads if r.excl]:
            if w.last_w is not None:
                deps.add(w.last_w)
            for e, rid in w.readers.items():
                deps.add(rid)
        gid = len(self.ops)
        final = set()
        raw_src = set()
        for r in reads:
            if r.last_w is not None:
                raw_src.add(r.last_w)
        for d in deps:
            dop = self.ops[d]
            if dop.dma_key is None and dop.eng == eng and not is_dma:
                if d in raw_src:
                    final.add(d)
            else:
                final.add(d)
        if is_dma:
            prev = self.dma_last.get(dma_key)
            if prev is not None:
                final.add(prev)
            n = self.dma_cnt.get(dma_key, 0) + 1
            self.dma_cnt[dma_key] = n
            op.dma_n = n
            self.dma_last[dma_key] = gid
        else:
            op.dma_n = 0
        op.deps = final
        self.ops.append(op)
        self.eng_ops[eng].append(gid)
        for r in reads:
            if r.excl:
                r.last_w = gid
                r.readers = {}
            else:
                r.readers[("dma", dma_key) if is_dma else eng] = gid
        for w in writes:
            w.last_w = gid
            w.readers = {}
        return gid

    def op(self, eng, fn, reads=(), writes=(), name=None):
        return self._add(eng, fn, list(reads), list(writes), None, name)

    def dma(self, queue, key, fn, reads=(), writes=(), name=None):
        return self._add(queue, fn, list(reads), list(writes), key, name)

    def emit(self, final_wait_ops=()):
        nc = self.nc
        ops = self.ops
        waits = {}
        for e in ENGS:
            known_c = {}
            known_d = {}
            for gid in self.eng_ops[e]:
                op = ops[gid]
                wl = []
                for d in sorted(op.deps):
                    dop = ops[d]
                    if dop.dma_key is not None:
                        if known_d.get(dop.dma_key, 0) >= dop.dma_n:
                            continue
                        known_d[dop.dma_key] = dop.dma_n
                        wl.append(d)
                    else:
                        if known_c.get(dop.eng, -1) >= dop.idx:
                            continue
                        known_c[dop.eng] = dop.idx
                        wl.append(d)
                        dop.signal = True
                waits[gid] = wl
        sig_rank = {}
        for e in ENGS:
            c = 0
            for gid in self.eng_ops[e]:
                op = ops[gid]
                if op.dma_key is None and op.signal:
                    c += 1
                    sig_rank[gid] = c
        self.n_waits = sum(len(v) for v in waits.values())
        self.n_sig = len(sig_rank)
        import contextlib
        with contextlib.ExitStack() as st:
            esem = {e: st.enter_context(nc.semaphore("s_" + e)) for e in ENGS}
            dsem = {k: st.enter_context(nc.semaphore("d_%s" % (k,))) for k in self.dma_cnt}
            block = st.enter_context(nc.Block())
            engobj = {"pe": "tensor", "act": "scalar", "dve": "vector", "pool": "gpsimd", "sp": "sync"}

            def run_engine(e):
                def body(eng):
                    for gid in self.eng_ops[e]:
                        op = ops[gid]
                        for d in waits[gid]:
                            dop = ops[d]
                            if dop.dma_key is not None:
                                eng.wait_ge(dsem[dop.dma_key], 16 * dop.dma_n)
                            else:
                                eng.wait_ge(esem[dop.eng], sig_rank[d])
                        ins = op.fn(eng)
                        if ins is None:
                            continue
                        if op.dma_key is not None:
                            ins.then_inc(dsem[op.dma_key], 16)
                        elif op.signal:
                            ins.then_inc(esem[e], 1)
                return body

            block.tensor(run_engine("pe"))
            block.scalar(run_engine("act"))
            block.vector(run_engine("dve"))
            block.gpsimd(run_engine("pool"))
            block.sync(run_engine("sp"))

import numpy as np
import contextlib
from concourse.bass_utils import run_bass_kernel_spmd

F32 = mybir.dt.float32
BF16 = mybir.dt.bfloat16
AF = mybir.ActivationFunctionType
ALU = mybir.AluOpType
T = 1024
NT = 8
EPS = 1e-6
NEG = -30000.0


def build_nc():
    nc = bass.Bass("TRN2", target_bir_lowering=False)
    di = {}

    def din(name, shape):
        di[name] = nc.dram_tensor(name, list(shape), F32, kind="ExternalInput").ap()
        return di[name]

    xT_own = din("xT_own", [2048, T]); xT_pre = din("xT_pre", [2048, T]); mflag_d = din("mflag", [128, 1])
    f_gu = [din("f1_gu", [44, 128, 2, 16, 128]), din("f2_gu", [44, 128, 2, 16, 128])]
    f_d = [din("f1_d", [11, 4, 128, 4, 4, 128]), din("f2_d", [11, 4, 128, 4, 4, 128])]
    n_f = [din("n_f1", [128, 16]), din("n_f2", [128, 16])]
    n_mix_d = din("n_mix", [128, 16])
    w_z = din("w_z", [32, 128, 16, 128]); w_xbc = din("w_xbc", [48, 128, 16, 128]); w_dt_d = din("w_dt", [128, 16, 64])
    w_q = din("w_q", [16, 128, 16, 128]); w_kv = din("w_kv", [4, 128, 16, 128])
    w_gs = din("w_gs", [16, 128, 16, 128]); w_ga = din("w_ga", [16, 128, 16, 128])
    conv_w_d = din("conv_w", [128, 48, 4]); conv_b_d = din("conv_b", [128, 48])
    dtb_d = din("dtb_bc", [128, 64]); alog_d = din("alog_bc", [128, 64]); dsk_d = din("dsk_bc", [128, 64])
    ssmn_d = din("ssm_n", [128, 32]); qn_d = din("qn", [128, 1]); kn_d = din("kn", [128, 1]); sinks_d = din("sinks_bc", [128, 32])
    w_os = din("w_os", [16, 128, 32, 128]); w_oa = din("w_oa", [16, 128, 16, 128]); w_out = din("w_out", [16, 128, 16, 128])
    c_ident = din("c_ident", [128, 128]); c_tri = din("c_tri", [128, 128]); c_sel127 = din("c_sel127", [128, 128])
    c_negm = din("c_negm", [128, 512]); c_selh = din("c_selh", [128, 64 * 128]); c_ones = din("c_ones", [128, 128])
    c_bd = din("c_bd", [128, 128]); c_ebo = din("c_ebo", [128, 4096]); c_ebp = din("c_ebp", [128, 4096])
    oT = nc.dram_tensor("oT", [2048, T], F32, kind="ExternalOutput").ap()

    def dscr(name, shape, dt=BF16):
        return nc.dram_tensor(name, list(shape), dt).ap()

    szT = dscr("szT", [32, 128, T]); bcT = dscr("bcT", [16, 128, T])
    xs_tm = [dscr("xs_tm0", [NT, 128, 4096]), dscr("xs_tm1", [NT, 128, 4096])]
    b_tm = [dscr("b_tm0", [NT, 128, 1024]), dscr("b_tm1", [NT, 128, 1024])]
    qT_d = dscr("qT_d", [16, 128, T]); kT_d = dscr("kT_d", [2, 128, T + 128]); v_tm = dscr("v_tm", [NT + 1, 128, 256])
    sgs = dscr("sgs", [16, 128, T]); sga = dscr("sga", [16, 128, T])
    ysT_d = dscr("ysT_d", [32, 128, T]); yaT_d = dscr("yaT_d", [16, 128, T])

    st = contextlib.ExitStack()
    with st:
        P = Prog(nc)

        def sb(name, shape, dt=F32):
            return st.enter_context(nc.sbuf_tensor(name, list(shape), dt))

        xT = sb("xT", [128, 16, T]); hT = sb("hT", [128, 16, T], BF16)
        R_xT = [[P.res("xT%d_%d" % (k, h)) for h in range(2)] for k in range(16)]
        R_hT = [P.res("hT%d" % h) for h in range(2)]
        NWB = 5
        wbuf = [sb("wbuf%d" % i, [128, 2048], BF16) for i in range(NWB)]
        R_wb = [P.res("wb%d" % i) for i in range(NWB)]
        wb_i = [0]
        SCR = sb("SCR", [128, 10880])
        R_SCR = P.res("SCRALL")
        ident_bf = sb("ident_bf", [128, 128], BF16); ident_f = sb("ident_f", [128, 128]); tri_f = sb("tri_f", [128, 128])
        sel127_f = sb("sel127_f", [128, 128]); negm_bf = sb("negm_bf", [128, 512], BF16); selh_bf = sb("selh_bf", [128, 64 * 128], BF16)
        ones_bf = sb("ones_bf", [128, 128], BF16); bd_bf = sb("bd_bf", [128, 128], BF16)
        nf_sb = [sb("nf1", [128, 16]), sb("nf2", [128, 16])]; nmix_sb = sb("nmix", [128, 16])
        wdt_sb = sb("wdt", [128, 16, 64], BF16); convw_sb = sb("convw", [128, 48, 4]); convb_sb = sb("convb", [128, 48])
        dtb_sb = sb("dtb", [128, 64]); a_sb = sb("a_sb", [128, 64]); dsk_sb = sb("dsk", [128, 64]); ssmn_sb = sb("ssmn", [128, 32])
        eps_sb = sb("eps_sb", [128, 1]); one_sb = sb("one_sb", [128, 1]); qn_sb = sb("qn_sb", [128, 1]); kn_sb = sb("kn_sb", [128, 1]); esink_sb = sb("esink", [128, 32]); mflag_sb = sb("mflag_sb", [128, 1])
        utail = sb("utail", [128, 48, 3]); S_run = sb("S_run", [128, 4096]); S_bf = sb("S_bf", [128, 4096], BF16)
        R_out = P.res("out"); R_const = P.res("const"); R_utail = P.res("utail"); R_S = P.res("S"); R_Sbf = P.res("Sbf")
        banks = [st.enter_context(nc.psum_tensor("bank%d" % i, [128, 512], F32)) for i in range(8)]
        R_bank = [P.res("bank%d" % i, excl=True) for i in range(8)]
        bk_i = [0]

        def nb():
            i = bk_i[0] % 8
            bk_i[0] += 1
            return banks[i], R_bank[i]

        pa_i = [0]; pb_i = [0]

        def nbA():
            i = pa_i[0] % 4
            pa_i[0] += 1
            return banks[i], R_bank[i]

        def nbB():
            i = 4 + pb_i[0] % 4
            pb_i[0] += 1
            return banks[i], R_bank[i]

        cst_n = [0]

        cst_res = []

        def cload(dst, src, cast):
            cst_n[0] += 1
            q = "pool" if cast else "sp"
            r = P.res("cst%d" % cst_n[0])
            cst_res.append(r)
            P.dma(q, "cst%d" % cst_n[0], lambda e: e.dma_start(out=dst, in_=src), writes=[r])

        cload(ident_bf[:], c_ident, True); cload(ident_f[:], c_ident, False); cload(tri_f[:], c_tri, False)
        cload(sel127_f[:], c_sel127, False); cload(negm_bf[:], c_negm, True); cload(selh_bf[:], c_selh, True)
        cload(ones_bf[:], c_ones, True); cload(bd_bf[:], c_bd, True)
        cload(nf_sb[0][:], n_f[0], False); cload(nf_sb[1][:], n_f[1], False); cload(nmix_sb[:], n_mix_d, False)
        cload(wdt_sb[:], w_dt_d, True); cload(convw_sb[:], conv_w_d, False); cload(convb_sb[:], conv_b_d, False)
        cload(dtb_sb[:], dtb_d, False); cload(a_sb[:], alog_d, False); cload(dsk_sb[:], dsk_d, False); cload(ssmn_sb[:], ssmn_d, False)
        cload(qn_sb[:], qn_d, False); cload(kn_sb[:], kn_d, False); cload(esink_sb[:], sinks_d, False); cload(mflag_sb[:], mflag_d, False)
        P.op("dve", lambda e: e.memset(eps_sb[:], EPS), reads=cst_res, writes=[R_const])
        P.op("act", lambda e: e.activation(a_sb[:], a_sb[:], AF.Exp), reads=[R_const], writes=[R_const])
        P.op("dve", lambda e: e.tensor_scalar(a_sb[:], a_sb[:], -1.0, None, ALU.mult), reads=[R_const], writes=[R_const])
        P.op("act", lambda e: e.activation(esink_sb[:], esink_sb[:], AF.Exp), reads=[R_const], writes=[R_const])
        P.op("dve", lambda e: e.memset(utail[:], 0.0), writes=[R_utail])
        P.op("dve", lambda e: e.memset(one_sb[:], 1.0), reads=[R_const], writes=[R_const])
        P.op("dve", lambda e: e.memset(S_run[:], 0.0), writes=[R_S])

        class Carver:
            def __init__(self):
                self.off = 0

            def f32(self, n, shape=None):
                v = SCR[:, self.off:self.off + n]
                self.off += n
                assert self.off <= 10880, self.off
                return v

            def bf(self, n):
                assert n % 2 == 0
                v = SCR[:, self.off:self.off + n // 2].bitcast(BF16)
                self.off += n // 2
                assert self.off <= 10880, self.off
                return v

        R_dr = {n: P.res("dr_" + n) for n in ["szT", "bcT", "xs_tm", "b_tm", "qT_d", "kT_d", "v_tm", "sgs", "sga", "ysT_d", "yaT_d"]}
        last_bar = [None]

        def sdma(key, fn, reads=(), writes=()):
            return P.dma("sp", key, fn, reads=list(reads) + list(R_dr.values()), writes=list(writes))

        def phase_barrier(wb=False):
            last_bar[0] = P.op("dve", lambda e: e.memset(SCR[:, 0:1], 0.0), writes=[R_SCR] + phase_res[0] + list(R_dr.values()) + (list(R_wb) if wb else []))
            phase_res[0] = []

        phase_res = [[]]

        def pres(name):
            r = P.res(name)
            r.last_w = last_bar[0]
            phase_res[0].append(r)
            return r

        def wload(src_ap, nelem, view):
            i = wb_i[0] % NWB
            wb_i[0] += 1
            buf = wbuf[i][:, 0:nelem]
            P.dma("pool", "wb%d" % i, lambda e: e.dma_start(out=view(buf), in_=src_ap), writes=[R_wb[i]])
            return view(buf), R_wb[i]

        def mm_group(bank, rb, n, pairs, reads):
            def fn(e):
                ins = None
                L = len(pairs)
                for i, (l, r) in enumerate(pairs):
                    ins = e.matmul(bank[:, 0:n], l, r, start=(i == 0), stop=(i == L - 1))
                return ins
            P.op("pe", fn, reads=reads, writes=[rb])

        def rmsnorm(gain_sb):
            c = Carver()
            sq = [c.bf(4 * 512) for _ in range(2)]
            R_sq = [pres("sq0"), pres("sq1")]
            rstd = c.f32(512); R_rstd = pres("rstd")
            for h in range(2):
                bank, rb = nb()
                for k4 in range(4):
                    i = k4 % 2
                    sqv = sq[i].rearrange("p (k t) -> p k t", k=4)
                    P.op("act", lambda e, sqv=sqv, k4=k4, h=h: e.activation(sqv, xT[:, 4 * k4:4 * k4 + 4, h * 512:(h + 1) * 512], AF.Square),
                         reads=[R_xT[k][h] for k in range(4 * k4, 4 * k4 + 4)], writes=[R_sq[i]])

                    def fn(e, sqv=sqv, k4=k4, bank=bank):
                        ins = None
                        for kk in range(4):
                            ins = e.matmul(bank[:, :], ones_bf[:], sqv[:, kk, :], start=(k4 == 0 and kk == 0), stop=(k4 == 3 and kk == 3))
                        return ins
                    P.op("pe", fn, reads=[R_sq[i], R_const], writes=[rb])
                P.op("act", lambda e, bank=bank: e.activation(rstd, bank[:, :], AF.Ln, bias=eps_sb[:, 0:1], scale=1.0 / 2048), reads=[rb, R_const], writes=[R_rstd])
                P.op("act", lambda e: e.activation(rstd, rstd, AF.Exp, scale=-0.5), reads=[R_rstd], writes=[R_rstd])
                for k in range(16):
                    P.op("dve", lambda e, k=k, h=h: e.scalar_tensor_tensor(hT[:, k, h * 512:(h + 1) * 512], xT[:, k, h * 512:(h + 1) * 512],
                                                                        gain_sb[:, k:k + 1], rstd, ALU.mult, ALU.mult),
                         reads=[R_xT[k][h], R_rstd, R_const], writes=[R_hT[h]])
            phase_barrier()

        def ffn(fi):
            rmsnorm(nf_sb[fi])
            c = Carver()
            actb = [c.bf(4 * T).rearrange("p (k t) -> p k t", k=4) for _ in range(2)]
            R_act = [pres("act0"), pres("act1")]
            sg = [c.f32(512) for _ in range(2)]; R_sg = [pres("sg0"), pres("sg1")]
            sgi = 0
            for q in range(11):
                ab = actb[q % 2]; rab = R_act[q % 2]
                for j in range(4):
                    jj = q * 4 + j
                    wg, rwg = wload(f_gu[fi][jj][:, 0, :, :], 16 * 128, lambda b: b.rearrange("p (k c) -> p k c", k=16))
                    wu, rwu = wload(f_gu[fi][jj][:, 1, :, :], 16 * 128, lambda b: b.rearrange("p (k c) -> p k c", k=16))
                    for h in range(2):
                        bg, rbg = nb(); bu, rbu = nb()
                        mm_group(bg, rbg, 512, [(wg[:, k, :], hT[:, k, h * 512:(h + 1) * 512]) for k in range(16)], [rwg, R_hT[h]])
                        mm_group(bu, rbu, 512, [(wu[:, k, :], hT[:, k, h * 512:(h + 1) * 512]) for k in range(16)], [rwu, R_hT[h]])
                        s = sg[sgi % 2]; rs = R_sg[sgi % 2]; sgi += 1
                        P.op("act", lambda e, s=s, bg=bg: e.activation(s, bg[:, :], AF.Silu), reads=[rbg], writes=[rs])
                        P.op("dve", lambda e, s=s, bu=bu, ab=ab, j=j, h=h: e.tensor_tensor(ab[:, j, h * 512:(h + 1) * 512], bu[:, :], s, ALU.mult),
                             reads=[rbu, rs], writes=[rab])
                for cg in range(4):
                    wv, rw = wload(f_d[fi][q, cg], 4 * 4 * 128, lambda b: b.rearrange("p (g k c) -> p g k c", g=4, k=4))
                    for gi in range(4):
                        cc = 4 * cg + gi
                        for h in range(2):
                            bk, rb = nb()
                            mm_group(bk, rb, 512, [(wv[:, gi, k, :], ab[:, k, h * 512:(h + 1) * 512]) for k in range(4)], [rw, rab])
                            P.op("dve", lambda e, bk=bk, cc=cc, h=h: e.scalar_tensor_tensor(xT[:, cc, h * 512:(h + 1) * 512], bk[:, :], 0.5,
                                                                                         xT[:, cc, h * 512:(h + 1) * 512], ALU.mult, ALU.add),
                                 reads=[rb, R_xT[cc][h]], writes=[R_xT[cc][h]])
            phase_barrier()

        def linear(w_ap, NJ, KCn, rhs_fn, rhs_res, evac, G=1, halves=(0, 1), defer=0):
            pend = []
            for j0 in range(0, NJ, G):
                g = min(G, NJ - j0)
                wv, rw = wload(w_ap[j0:j0 + g].rearrange("g p k c -> p g k c"), g * KCn * 128,
                               lambda b, g=g: b.rearrange("p (g k c) -> p g k c", g=g, k=KCn))
                for gi in range(g):
                    for h in halves:
                        bk, rb = nb()
                        mm_group(bk, rb, 512, [(wv[:, gi, k, :], rhs_fn(k, h)) for k in range(KCn)], [rw] + rhs_res(h))
                        pend.append(evac(j0 + gi, h, bk, rb))
                        if len(pend) > defer:
                            d0 = pend.pop(0)
                            if d0 is not None:
                                d0()
            for d0 in pend:
                if d0 is not None:
                    d0()

        def pipeline(items, stages, extras=(), rev=False):
            n = len(items); S_ = len(stages)
            nsteps = n + S_ - 1
            extras = list(extras)
            per = -(-len(extras) // nsteps) if extras else 0
            for step in range(nsteps):
                order = range(S_ - 1, -1, -1) if rev else range(S_)
                for si in order:
                    k = step - si
                    if 0 <= k < n:
                        stages[si](items[k])
                for _ in range(per):
                    if extras:
                        extras.pop(0)()
            for o in extras:
                o()

        def hT_rhs(k, h):
            return hT[:, k, h * 512:(h + 1) * 512]

        def hT_res(h):
            return [R_hT[h]]

        def inproj(own):
            pz = 1 if own else 0
            c = Carver()
            stg = [c.bf(T) for _ in range(3)]; R_stg = [pres("stg%d" % i) for i in range(3)]
            stg_i = [0]
            U = [c.f32(T + 4) for _ in range(2)]; R_U = [pres("U0"), pres("U1")]
            acc = [c.f32(T) for _ in range(2)]; R_acc = [pres("acc0"), pres("acc1")]
            tst = [c.bf(NT * 128).rearrange("p (t c) -> p t c", t=NT) for _ in range(2)]; R_tst = [pres("tst0"), pres("tst1")]
            tst_i = [0]
            qraw = [c.f32(512) for _ in range(2)]; R_qraw = [pres("qraw0"), pres("qraw1")]
            sqq = [c.bf(512) for _ in range(2)]; R_sqq = [pres("sqq0"), pres("sqq1")]
            rq = [c.f32(512) for _ in range(2)]; R_rq = [pres("rq0"), pres("rq1")]
            qi = [0]
            dcnt = [0]

            def dkey():
                dcnt[0] += 1
                return "ip%d" % (dcnt[0] % 6)

            def getstg():
                i = stg_i[0] % 3
                stg_i[0] += 1
                return stg[i], R_stg[i]

            def act_store(func, dst_d):
                def evac(j, h, bk, rb):
                    s, rs = getstg()
                    P.op("act", lambda e: e.activation(s[:, 0:512], bk[:, :], func), reads=[rb], writes=[rs])
                    sdma(dkey(), lambda e: e.dma_start(out=dst_d[j][:, h * 512:(h + 1) * 512], in_=s[:, 0:512]), reads=[rs])
                    return None
                return evac

            def transpose_store(src_bf, rsrc, dst_ap_fn):
                bk, rb = nb()
                bkb = bk[:, :].bitcast(BF16)

                def fn(e):
                    ins = None
                    for t in range(NT):
                        ins = e.transpose(bkb[:, t * 128:(t + 1) * 128], src_bf[:, t * 128:(t + 1) * 128], ident_bf[:])
                    return ins
                P.op("pe", fn, reads=[rsrc, R_const], writes=[rb])
                i = tst_i[0] % 2
                tst_i[0] += 1
                P.op("act", lambda e: e.activation(tst[i].rearrange("p t c -> p (t c)"), bkb, AF.Copy), reads=[rb], writes=[R_tst[i]])
                sdma(dkey(), lambda e: dst_ap_fn(e, tst[i]), reads=[R_tst[i]])

            def qk_evac(gain_sb, store):
                def evac(j, h, bk, rb):
                    i = qi[0] % 2
                    qi[0] += 1
                    P.op("act", lambda e: e.activation(qraw[i], bk[:, :], AF.Copy), reads=[rb], writes=[R_qraw[i]])
                    P.op("act", lambda e: e.activation(sqq[i], bk[:, :], AF.Square), reads=[rb], writes=[R_sqq[i]])

                    def later():
                        b2, rb2 = nb()
                        P.op("pe", lambda e: e.matmul(b2[:, :], bd_bf[:], sqq[i], start=True, stop=True), reads=[R_sqq[i], R_const], writes=[rb2])
                        P.op("act", lambda e: e.activation(rq[i], b2[:, :], AF.Ln, bias=eps_sb[:, 0:1], scale=1.0 / 64), reads=[rb2, R_const], writes=[R_rq[i]])
                        P.op("act", lambda e: e.activation(rq[i], rq[i], AF.Exp, scale=-0.5), reads=[R_rq[i]], writes=[R_rq[i]])
                        s, rs = getstg()
                        P.op("dve", lambda e: e.scalar_tensor_tensor(s[:, 0:512], qraw[i], gain_sb[:, 0:1], rq[i], ALU.mult, ALU.mult),
                             reads=[R_qraw[i], R_rq[i], R_const], writes=[rs])
                        store(j, h, s, rs)
                    return later
                return evac

            def xbc_evac(j, h, bk, rb):
                ub = U[j % 2]; rub = R_U[j % 2]; ab = acc[j % 2]; rab = R_acc[j % 2]
                P.op("act", lambda e: e.activation(ub[:, 3 + h * 512:3 + (h + 1) * 512], bk[:, :], AF.Copy), reads=[rb], writes=[rub])
                if h == 0:
                    return None
                if (not own) and j >= 40:
                    P.op("dve", lambda e: e.tensor_copy(utail[:, j, :], ub[:, T:T + 3]), reads=[rub], writes=[R_utail])
                    return None
                P.op("dve", lambda e: e.tensor_copy(ub[:, 0:3], utail[:, j, :]), reads=[R_utail], writes=[rub])
                P.op("dve", lambda e: e.tensor_scalar(ab, ub[:, 0:T], convw_sb[:, j, 0:1], None, ALU.mult), reads=[rub, R_const], writes=[rab])
                for kk in range(1, 4):
                    P.op("dve", lambda e, kk=kk: e.scalar_tensor_tensor(ab, ub[:, kk:kk + T], convw_sb[:, j, kk:kk + 1], ab, ALU.mult, ALU.add),
                         reads=[rub, rab, R_const], writes=[rab])
                if not own:
                    P.op("dve", lambda e: e.tensor_copy(utail[:, j, :], ub[:, T:T + 3]), reads=[rub], writes=[R_utail])

                def later():
                    s, rs = getstg()
                    P.op("act", lambda e: e.activation(s, ab, AF.Silu, bias=convb_sb[:, j:j + 1]), reads=[rab, R_const], writes=[rs])
                    if j < 32:
                        transpose_store(s, rs, lambda e, stt: e.dma_start(out=xs_tm[pz][:, :, j * 128:(j + 1) * 128].rearrange("t p c -> p t c"), in_=stt))
                    elif j < 40:
                        transpose_store(s, rs, lambda e, stt: e.dma_start(out=b_tm[pz][:, :, (j - 32) * 128:(j - 31) * 128].rearrange("t p c -> p t c"), in_=stt))
                    if own and j >= 32:
                        sdma(dkey(), lambda e: e.dma_start(out=bcT[j - 32], in_=s), reads=[rs])
                return later

            def k_store(j, h, s, rs):
                if own:
                    sdma(dkey(), lambda e: e.dma_start(out=kT_d[j][:, 128 + h * 512:128 + (h + 1) * 512], in_=s[:, 0:512]), reads=[rs])
                elif h == 1:
                    sdma(dkey(), lambda e: e.dma_start(out=kT_d[j][:, 0:128], in_=s[:, 384:512]), reads=[rs])

            def q_store(j, h, s, rs):
                sdma(dkey(), lambda e: e.dma_start(out=qT_d[j][:, h * 512:(h + 1) * 512], in_=s[:, 0:512]), reads=[rs])

            vfull = [None]

            def kv_evac(j, h, bk, rb):
                if j < 2:
                    return qk_evac(kn_sb, k_store)(j, h, bk, rb)
                s = acc[j % 2].bitcast(BF16)[:, 0:T]; rs = R_acc[j % 2]
                P.op("act", lambda e: e.activation(s[:, h * 512:(h + 1) * 512], bk[:, :], AF.Copy), reads=[rb], writes=[rs])
                if h == 0:
                    return None
                jv = j - 2

                def later():
                    if own:
                        transpose_store(s, rs, lambda e, stt: e.dma_start(out=v_tm[1:NT + 1, :, jv * 128:(jv + 1) * 128].rearrange("t p c -> p t c"), in_=stt))
                    else:
                        transpose_store(s, rs, lambda e, stt: e.dma_start(out=v_tm[0][:, jv * 128:(jv + 1) * 128], in_=stt[:, NT - 1, :]))
                return later

            if own:
                linear(w_z, 32, 16, hT_rhs, hT_res, act_store(AF.Silu, szT))
                linear(w_gs, 16, 16, hT_rhs, hT_res, act_store(AF.Sigmoid, sgs))
                linear(w_ga, 16, 16, hT_rhs, hT_res, act_store(AF.Sigmoid, sga))
                linear(w_q, 16, 16, hT_rhs, hT_res, qk_evac(qn_sb, q_store), defer=1)
            linear(w_kv, 4, 16, hT_rhs, hT_res, kv_evac, defer=1)
            linear(w_xbc, 48 if own else 40, 16, hT_rhs, hT_res, xbc_evac, defer=2)
            if not own:
                linear(w_xbc[40:48], 8, 16, hT_rhs, hT_res, lambda j, h, bk, rb: xbc_evac(j + 40, h, bk, rb), halves=(1,))
            phase_barrier(wb=True)

        def ssd(own):
            pz = 1 if own else 0
            c = Carver()
            xs_b = [c.bf(4096) for _ in range(2)]; R_xs = [pres("xs0"), pres("xs1")]
            xsc2 = [c.bf(512) for _ in range(2)]; R_xsc2 = [pres("xsc0"), pres("xsc1")]
            BTs = [wbuf[i][:, 0:1024].rearrange("p (g n) -> p g n", g=8) for i in range(2)]
            CTs = [wbuf[i][:, 1024:2048].rearrange("p (g n) -> p g n", g=8) for i in range(2)]
            R_BC = [pres("BC0"), pres("BC1")]
            NMS = ["dtx", "ax", "ee", "dt", "adt", "acum", "lndt", "bias", "expA", "dd", "dtdec", "cdec"]
            sms = [{nm: wbuf[2 + i][:, k * 128:(k + 1) * 128].bitcast(F32) for k, nm in enumerate(NMS)} for i in range(2)]
            R_sms = [pres("sm0"), pres("sm1")]
            AT_his = [wbuf[2 + i][:, 1536:1664] for i in range(2)]
            adt2s = [wbuf[2 + i][:, 1664:1920].bitcast(F32) for i in range(2)]
            for i in range(2):
                sms[i]["adt"] = adt2s[i][:, 0:64]
            R_ATs = [pres("AT0"), pres("AT1")]
            b_t = wbuf[4][:, 0:1024]; R_bt = pres("b_t")
            sz_t = [wbuf[4][:, 1024:1536], wbuf[4][:, 1536:2048]]; R_sz = [pres("sz_t0"), pres("sz_t1")]
            if own:
                xsd = [c.bf(512) for _ in range(2)]; R_xsd = [pres("xsd0"), pres("xsd1")]
                cbT = [c.bf(128) for _ in range(2)]; R_cb = [pres("cbT0"), pres("cbT1")]
                E = [[c.bf(512) for _ in range(2)] for _ in range(2)]; R_E = [[pres("E%d%d" % (a_, b_)) for b_ in range(2)] for a_ in range(2)]
                WT = E; R_WT = R_E
                t1 = [c.f32(512) for _ in range(2)]; R_t1 = [pres("t10"), pres("t11")]
                ytm = [c.bf(512) for _ in range(2)]; R_ytm = [pres("ytm0"), pres("ytm1")]
                yg = [c.f32(512) for _ in range(2)]; R_yg = [pres("yg0"), pres("yg1")]
                sqy = [c.bf(512) for _ in range(2)]; R_sqy = [pres("sqy0"), pres("sqy1")]
                rs_y = c.f32(128); R_rsy = pres("rs_y")
                ysn = [c.bf(512) for _ in range(2)]; R_ysn = [pres("ysn0"), pres("ysn1")]

            def prep_ops(t):
                q_ = t % 2
                sm = sms[q_]; R_sm = R_sms[q_]; xs_t = xs_b[q_]
                ts = slice(t * 128, (t + 1) * 128)
                ops = []
                ops.append(lambda: sdma("ss0%d" % q_, lambda e: e.dma_start(out=xs_t, in_=xs_tm[pz][t]), writes=[R_xs[q_]]))
                if own:
                    ops.append(lambda: sdma("ss2%d" % q_, lambda e: e.dma_start(out=BTs[q_], in_=bcT[0:8, :, ts].rearrange("g p n -> p g n")), writes=[R_BC[q_]]))
                    ops.append(lambda: sdma("ss3%d" % q_, lambda e: e.dma_start(out=CTs[q_], in_=bcT[8:16, :, ts].rearrange("g p n -> p g n")), writes=[R_BC[q_]]))
                st_ = {}

                def o1():
                    st_["bk"], st_["rb"] = nbB()
                    mm_group(st_["bk"], st_["rb"], 64, [(hT[:, k, ts], wdt_sb[:, k, :]) for k in range(16)], [R_hT[t // 4], R_const])
                    P.op("dve", lambda e: e.tensor_tensor(sm["dtx"], st_["bk"][:, 0:64], dtb_sb[:], ALU.add), reads=[st_["rb"], R_const], writes=[R_sm])
                ops.append(o1)
                ops.append(lambda: P.op("act", lambda e: e.activation(sm["ax"], sm["dtx"], AF.Abs), reads=[R_sm], writes=[R_sm]))
                ops.append(lambda: P.op("act", lambda e: e.activation(sm["ee"], sm["ax"], AF.Exp, scale=-1.0), reads=[R_sm], writes=[R_sm]))
                ops.append(lambda: P.op("act", lambda e: e.activation(sm["ee"], sm["ee"], AF.Ln, bias=one_sb[:, 0:1]), reads=[R_sm, R_const], writes=[R_sm]))
                ops.append(lambda: P.op("dve", lambda e: e.scalar_tensor_tensor(sm["dt"], sm["dtx"], 0.0, sm["ee"], ALU.max, ALU.add), reads=[R_sm], writes=[R_sm]))
                ops.append(lambda: P.op("dve", lambda e: e.tensor_tensor(sm["adt"], sm["dt"], a_sb[:], ALU.mult), reads=[R_sm, R_const], writes=[R_sm]))
                if own:
                    ops.append(lambda: P.op("dve", lambda e: e.tensor_tensor(adt2s[q_][:, 64:128], sm["dt"], a_sb[:], ALU.mult), reads=[R_sm, R_const], writes=[R_sm]))

                def o2():
                    bk2, rb2 = nbB()
                    P.op("pe", lambda e: e.matmul(bk2[:, 0:64], tri_f[:], sm["adt"], start=True, stop=True), reads=[R_sm, R_const], writes=[rb2])
                    P.op("act", lambda e: e.activation(sm["acum"], bk2[:, 0:64], AF.Copy), reads=[rb2], writes=[R_sm])
                ops.append(o2)

                def o3():
                    bk4, rb4 = nbB()
                    P.op("pe", lambda e: e.matmul(bk4[:, 0:64], sel127_f[:], sm["acum"], start=True, stop=True), reads=[R_sm, R_const], writes=[rb4])
                    P.op("dve", lambda e: e.tensor_tensor(sm["dd"], bk4[:, 0:64], sm["acum"], ALU.subtract), reads=[rb4, R_sm], writes=[R_sm])
                    P.op("act", lambda e: e.activation(sm["cdec"], bk4[:, 0:64], AF.Exp), reads=[rb4], writes=[R_sm])
                ops.append(o3)
                ops.append(lambda: P.op("act", lambda e: e.activation(sm["dd"], sm["dd"], AF.Exp), reads=[R_sm], writes=[R_sm]))
                ops.append(lambda: P.op("dve", lambda e: e.tensor_tensor(sm["dtdec"], sm["dd"], sm["dt"], ALU.mult), reads=[R_sm], writes=[R_sm]))
                if own:
                    def o4():
                        bk3, rb3 = nbB()
                        P.op("pe", lambda e: e.matmul(bk3[:, 0:128], adt2s[q_], tri_f[:], start=True, stop=True), reads=[R_sm, R_const], writes=[rb3])
                        P.op("dve", lambda e: e.tensor_copy(AT_his[q_], bk3[:, 0:128]), reads=[rb3], writes=[R_ATs[q_]])
                        P.op("dve", lambda e: e.tensor_tensor(AT_his[q_][64:128, :], bk3[64:128, 0:128], AT_his[q_][64:128, :], ALU.subtract),
                             reads=[rb3, R_ATs[q_]], writes=[R_ATs[q_]])
                    ops.append(o4)
                    ops.append(lambda: P.op("act", lambda e: e.activation(sm["lndt"], sm["dt"], AF.Ln), reads=[R_sm], writes=[R_sm]))
                    ops.append(lambda: P.op("dve", lambda e: e.tensor_tensor(sm["bias"], sm["lndt"], sm["acum"], ALU.subtract), reads=[R_sm], writes=[R_sm]))
                    ops.append(lambda: P.op("act", lambda e: e.activation(sm["expA"], sm["acum"], AF.Exp), reads=[R_sm], writes=[R_sm]))
                return ops

            def state_ops(t):
                q_ = t % 2
                sm = sms[q_]; R_sm = R_sms[q_]; xs_t = xs_b[q_]
                ops = [lambda: sdma("ss1", lambda e: e.dma_start(out=b_t, in_=b_tm[pz][t]), writes=[R_bt])]
                ops.append(lambda: P.op("dve", lambda e: e.tensor_tensor(S_run[:].rearrange("p (h d) -> p h d", h=64), S_run[:].rearrange("p (h d) -> p h d", h=64),
                                                                      sm["cdec"].unsqueeze(2).to_broadcast([128, 64, 64]), ALU.mult),
                                        reads=[R_S, R_sm], writes=[R_S]))
                pend = {}

                def og(g):
                    xb = xsc2[g % 2]; rxb = R_xsc2[g % 2]
                    P.op("dve", lambda e: e.tensor_tensor(xb.rearrange("p (h d) -> p h d", h=8), xs_t[:, g * 512:(g + 1) * 512].rearrange("p (h d) -> p h d", h=8),
                                                          sm["dtdec"][:, g * 8:(g + 1) * 8].unsqueeze(2).to_broadcast([128, 8, 64]), ALU.mult),
                         reads=[R_xs[q_], R_sm], writes=[rxb])
                    bS, rbS = nbB()
                    P.op("pe", lambda e: e.matmul(bS[:, :], b_t[:, g * 128:(g + 1) * 128], xb, start=True, stop=True), reads=[R_bt, rxb], writes=[rbS])
                    P.op("dve", lambda e: e.tensor_tensor(S_run[:, g * 512:(g + 1) * 512], S_run[:, g * 512:(g + 1) * 512], bS[:, :], ALU.add),
                         reads=[R_S, rbS], writes=[R_S])
                for g in range(8):
                    ops.append(lambda g=g: og(g))
                return ops

            def y_stages(t):
                q_ = t % 2
                sm = sms[q_]; R_sm = R_sms[q_]; xs_t = xs_b[q_]; BT_t = BTs[q_]; CT_t = CTs[q_]
                AT_hi = AT_his[q_]; R_AT = R_ATs[q_]
                ts = slice(t * 128, (t + 1) * 128)
                ctx = [dict() for _ in range(8)]

                def s0(g):
                    d = ctx[g]
                    d["R"] = []
                    for hh in range(2):
                        bR, rbR = nbA()
                        d["R"].append((bR, rbR))

                        def fnR(e, bR=bR, hh=hh):
                            e.matmul(bR[:, :], ident_bf[:], negm_bf[:], start=True, stop=False)
                            ins = None
                            for i in range(4):
                                hd = g * 8 + hh * 4 + i
                                ins = e.matmul(bR[:, i * 128:(i + 1) * 128], selh_bf[:, hd * 128:(hd + 1) * 128], AT_hi, start=False, stop=(i == 3))
                            return ins
                        P.op("pe", fnR, reads=[R_AT, R_const], writes=[rbR])
                    bkc, rbc = nbA(); d["cb"] = (bkc, rbc)
                    P.op("pe", lambda e: e.matmul(bkc[:, 0:128], BT_t[:, g, :], CT_t[:, g, :], start=True, stop=True), reads=[R_BC[q_]], writes=[rbc])
                    bO, rbO = nbA(); d["O"] = (bO, rbO)
                    P.op("pe", lambda e: e.matmul(bO[:, :], CT_t[:, g, :], S_bf[:, g * 512:(g + 1) * 512], start=True, stop=True),
                         reads=[R_BC[q_], R_Sbf], writes=[rbO])

                def s1(g):
                    d = ctx[g]; st_ = g % 2
                    bkc, rbc = d["cb"]
                    P.op("act", lambda e: e.activation(cbT[st_], bkc[:, 0:128], AF.Copy), reads=[rbc], writes=[R_cb[st_]])
                    for hh in range(2):
                        bR, rbR = d["R"][hh]
                        Eb = E[st_][hh]
                        for i in range(4):
                            hd = g * 8 + hh * 4 + i
                            P.op("act", lambda e, bR=bR, i=i, hd=hd, Eb=Eb: e.activation(Eb[:, i * 128:(i + 1) * 128], bR[:, i * 128:(i + 1) * 128], AF.Exp,
                                                                                       bias=sm["bias"][:, hd:hd + 1]),
                                 reads=[rbR, R_sm], writes=[R_E[st_][hh]])
                    bO, rbO = d["O"]
                    P.op("dve", lambda e: e.tensor_tensor(xsd[st_].rearrange("p (h d) -> p h d", h=8), xs_t[:, g * 512:(g + 1) * 512].rearrange("p (h d) -> p h d", h=8),
                                                          dsk_sb[:, g * 8:(g + 1) * 8].unsqueeze(2).to_broadcast([128, 8, 64]), ALU.mult), reads=[R_xs[q_], R_const], writes=[R_xsd[st_]])
                    P.op("dve", lambda e: e.tensor_tensor(t1[st_].rearrange("p (h d) -> p h d", h=8), bO[:, :].rearrange("p (h d) -> p h d", h=8),
                                                          sm["expA"][:, g * 8:(g + 1) * 8].unsqueeze(2).to_broadcast([128, 8, 64]), ALU.mult),
                         reads=[rbO, R_sm], writes=[R_t1[st_]])

                def s2(g):
                    st_ = g % 2
                    for hh in range(2):
                        Eb = E[st_][hh]; Wb = WT[st_][hh]
                        P.op("dve", lambda e, Eb=Eb, Wb=Wb: e.tensor_tensor(Wb.rearrange("p (i l) -> p i l", i=4), Eb.rearrange("p (i l) -> p i l", i=4),
                                                                              cbT[st_].unsqueeze(1).to_broadcast([128, 4, 128]), ALU.mult),
                             reads=[R_E[st_][hh], R_cb[st_]], writes=[R_WT[st_][hh]])

                def s3(g):
                    st_ = g % 2
                    bY, rbY = nbB()

                    def fnY(e):
                        e.matmul(bY[:, :], ident_bf[:], xsd[st_], start=True, stop=False)
                        ins = None
                        for hl in range(8):
                            hd = g * 8 + hl
                            Wb = WT[st_][hl // 4]
                            ins = e.matmul(bY[:, hl * 64:(hl + 1) * 64], Wb[:, (hl % 4) * 128:(hl % 4 + 1) * 128], xs_t[:, hd * 64:(hd + 1) * 64],
                                           start=False, stop=(hl == 7))
                        return ins
                    P.op("pe", fnY, reads=[R_xsd[st_], R_WT[st_][0], R_WT[st_][1], R_xs[q_], R_const], writes=[rbY])
                    P.op("dve", lambda e: e.tensor_tensor(ytm[st_], bY[:, :], t1[st_], ALU.add), reads=[rbY, R_t1[st_]], writes=[R_ytm[st_]])
                    sdma("ss4%d" % st_, lambda e: e.dma_start(out=sz_t[st_].rearrange("p (i l) -> p i l", i=4),
                                                            in_=szT[g * 4:(g + 1) * 4, :, ts].rearrange("i p l -> p i l")), writes=[R_sz[st_]])

                def s4(g):
                    st_ = g % 2
                    bT, rbT = nbB()

                    bTb = bT[:, :].bitcast(BF16)

                    def fnT(e):
                        ins = None
                        for i in range(4):
                            ins = e.transpose(bTb[:, i * 128:(i + 1) * 128], ytm[st_][:, i * 128:(i + 1) * 128], ident_bf[:])
                        return ins
                    P.op("pe", fnT, reads=[R_ytm[st_], R_const], writes=[rbT])
                    P.op("dve", lambda e: e.tensor_tensor(yg[st_], bTb[:, 0:512], sz_t[st_], ALU.mult), reads=[rbT, R_sz[st_]], writes=[R_yg[st_]])
                    P.op("act", lambda e: e.activation(sqy[st_], yg[st_], AF.Square), reads=[R_yg[st_]], writes=[R_sqy[st_]])

                def s5(g):
                    st_ = g % 2
                    bN, rbN = nbB()
                    mm_group(bN, rbN, 128, [(ones_bf[:], sqy[st_][:, i * 128:(i + 1) * 128]) for i in range(4)], [R_sqy[st_], R_const])
                    P.op("act", lambda e: e.activation(rs_y, bN[:, 0:128], AF.Ln, bias=eps_sb[:, 0:1], scale=1.0 / 512), reads=[rbN, R_const], writes=[R_rsy])
                    P.op("act", lambda e: e.activation(rs_y, rs_y, AF.Exp, scale=-0.5), reads=[R_rsy], writes=[R_rsy])
                    yb = ysn[st_]; ryb = R_ysn[st_]
                    for i in range(4):
                        jj = g * 4 + i
                        P.op("dve", lambda e, i=i, jj=jj: e.scalar_tensor_tensor(yb[:, i * 128:(i + 1) * 128], yg[st_][:, i * 128:(i + 1) * 128],
                                                                               ssmn_sb[:, jj:jj + 1], rs_y, ALU.mult, ALU.mult),
                             reads=[R_yg[st_], R_rsy, R_const], writes=[ryb])
                    sdma("ss5%d" % st_, lambda e: e.dma_start(out=ysT_d[g * 4:(g + 1) * 4, :, ts].rearrange("i p l -> p i l"),
                                                            in_=yb.rearrange("p (i l) -> p i l", i=4)), reads=[ryb])
                return [s0, s1, s2, s3, s4, s5]

            for o in prep_ops(0):
                o()
            for t in range(NT):
                so_ = state_ops(t); po_ = (prep_ops(t + 1) if t + 1 < NT else [])
                extras = []
                while so_ or po_:
                    if so_:
                        extras.append(so_.pop(0))
                    if po_:
                        extras.append(po_.pop(0))
                    if po_:
                        extras.append(po_.pop(0))
                if own:
                    pipeline(list(range(8)), y_stages(t), extras=extras, rev=True)
                else:
                    for o in extras:
                        o()
                if own and t < NT - 1:
                    P.op("act", lambda e: e.activation(S_bf[:], S_run[:], AF.Copy), reads=[R_S], writes=[R_Sbf])
            if not own:
                P.op("dve", lambda e: e.tensor_scalar(S_run[:], S_run[:], mflag_sb[:, 0:1], None, ALU.mult), reads=[R_S, R_const], writes=[R_S])
                P.op("act", lambda e: e.activation(S_bf[:], S_run[:], AF.Copy), reads=[R_S], writes=[R_Sbf])
            phase_barrier(wb=True)

        def attention():
            c = Carver()
            ebo = c.bf(4096); ebp = c.bf(4096); R_eb = pres("eb")
            qT_t = c.bf(16 * 128).rearrange("p (j q) -> p j q", j=16); R_q = pres("qT_t")
            kT_t = c.bf(4 * 256).rearrange("p (k s) -> p k s", k=4); R_k = pres("kT_t")
            v1 = [c.f32(2 * 4 * 65 // 2 + 2) for _ in range(2)]
            v1 = [v[:, 0:260].bitcast(BF16).rearrange("p (k g d) -> p k g d", k=2, g=4) for v in v1]; R_v = [pres("v0"), pres("v1")]
            Eb = [[c.bf(512) for _ in range(2)] for _ in range(2)]; R_Eb = [[pres("aE%d%d" % (a_, b_)) for b_ in range(2)] for a_ in range(2)]
            PT = [[c.bf(512) for _ in range(2)] for _ in range(2)]; R_PT = [[pres("PT%d%d" % (a, b)) for b in range(2)] for a in range(2)]
            den = c.f32(4); R_den = pres("den")
            ya = c.bf(2048); R_ya = pres("ya")
            yst = c.bf(2048); R_yst = pres("yst")
            P.dma("pool", "at0", lambda e: e.dma_start(out=ebo, in_=c_ebo), writes=[R_eb])
            P.dma("pool", "at1", lambda e: e.dma_start(out=ebp, in_=c_ebp), writes=[R_eb])
            for i in range(2):
                P.op("dve", lambda e, i=i: e.memset(v1[i], 1.0), writes=[R_v[i]])
            for t in range(NT):
                vb = v1[t % 2]; rv = R_v[t % 2]
                sdma("at2", lambda e, t=t: e.dma_start(out=qT_t, in_=qT_d[:, :, t * 128:(t + 1) * 128].rearrange("j p q -> p j q")), writes=[R_q])
                for par in range(2):
                    sdma("at3%d" % par, lambda e, t=t, par=par: e.dma_start(
                        out=kT_t[par * 64:(par + 1) * 64, :, :],
                        in_=kT_d[:, :, t * 128:t * 128 + 256].rearrange("j (u d) s -> d (j u) s", u=2)), writes=[R_k])
                for k2 in range(2):
                    sdma("at4%d%d" % (t % 2, k2), lambda e, t=t, vb=vb, k2=k2: e.dma_start(out=vb[:, k2, :, 0:64],
                                                                                       in_=v_tm[t + k2].rearrange("p (g d) -> p g d", g=4)), writes=[rv])
                actx = {}

                def a0(u, t=t):
                    kv, par = u
                    actx[u] = []
                    for kt in range(2):
                        bS, rbS = nbA()
                        actx[u].append((bS, rbS))
                        P.op("pe", lambda e, bS=bS, kt=kt: e.matmul(
                            bS[:, :], kT_t[par * 64:(par + 1) * 64, kv, kt * 128:(kt + 1) * 128],
                            qT_t[par * 64:(par + 1) * 64, kv * 4:(kv + 1) * 4, :], start=True, stop=True), reads=[R_k, R_q], writes=[rbS])

                def a1(u, t=t):
                    kv, par = u
                    for kt in range(2):
                        bS, rbS = actx[u][kt]
                        Ei = Eb[par][kt]
                        P.op("act", lambda e, bS=bS, Ei=Ei: e.activation(Ei, bS[:, :], AF.Exp, scale=0.125), reads=[rbS], writes=[R_Eb[par][kt]])

                def a2(u, t=t):
                    kv, par = u
                    for kt in range(2):
                        Ei = Eb[par][kt]
                        ebsrc = ebo if kt == 1 else ebp
                        ebv = ebsrc.rearrange("p (h q) -> p h q", h=32)[:, kv * 8 + par:kv * 8 + 8:2, :]
                        P.op("dve", lambda e, Ei=Ei, ebv=ebv, kt=kt: e.tensor_tensor(PT[par][kt].rearrange("p (i q) -> p i q", i=4),
                                                                                  Ei.rearrange("p (i q) -> p i q", i=4), ebv, ALU.mult),
                             reads=[R_Eb[par][kt], R_eb], writes=[R_PT[par][kt]])
                        if t == 0 and kt == 0:
                            P.op("dve", lambda e, kt=kt: e.tensor_scalar(PT[par][kt], PT[par][kt], mflag_sb[:, 0:1], None, ALU.mult),
                                 reads=[R_PT[par][kt], R_const], writes=[R_PT[par][kt]])

                def a3(u, t=t, vb=vb, rv=rv):
                    kv, par = u
                    bO, rbO = nbB()

                    def fnO(e, bO=bO):
                        ins = None
                        for i in range(4):
                            for kt in range(2):
                                ins = e.matmul(bO[:, i * 128:i * 128 + 65], PT[par][kt][:, i * 128:(i + 1) * 128], vb[:, kt, kv, :],
                                               start=(kt == 0), stop=(kt == 1))
                        return ins
                    P.op("pe", fnO, reads=[R_PT[par][0], R_PT[par][1], rv], writes=[rbO])
                    bOv = bO[:, :].rearrange("p (i c) -> p i c", i=4)
                    esv = esink_sb[:, kv * 8 + par:kv * 8 + 8:2]
                    P.op("dve", lambda e: e.tensor_tensor(den.unsqueeze(2), bOv[:, :, 64:65], esv.unsqueeze(2), ALU.add),
                         reads=[rbO, R_const], writes=[R_den])
                    P.op("dve", lambda e: e.reciprocal(den, den), reads=[R_den], writes=[R_den])
                    yav = ya.rearrange("p (h d) -> p h d", h=32)[:, kv * 8 + par:kv * 8 + 8:2, :]
                    P.op("dve", lambda e: e.tensor_tensor(yav, bOv[:, :, 0:64], den.unsqueeze(2).to_broadcast([128, 4, 64]), ALU.mult),
                         reads=[rbO, R_den], writes=[R_ya])
                pipeline([(kv, par) for kv in range(4) for par in range(2)], [a0, a1, a2, a3], rev=True)
                for half in range(2):
                    bT, rbT = nbB()
                    bTb = bT[:, :].bitcast(BF16)

                    def fnT(e, bTb=bTb, half=half):
                        ins = None
                        for i in range(8):
                            cch = half * 8 + i
                            ins = e.transpose(bTb[:, i * 128:(i + 1) * 128], ya[:, cch * 128:(cch + 1) * 128], ident_bf[:])
                        return ins
                    P.op("pe", fnT, reads=[R_ya, R_const], writes=[rbT])
                    P.op("act", lambda e, bTb=bTb, half=half: e.activation(yst[:, half * 1024:(half + 1) * 1024], bTb, AF.Copy), reads=[rbT], writes=[R_yst])
                sdma("at5", lambda e, t=t: e.dma_start(out=yaT_d[:, :, t * 128:(t + 1) * 128].rearrange("j p q -> p j q"),
                                                              in_=yst.rearrange("p (j q) -> p j q", j=16)), reads=[R_yst])
            phase_barrier()

        def outproj():
            for h in range(2):
                hs = slice(h * 512, (h + 1) * 512)
                c = Carver()
                ysT_h = hT[:, :, :].rearrange("p k t -> p (k t)").rearrange("p (k t) -> p k t", k=32)
                yaT_h = c.bf(16 * 512).rearrange("p (k t) -> p k t", k=16); R_ya = pres("yaT_h")
                mT_h = c.bf(16 * 512).rearrange("p (k t) -> p k t", k=16); R_m = pres("mT_h")
                gsb = [c.bf(512) for _ in range(2)]; gab = [c.bf(512) for _ in range(2)]
                R_g = [pres("g0"), pres("g1")]
                m1 = c.f32(512); R_m1 = pres("m1")
                t2 = c.f32(512); R_t2 = pres("t2")
                R_ys = R_hT[0]
                sdma("op0", lambda e, hs=hs: e.dma_start(out=ysT_h, in_=ysT_d[:, :, hs].rearrange("k p t -> p k t")), writes=[R_hT[0], R_hT[1]])
                sdma("op1", lambda e, hs=hs: e.dma_start(out=yaT_h, in_=yaT_d[:, :, hs].rearrange("k p t -> p k t")), writes=[R_ya])
                for j in range(16):
                    wv1a, rw1a = wload(w_os[j][:, 0:16, :], 16 * 128, lambda b: b.rearrange("p (k c) -> p k c", k=16))
                    wv1b, rw1b = wload(w_os[j][:, 16:32, :], 16 * 128, lambda b: b.rearrange("p (k c) -> p k c", k=16))
                    wv2, rw2 = wload(w_oa[j], 16 * 128, lambda b: b.rearrange("p (k c) -> p k c", k=16))
                    gi = j % 2
                    sdma("op2%d" % gi, lambda e, j=j, hs=hs, gi=gi: e.dma_start(out=gsb[gi], in_=sgs[j][:, hs]), writes=[R_g[gi]])
                    sdma("op3%d" % gi, lambda e, j=j, hs=hs, gi=gi: e.dma_start(out=gab[gi], in_=sga[j][:, hs]), writes=[R_g[gi]])
                    bA, rbA = nb(); bB, rbB = nb()
                    mm_group(bA, rbA, 512, [(wv1a[:, k, :], ysT_h[:, k, :]) for k in range(16)] + [(wv1b[:, k, :], ysT_h[:, 16 + k, :]) for k in range(16)], [rw1a, rw1b, R_hT[0], R_hT[1]])
                    mm_group(bB, rbB, 512, [(wv2[:, k, :], yaT_h[:, k, :]) for k in range(16)], [rw2, R_ya])
                    P.op("dve", lambda e, bA=bA, gi=gi: e.tensor_tensor(m1, bA[:, :], gsb[gi], ALU.mult), reads=[rbA, R_g[gi]], writes=[R_m1])
                    P.op("dve", lambda e, bB=bB, gi=gi: e.tensor_tensor(t2, bB[:, :], gab[gi], ALU.mult), reads=[rbB, R_g[gi]], writes=[R_t2])
                    P.op("dve", lambda e, j=j: e.tensor_tensor(mT_h[:, j, :], m1, t2, ALU.add), reads=[R_m1, R_t2], writes=[R_m])

                def evac(j, hh, bk, rb, h=h):
                    P.op("dve", lambda e: e.tensor_tensor(xT[:, j, h * 512:(h + 1) * 512], bk[:, :], xT[:, j, h * 512:(h + 1) * 512], ALU.add),
                         reads=[rb, R_xT[j][h]], writes=[R_xT[j][h]])
                linear(w_out, 16, 16, lambda k, hh: mT_h[:, k, :], lambda hh: [R_m], evac, halves=(0,))
                phase_barrier()

        def load_x(src):
            for k in range(16):
                for h in range(2):
                    sdma("lx%d" % ((2 * k + h) % 4), lambda e, k=k, h=h: e.dma_start(out=xT[:, k, h * 512:(h + 1) * 512],
                                                                                           in_=src[k * 128:(k + 1) * 128, h * 512:(h + 1) * 512]),
                          writes=[R_xT[k][h]])

        load_x(xT_pre)
        ffn(0)
        rmsnorm(nmix_sb)
        inproj(False)
        ssd(False)
        load_x(xT_own)
        ffn(0)
        rmsnorm(nmix_sb)
        inproj(True)
        ssd(True)
        attention()
        outproj()
        ffn(1)
        outs = []
        for k in range(16):
            ro = P.res("out%d" % k)
            outs.append(ro)
            sdma("ox%d" % (k % 8), lambda e, k=k: e.dma_start(out=oT[k * 128:(k + 1) * 128, :], in_=xT[:, k, :]),
                  reads=[R_xT[k][0], R_xT[k][1]], writes=[ro])
        P.op("sp", lambda e: None, reads=outs)
        P.emit()
    return nc


def _tile_w(W, KCn):
    K, N = W.shape
    return np.ascontiguousarray(W.reshape(KCn, 128, N // 128, 128).transpose(2, 1, 0, 3))


def _vec_pk(v, n):
    return np.ascontiguousarray(v.reshape(n, 128).T)


def _consts():
    c = {}
    c["c_ident"] = np.eye(128, dtype=np.float32)
    s = np.arange(128)
    c["c_tri"] = (s[:, None] <= s[None, :]).astype(np.float32)
    sel = np.zeros((128, 128), np.float32); sel[127, :] = 1.0
    c["c_sel127"] = sel
    negm = np.where(s[None, :] < s[:, None], NEG, 0.0).astype(np.float32)
    c["c_negm"] = np.ascontiguousarray(np.tile(negm, (1, 4)))
    selh = np.zeros((128, 64, 128), np.float32)
    for h in range(64):
        selh[h, h, :] = 1.0
        selh[64 + h, h, :] = 1.0
    c["c_selh"] = selh.reshape(128, 64 * 128)
    c["c_ones"] = np.ones((128, 128), np.float32)
    bd = np.zeros((128, 128), np.float32); bd[:64, :64] = 1.0; bd[64:, 64:] = 1.0
    c["c_bd"] = bd
    slopes = np.array([2.0 ** (-8.0 * (h + 1) / 32) for h in range(32)], dtype=np.float64)
    key = np.arange(128)[:, None, None]; q = np.arange(128)[None, None, :]
    dist_o = (q - key).astype(np.float64)
    ebo = np.where(dist_o >= 0, np.exp(-slopes[None, :, None] * dist_o), 0.0)
    dist_p = (q + 128 - key).astype(np.float64)
    ebp = np.where(dist_p < 128, np.exp(-slopes[None, :, None] * dist_p), 0.0)
    c["c_ebo"] = np.ascontiguousarray(ebo.reshape(128, 4096).astype(np.float32))
    c["c_ebp"] = np.ascontiguousarray(ebp.reshape(128, 4096).astype(np.float32))
    return c


_NC_CACHE = {}


def kernel(x, ffn1_norm, ffn1_w_gate, ffn1_w_up, ffn1_w_down, mix_norm, w_in, conv_w, conv_b,
           dt_bias, a_log, d_skip, ssm_norm, q_norm, k_norm, sinks, w_o_ssm, w_o_attn, w_out,
           ffn2_norm, ffn2_w_gate, ffn2_w_up, ffn2_w_down):
    f = lambda a: np.asarray(a, dtype=np.float32)
    x = f(x)
    shared = dict(_consts())

    def ffn_w(wg, wu, wd, pre):
        g = _tile_w(f(wg)[0], 16); u = _tile_w(f(wu)[0], 16)
        shared[pre + "_gu"] = np.ascontiguousarray(np.stack([g, u], axis=2))
        d = f(wd)[0].reshape(11, 4, 128, 4, 4, 128).transpose(0, 3, 2, 4, 1, 5)
        shared[pre + "_d"] = np.ascontiguousarray(d)
    ffn_w(ffn1_w_gate, ffn1_w_up, ffn1_w_down, "f1")
    ffn_w(ffn2_w_gate, ffn2_w_up, ffn2_w_down, "f2")
    shared["n_f1"] = _vec_pk(f(ffn1_norm)[0], 16); shared["n_f2"] = _vec_pk(f(ffn2_norm)[0], 16); shared["n_mix"] = _vec_pk(f(mix_norm)[0], 16)
    W = f(w_in)[0]
    o = 0
    seg = {}
    for nm, sz in [("z", 4096), ("xbc", 6144), ("dt", 64), ("q", 2048), ("k", 256), ("v", 256), ("gs", 2048), ("ga", 2048)]:
        seg[nm] = W[:, o:o + sz]; o += sz
    shared["w_z"] = _tile_w(seg["z"], 16); shared["w_xbc"] = _tile_w(seg["xbc"], 16)
    shared["w_dt"] = np.ascontiguousarray(seg["dt"].reshape(16, 128, 64).transpose(1, 0, 2))
    shared["w_q"] = _tile_w(seg["q"], 16)
    shared["w_kv"] = _tile_w(np.concatenate([seg["k"], seg["v"]], axis=1), 16)
    shared["w_gs"] = _tile_w(seg["gs"], 16); shared["w_ga"] = _tile_w(seg["ga"], 16)
    shared["conv_w"] = np.ascontiguousarray(f(conv_w)[0].reshape(4, 48, 128).transpose(2, 1, 0))
    shared["conv_b"] = _vec_pk(f(conv_b)[0], 48)
    bc = lambda v: np.ascontiguousarray(np.broadcast_to(f(v)[0][None, :], (128, f(v).shape[1])))
    shared["dtb_bc"] = bc(dt_bias); shared["alog_bc"] = bc(a_log); shared["dsk_bc"] = bc(d_skip); shared["sinks_bc"] = bc(sinks)
    shared["ssm_n"] = _vec_pk(f(ssm_norm)[0], 32)
    shared["qn"] = np.ascontiguousarray(np.tile(f(q_norm)[0], 2)[:, None]); shared["kn"] = np.ascontiguousarray(np.tile(f(k_norm)[0], 2)[:, None])
    shared["w_os"] = _tile_w(f(w_o_ssm)[0], 32); shared["w_oa"] = _tile_w(f(w_o_attn)[0], 16); shared["w_out"] = _tile_w(f(w_out)[0], 16)
    in_maps = []
    for c in range(8):
        b, hf = c // 2, c % 2
        m = dict(shared)
        m["xT_own"] = np.ascontiguousarray(x[b, hf * T:(hf + 1) * T, :].T)
        m["xT_pre"] = np.ascontiguousarray(x[b, 0:T, :].T) if hf == 1 else np.zeros((2048, T), np.float32)
        m["mflag"] = np.full((128, 1), float(hf), np.float32)
        in_maps.append(m)
    if "nc" not in _NC_CACHE:
        _NC_CACHE["nc"] = build_nc()
    res = run_bass_kernel_spmd(_NC_CACHE["nc"], in_maps, core_ids=list(range(8)))
    out = np.empty((4, 2048, 2048), np.float32)
    for c in range(8):
        b, hf = c // 2, c % 2
        out[b, hf * T:(hf + 1) * T, :] = res.results[c]["oT"].T
    return out
```

```python
import concourse.bass as bass
import concourse.mybir as mybir

ENGS = ("pe", "act", "dve", "pool", "sp")


class Res:
    __slots__ = ("name", "excl", "last_w", "readers")

    def __init__(self, name, excl=False):
        self.name = name
        self.excl = excl
        self.last_w = None
        self.readers = {}


class Op:
    __slots__ = ("eng", "idx", "fn", "deps", "dma_key", "dma_n", "signal", "name")


class Prog:
    def __init__(self, nc):
        self.nc = nc
        self.ops = []
        self.eng_ops = {e: [] for e in ENGS}
        self.dma_cnt = {}
        self.dma_last = {}

    def res(self, name, excl=False):
        return Res(name, excl)

    def _add(self, eng, fn, reads, writes, dma_key=None, name=None):
        op = Op()
        op.eng = eng
        op.fn = fn
        op.name = name
        op.dma_key = dma_key
        op.signal = False
        op.idx = len(self.eng_ops[eng])
        deps = set()
        is_dma = dma_key is not None
        for r in reads:
            if r.excl:
                continue
            if r.last_w is not None:
                deps.add(r.last_w)
        for w in list(writes) + [r for r in reads if r.excl]:
            if w.last_w is not None:
                deps.add(w.last_w)
            for e, rid in w.readers.items():
                deps.add(rid)
        gid = len(self.ops)
        final = set()
        raw_src = set()
        for r in reads:
            if r.last_w is not None:
                raw_src.add(r.last_w)
        for d in deps:
            dop = self.ops[d]
            if dop.dma_key is None and dop.eng == eng and not is_dma:
                if d in raw_src:
                    final.add(d)
            else:
                final.add(d)
        if is_dma:
            prev = self.dma_last.get(dma_key)
            if prev is not None:
                final.add(prev)
            n = self.dma_cnt.get(dma_key, 0) + 1
            self.dma_cnt[dma_key] = n
            op.dma_n = n
            self.dma_last[dma_key] = gid
        else:
            op.dma_n = 0
        op.deps = final
        self.ops.append(op)
        self.eng_ops[eng].append(gid)
        for r in reads:
            if r.excl:
                r.last_w = gid
                r.readers = {}
            else:
                r.readers[("dma", dma_key) if is_dma else eng] = gid
        for w in writes:
            w.last_w = gid
            w.readers = {}
        return gid

    def op(self, eng, fn, reads=(), writes=(), name=None):
        return self._add(eng, fn, list(reads), list(writes), None, name)

    def dma(self, queue, key, fn, reads=(), writes=(), name=None):
        return self._add(queue, fn, list(reads), list(writes), key, name)

    def emit(self, final_wait_ops=()):
        nc = self.nc
        ops = self.ops
        waits = {}
        for e in ENGS:
            known_c = {}
            known_d = {}
            for gid in self.eng_ops[e]:
                op = ops[gid]
                wl = []
                for d in sorted(op.deps):
                    dop = ops[d]
                    if dop.dma_key is not None:
                        if known_d.get(dop.dma_key, 0) >= dop.dma_n:
                            continue
                        known_d[dop.dma_key] = dop.dma_n
                        wl.append(d)
                    else:
                        if known_c.get(dop.eng, -1) >= dop.idx:
                            continue
                        known_c[dop.eng] = dop.idx
                        wl.append(d)
                        dop.signal = True
                waits[gid] = wl
        sig_rank = {}
        for e in ENGS:
            c = 0
            for gid in self.eng_ops[e]:
                op = ops[gid]
                if op.dma_key is None and op.signal:
                    c += 1
                    sig_rank[gid] = c
        self.n_waits = sum(len(v) for v in waits.values())
        self.n_sig = len(sig_rank)
        import contextlib
        with contextlib.ExitStack() as st:
            esem = {e: st.enter_context(nc.semaphore("s_" + e)) for e in ENGS}
            dsem = {k: st.enter_context(nc.semaphore("d_%s" % (k,))) for k in self.dma_cnt}
            block = st.enter_context(nc.Block())
            engobj = {"pe": "tensor", "act": "scalar", "dve": "vector", "pool": "gpsimd", "sp": "sync"}

            def run_engine(e):
                def body(eng):
                    for gid in self.eng_ops[e]:
                        op = ops[gid]
                        for d in waits[gid]:
                            dop = ops[d]
                            if dop.dma_key is not None:
                                eng.wait_ge(dsem[dop.dma_key], 16 * dop.dma_n)
                            else:
                                eng.wait_ge(esem[dop.eng], sig_rank[d])
                        ins = op.fn(eng)
                        if ins is None:
                            continue
                        if op.dma_key is not None:
                            ins.then_inc(dsem[op.dma_key], 16)
                        elif op.signal:
                            ins.then_inc(esem[e], 1)
                return body

            block.tensor(run_engine("pe"))
            block.scalar(run_engine("act"))
            block.vector(run_engine("dve"))
            block.gpsimd(run_engine("pool"))
            block.sync(run_engine("sp"))

import numpy as np
import contextlib
from concourse.bass_utils import run_bass_kernel_spmd

F32 = mybir.dt.float32
BF16 = mybir.dt.bfloat16
AF = mybir.ActivationFunctionType
ALU = mybir.AluOpType
T = 1024
NT = 8
EPS = 1e-6
NEG = -30000.0


def build_nc():
    nc = bass.Bass("TRN2", target_bir_lowering=False)
    di = {}

    def din(name, shape):
        di[name] = nc.dram_tensor(name, list(shape), F32, kind="ExternalInput").ap()
        return di[name]

    xT_own = din("xT_own", [2048, T]); xT_pre = din("xT_pre", [2048, T]); mflag_d = din("mflag", [128, 1])
    f_gu = [din("f1_gu", [44, 128, 2, 16, 128]), din("f2_gu", [44, 128, 2, 16, 128])]
    f_d = [din("f1_d", [11, 4, 128, 4, 4, 128]), din("f2_d", [11, 4, 128, 4, 4, 128])]
    n_f = [din("n_f1", [128, 16]), din("n_f2", [128, 16])]
    n_mix_d = din("n_mix", [128, 16])
    w_z = din("w_z", [32, 128, 16, 128]); w_xbc = din("w_xbc", [48, 128, 16, 128]); w_dt_d = din("w_dt", [128, 16, 64])
    w_q = din("w_q", [16, 128, 16, 128]); w_kv = din("w_kv", [4, 128, 16, 128])
    w_gs = din("w_gs", [16, 128, 16, 128]); w_ga = din("w_ga", [16, 128, 16, 128])
    conv_w_d = din("conv_w", [128, 48, 4]); conv_b_d = din("conv_b", [128, 48])
    dtb_d = din("dtb_bc", [128, 64]); alog_d = din("alog_bc", [128, 64]); dsk_d = din("dsk_bc", [128, 64])
    ssmn_d = din("ssm_n", [128, 32]); qn_d = din("qn", [128, 1]); kn_d = din("kn", [128, 1]); sinks_d = din("sinks_bc", [128, 32])
    w_os = din("w_os", [16, 128, 32, 128]); w_oa = din("w_oa", [16, 128, 16, 128]); w_out = din("w_out", [16, 128, 16, 128])
    c_ident = din("c_ident", [128, 128]); c_tri = din("c_tri", [128, 128]); c_sel127 = din("c_sel127", [128, 128])
    c_negm = din("c_negm", [128, 512]); c_selh = din("c_selh", [128, 64 * 128]); c_ones = din("c_ones", [128, 128])
    c_bd = din("c_bd", [128, 128]); c_ebo = din("c_ebo", [128, 4096]); c_ebp = din("c_ebp", [128, 4096])
    oT = nc.dram_tensor("oT", [2048, T], F32, kind="ExternalOutput").ap()

    def dscr(name, shape, dt=BF16):
        return nc.dram_tensor(name, list(shape), dt).ap()

    szT = dscr("szT", [32, 128, T]); bcT = dscr("bcT", [16, 128, T])
    xs_tm = [dscr("xs_tm0", [NT, 128, 4096]), dscr("xs_tm1", [NT, 128, 4096])]
    b_tm = [dscr("b_tm0", [NT, 128, 1024]), dscr("b_tm1", [NT, 128, 1024])]
    qT_d = dscr("qT_d", [16, 128, T]); kT_d = dscr("kT_d", [2, 128, T + 128]); v_tm = dscr("v_tm", [NT + 1, 128, 256])
    sgs = dscr("sgs", [16, 128, T]); sga = dscr("sga", [16, 128, T])
    ysT_d = dscr("ysT_d", [32, 128, T]); yaT_d = dscr("yaT_d", [16, 128, T])

    st = contextlib.ExitStack()
    with st:
        P = Prog(nc)

        def sb(name, shape, dt=F32):
            return st.enter_context(nc.sbuf_tensor(name, list(shape), dt))

        xT = sb("xT", [128, 16, T]); hT = sb("hT", [128, 16, T], BF16)
        R_xT = [[P.res("xT%d_%d" % (k, h)) for h in range(2)] for k in range(16)]
        R_hT = [P.res("hT%d" % h) for h in range(2)]
        NWB = 5
        wbuf = [sb("wbuf%d" % i, [128, 2048], BF16) for i in range(NWB)]
        R_wb = [P.res("wb%d" % i) for i in range(NWB)]
        wb_i = [0]
        SCR = sb("SCR", [128, 10880])
        R_SCR = P.res("SCRALL")
        ident_bf = sb("ident_bf", [128, 128], BF16); ident_f = sb("ident_f", [128, 128]); tri_f = sb("tri_f", [128, 128])
        sel127_f = sb("sel127_f", [128, 128]); negm_bf = sb("negm_bf", [128, 512], BF16); selh_bf = sb("selh_bf", [128, 64 * 128], BF16)
        ones_bf = sb("ones_bf", [128, 128], BF16); bd_bf = sb("bd_bf", [128, 128], BF16)
        nf_sb = [sb("nf1", [128, 16]), sb("nf2", [128, 16])]; nmix_sb = sb("nmix", [128, 16])
        wdt_sb = sb("wdt", [128, 16, 64], BF16); convw_sb = sb("convw", [128, 48, 4]); convb_sb = sb("convb", [128, 48])
        dtb_sb = sb("dtb", [128, 64]); a_sb = sb("a_sb", [128, 64]); dsk_sb = sb("dsk", [128, 64]); ssmn_sb = sb("ssmn", [128, 32])
        eps_sb = sb("eps_sb", [128, 1]); one_sb = sb("one_sb", [128, 1]); qn_sb = sb("qn_sb", [128, 1]); kn_sb = sb("kn_sb", [128, 1]); esink_sb = sb("esink", [128, 32]); mflag_sb = sb("mflag_sb", [128, 1])
        utail = sb("utail", [128, 48, 3]); S_run = sb("S_run", [128, 4096]); S_bf = sb("S_bf", [128, 4096], BF16)
        R_out = P.res("out"); R_const = P.res("const"); R_utail = P.res("utail"); R_S = P.res("S"); R_Sbf = P.res("Sbf")
        banks = [st.enter_context(nc.psum_tensor("bank%d" % i, [128, 512], F32)) for i in range(8)]
        R_bank = [P.res("bank%d" % i, excl=True) for i in range(8)]
        bk_i = [0]

        def nb():
            i = bk_i[0] % 8
            bk_i[0] += 1
            return banks[i], R_bank[i]

        pa_i = [0]; pb_i = [0]

        def nbA():
            i = pa_i[0] % 4
            pa_i[0] += 1
            return banks[i], R_bank[i]

        def nbB():
            i = 4 + pb_i[0] % 4
            pb_i[0] += 1
            return banks[i], R_bank[i]

        cst_n = [0]

        cst_res = []

        def cload(dst, src, cast):
            cst_n[0] += 1
            q = "pool" if cast else "sp"
            r = P.res("cst%d" % cst_n[0])
            cst_res.append(r)
            P.dma(q, "cst%d" % cst_n[0], lambda e: e.dma_start(out=dst, in_=src), writes=[r])

        cload(ident_bf[:], c_ident, True); cload(ident_f[:], c_ident, False); cload(tri_f[:], c_tri, False)
        cload(sel127_f[:], c_sel127, False); cload(negm_bf[:], c_negm, True); cload(selh_bf[:], c_selh, True)
        cload(ones_bf[:], c_ones, True); cload(bd_bf[:], c_bd, True)
        cload(nf_sb[0][:], n_f[0], False); cload(nf_sb[1][:], n_f[1], False); cload(nmix_sb[:], n_mix_d, False)
        cload(wdt_sb[:], w_dt_d, True); cload(convw_sb[:], conv_w_d, False); cload(convb_sb[:], conv_b_d, False)
        cload(dtb_sb[:], dtb_d, False); cload(a_sb[:], alog_d, False); cload(dsk_sb[:], dsk_d, False); cload(ssmn_sb[:], ssmn_d, False)
        cload(qn_sb[:], qn_d, False); cload(kn_sb[:], kn_d, False); cload(esink_sb[:], sinks_d, False); cload(mflag_sb[:], mflag_d, False)
        P.op("dve", lambda e: e.memset(eps_sb[:], EPS), reads=cst_res, writes=[R_const])
        P.op("act", lambda e: e.activation(a_sb[:], a_sb[:], AF.Exp), reads=[R_const], writes=[R_const])
        P.op("dve", lambda e: e.tensor_scalar(a_sb[:], a_sb[:], -1.0, None, ALU.mult), reads=[R_const], writes=[R_const])
        P.op("act", lambda e: e.activation(esink_sb[:], esink_sb[:], AF.Exp), reads=[R_const], writes=[R_const])
        P.op("dve", lambda e: e.memset(utail[:], 0.0), writes=[R_utail])
        P.op("dve", lambda e: e.memset(one_sb[:], 1.0), reads=[R_const], writes=[R_const])
        P.op("dve", lambda e: e.memset(S_run[:], 0.0), writes=[R_S])

        class Carver:
            def __init__(self):
                self.off = 0

            def f32(self, n, shape=None):
                v = SCR[:, self.off:self.off + n]
                self.off += n
                assert self.off <= 10880, self.off
                return v

            def bf(self, n):
                assert n % 2 == 0
                v = SCR[:, self.off:self.off + n // 2].bitcast(BF16)
                self.off += n // 2
                assert self.off <= 10880, self.off
                return v

        R_dr = {n: P.res("dr_" + n) for n in ["szT", "bcT", "xs_tm", "b_tm", "qT_d", "kT_d", "v_tm", "sgs", "sga", "ysT_d", "yaT_d"]}
        last_bar = [None]

        def sdma(key, fn, reads=(), writes=()):
            return P.dma("sp", key, fn, reads=list(reads) + list(R_dr.values()), writes=list(writes))

        def phase_barrier(wb=False):
            last_bar[0] = P.op("dve", lambda e: e.memset(SCR[:, 0:1], 0.0), writes=[R_SCR] + phase_res[0] + list(R_dr.values()) + (list(R_wb) if wb else []))
            phase_res[0] = []

        phase_res = [[]]

        def pres(name):
            r = P.res(name)
            r.last_w = last_bar[0]
            phase_res[0].append(r)
            return r

        def wload(src_ap, nelem, view):
            i = wb_i[0] % NWB
            wb_i[0] += 1
            buf = wbuf[i][:, 0:nelem]
            P.dma("pool", "wb%d" % i, lambda e: e.dma_start(out=view(buf), in_=src_ap), writes=[R_wb[i]])
            return view(buf), R_wb[i]

        def mm_group(bank, rb, n, pairs, reads):
            def fn(e):
                ins = None
                L = len(pairs)
                for i, (l, r) in enumerate(pairs):
                    ins = e.matmul(bank[:, 0:n], l, r, start=(i == 0), stop=(i == L - 1))
                return ins
            P.op("pe", fn, reads=reads, writes=[rb])

        def rmsnorm(gain_sb):
            c = Carver()
            sq = [c.bf(4 * 512) for _ in range(2)]
            R_sq = [pres("sq0"), pres("sq1")]
            rstd = c.f32(512); R_rstd = pres("rstd")
            for h in range(2):
                bank, rb = nb()
                for k4 in range(4):
                    i = k4 % 2
                    sqv = sq[i].rearrange("p (k t) -> p k t", k=4)
                    P.op("act", lambda e, sqv=sqv, k4=k4, h=h: e.activation(sqv, xT[:, 4 * k4:4 * k4 + 4, h * 512:(h + 1) * 512], AF.Square),
                         reads=[R_xT[k][h] for k in range(4 * k4, 4 * k4 + 4)], writes=[R_sq[i]])

                    def fn(e, sqv=sqv, k4=k4, bank=bank):
                        ins = None
                        for kk in range(4):
                            ins = e.matmul(bank[:, :], ones_bf[:], sqv[:, kk, :], start=(k4 == 0 and kk == 0), stop=(k4 == 3 and kk == 3))
                        return ins
                    P.op("pe", fn, reads=[R_sq[i], R_const], writes=[rb])
                P.op("act", lambda e, bank=bank: e.activation(rstd, bank[:, :], AF.Ln, bias=eps_sb[:, 0:1], scale=1.0 / 2048), reads=[rb, R_const], writes=[R_rstd])
                P.op("act", lambda e: e.activation(rstd, rstd, AF.Exp, scale=-0.5), reads=[R_rstd], writes=[R_rstd])
                for k in range(16):
                    P.op("dve", lambda e, k=k, h=h: e.scalar_tensor_tensor(hT[:, k, h * 512:(h + 1) * 512], xT[:, k, h * 512:(h + 1) * 512],
                                                                        gain_sb[:, k:k + 1], rstd, ALU.mult, ALU.mult),
                         reads=[R_xT[k][h], R_rstd, R_const], writes=[R_hT[h]])
            phase_barrier()

        def ffn(fi):
            rmsnorm(nf_sb[fi])
            c = Carver()
            actb = [c.bf(4 * T).rearrange("p (k t) -> p k t", k=4) for _ in range(2)]
            R_act = [pres("act0"), pres("act1")]
            sg = [c.f32(512) for _ in range(2)]; R_sg = [pres("sg0"), pres("sg1")]
            sgi = 0
            for q in range(11):
                ab = actb[q % 2]; rab = R_act[q % 2]
                for j in range(4):
                    jj = q * 4 + j
                    wg, rwg = wload(f_gu[fi][jj][:, 0, :, :], 16 * 128, lambda b: b.rearrange("p (k c) -> p k c", k=16))
                    wu, rwu = wload(f_gu[fi][jj][:, 1, :, :], 16 * 128, lambda b: b.rearrange("p (k c) -> p k c", k=16))
                    for h in range(2):
                        bg, rbg = nb(); bu, rbu = nb()
                        mm_group(bg, rbg, 512, [(wg[:, k, :], hT[:, k, h * 512:(h + 1) * 512]) for k in range(16)], [rwg, R_hT[h]])
                        mm_group(bu, rbu, 512, [(wu[:, k, :], hT[:, k, h * 512:(h + 1) * 512]) for k in range(16)], [rwu, R_hT[h]])
                        s = sg[sgi % 2]; rs = R_sg[sgi % 2]; sgi += 1
                        P.op("act", lambda e, s=s, bg=bg: e.activation(s, bg[:, :], AF.Silu), reads=[rbg], writes=[rs])
                        P.op("dve", lambda e, s=s, bu=bu, ab=ab, j=j, h=h: e.tensor_tensor(ab[:, j, h * 512:(h + 1) * 512], bu[:, :], s, ALU.mult),
                             reads=[rbu, rs], writes=[rab])
                for cg in range(4):
                    wv, rw = wload(f_d[fi][q, cg], 4 * 4 * 128, lambda b: b.rearrange("p (g k c) -> p g k c", g=4, k=4))
                    for gi in range(4):
                        cc = 4 * cg + gi
                        for h in range(2):
                            bk, rb = nb()
                            mm_group(bk, rb, 512, [(wv[:, gi, k, :], ab[:, k, h * 512:(h + 1) * 512]) for k in range(4)], [rw, rab])
                            P.op("dve", lambda e, bk=bk, cc=cc, h=h: e.scalar_tensor_tensor(xT[:, cc, h * 512:(h + 1) * 512], bk[:, :], 0.5,
                                                                                         xT[:, cc, h * 512:(h + 1) * 512], ALU.mult, ALU.add),
                                 reads=[rb, R_xT[cc][h]], writes=[R_xT[cc][h]])
            phase_barrier()

        def linear(w_ap, NJ, KCn, rhs_fn, rhs_res, evac, G=1, halves=(0, 1), defer=0):
            sched = {}
            u = [0]
            for j0 in range(0, NJ, G):
                g = min(G, NJ - j0)
                wv, rw = wload(w_ap[j0:j0 + g].rearrange("g p k c -> p g k c"), g * KCn * 128,
                               lambda b, g=g: b.rearrange("p (g k c) -> p g k c", g=g, k=KCn))
                for gi in range(g):
                    for h in halves:
                        bk, rb = nb()
                        mm_group(bk, rb, 512, [(wv[:, gi, k, :], rhs_fn(k, h)) for k in range(KCn)], [rw] + rhs_res(h))
                        r = evac(j0 + gi, h, bk, rb)
                        if r is not None:
                            if callable(r):
                                r = [(defer, r)]
                            for dl, fn in r:
                                sched.setdefault(u[0] + dl, []).append(fn)
                        for fn in sched.pop(u[0], []):
                            fn()
                        u[0] += 1
            for k in sorted(sched):
                for fn in sched[k]:
                    fn()

        def pipeline(items, stages, extras=(), rev=False):
            n = len(items); S_ = len(stages)
            nsteps = n + S_ - 1
            extras = list(extras)
            per = -(-len(extras) // nsteps) if extras else 0
            for step in range(nsteps):
                order = range(S_ - 1, -1, -1) if rev else range(S_)
                for si in order:
                    k = step - si
                    if 0 <= k < n:
                        stages[si](items[k])
                for _ in range(per):
                    if extras:
                        extras.pop(0)()
            for o in extras:
                o()

        def hT_rhs(k, h):
            return hT[:, k, h * 512:(h + 1) * 512]

        def hT_res(h):
            return [R_hT[h]]

        def inproj(own):
            pz = 1 if own else 0
            c = Carver()
            stg = [c.bf(T) for _ in range(3)]; R_stg = [pres("stg%d" % i) for i in range(3)]
            stg_i = [0]
            U = [c.f32(T + 4) for _ in range(2)]; R_U = [pres("U0"), pres("U1")]
            acc = [c.f32(T) for _ in range(2)]; R_acc = [pres("acc0"), pres("acc1")]
            tst = [c.bf(NT * 128).rearrange("p (t c) -> p t c", t=NT) for _ in range(2)]; R_tst = [pres("tst0"), pres("tst1")]
            tst_i = [0]
            qraw = [c.f32(512) for _ in range(2)]; R_qraw = [pres("qraw0"), pres("qraw1")]
            sqq = [c.bf(512) for _ in range(2)]; R_sqq = [pres("sqq0"), pres("sqq1")]
            rq = [c.f32(512) for _ in range(2)]; R_rq = [pres("rq0"), pres("rq1")]
            qi = [0]
            dcnt = [0]

            def dkey():
                dcnt[0] += 1
                return "ip%d" % (dcnt[0] % 6)

            def getstg():
                i = stg_i[0] % 3
                stg_i[0] += 1
                return stg[i], R_stg[i]

            def act_store(func, dst_d):
                def evac(j, h, bk, rb):
                    s, rs = getstg()
                    P.op("act", lambda e: e.activation(s[:, 0:512], bk[:, :], func), reads=[rb], writes=[rs])
                    sdma(dkey(), lambda e: e.dma_start(out=dst_d[j][:, h * 512:(h + 1) * 512], in_=s[:, 0:512]), reads=[rs])
                    return None
                return evac

            def transpose_store(src_bf, rsrc, dst_ap_fn):
                bk, rb = nb()
                bkb = bk[:, :].bitcast(BF16)

                def fn(e):
                    ins = None
                    for t in range(NT):
                        ins = e.transpose(bkb[:, t * 128:(t + 1) * 128], src_bf[:, t * 128:(t + 1) * 128], ident_bf[:])
                    return ins
                P.op("pe", fn, reads=[rsrc, R_const], writes=[rb])
                i = tst_i[0] % 2
                tst_i[0] += 1
                P.op("act", lambda e: e.activation(tst[i].rearrange("p t c -> p (t c)"), bkb, AF.Copy), reads=[rb], writes=[R_tst[i]])
                sdma(dkey(), lambda e: dst_ap_fn(e, tst[i]), reads=[R_tst[i]])

            def qk_evac(gain_sb, store):
                def evac(j, h, bk, rb):
                    i = qi[0] % 2
                    qi[0] += 1
                    P.op("act", lambda e: e.activation(qraw[i], bk[:, :], AF.Copy), reads=[rb], writes=[R_qraw[i]])
                    P.op("act", lambda e: e.activation(sqq[i], bk[:, :], AF.Square), reads=[rb], writes=[R_sqq[i]])

                    def later():
                        b2, rb2 = nb()
                        P.op("pe", lambda e: e.matmul(b2[:, :], bd_bf[:], sqq[i], start=True, stop=True), reads=[R_sqq[i], R_const], writes=[rb2])
                        P.op("act", lambda e: e.activation(rq[i], b2[:, :], AF.Ln, bias=eps_sb[:, 0:1], scale=1.0 / 64), reads=[rb2, R_const], writes=[R_rq[i]])
                        P.op("act", lambda e: e.activation(rq[i], rq[i], AF.Exp, scale=-0.5), reads=[R_rq[i]], writes=[R_rq[i]])
                        s, rs = getstg()
                        P.op("dve", lambda e: e.scalar_tensor_tensor(s[:, 0:512], qraw[i], gain_sb[:, 0:1], rq[i], ALU.mult, ALU.mult),
                             reads=[R_qraw[i], R_rq[i], R_const], writes=[rs])
                        store(j, h, s, rs)
                    return later
                return evac

            def xbc_evac(j, h, bk, rb):
                ub = U[j % 2]; rub = R_U[j % 2]; ab = acc[j % 2]; rab = R_acc[j % 2]
                P.op("act", lambda e: e.activation(ub[:, 3 + h * 512:3 + (h + 1) * 512], bk[:, :], AF.Copy), reads=[rb], writes=[rub])
                if h == 0:
                    return None
                if (not own) and j >= 40:
                    P.op("dve", lambda e: e.tensor_copy(utail[:, j, :], ub[:, T:T + 3]), reads=[rub], writes=[R_utail])
                    return None
                P.op("dve", lambda e: e.tensor_copy(ub[:, 0:3], utail[:, j, :]), reads=[R_utail], writes=[rub])
                P.op("dve", lambda e: e.tensor_scalar(ab, ub[:, 0:T], convw_sb[:, j, 0:1], None, ALU.mult), reads=[rub, R_const], writes=[rab])
                for kk in range(1, 4):
                    P.op("dve", lambda e, kk=kk: e.scalar_tensor_tensor(ab, ub[:, kk:kk + T], convw_sb[:, j, kk:kk + 1], ab, ALU.mult, ALU.add),
                         reads=[rub, rab, R_const], writes=[rab])
                if not own:
                    P.op("dve", lambda e: e.tensor_copy(utail[:, j, :], ub[:, T:T + 3]), reads=[rub], writes=[R_utail])

                hold = {}

                def later1():
                    hold["s"], hold["rs"] = getstg()
                    P.op("act", lambda e: e.activation(hold["s"], ab, AF.Silu, bias=convb_sb[:, j:j + 1]), reads=[rab, R_const], writes=[hold["rs"]])

                def later2():
                    s, rs = hold["s"], hold["rs"]
                    if j < 32:
                        transpose_store(s, rs, lambda e, stt: e.dma_start(out=xs_tm[pz][:, :, j * 128:(j + 1) * 128].rearrange("t p c -> p t c"), in_=stt))
                    elif j < 40:
                        transpose_store(s, rs, lambda e, stt: e.dma_start(out=b_tm[pz][:, :, (j - 32) * 128:(j - 31) * 128].rearrange("t p c -> p t c"), in_=stt))
                    if own and j >= 32:
                        sdma(dkey(), lambda e: e.dma_start(out=bcT[j - 32], in_=s), reads=[rs])
                return [(1, later1), (2, later2)]

            def k_store(j, h, s, rs):
                if own:
                    sdma(dkey(), lambda e: e.dma_start(out=kT_d[j][:, 128 + h * 512:128 + (h + 1) * 512], in_=s[:, 0:512]), reads=[rs])
                elif h == 1:
                    sdma(dkey(), lambda e: e.dma_start(out=kT_d[j][:, 0:128], in_=s[:, 384:512]), reads=[rs])

            def q_store(j, h, s, rs):
                sdma(dkey(), lambda e: e.dma_start(out=qT_d[j][:, h * 512:(h + 1) * 512], in_=s[:, 0:512]), reads=[rs])

            vfull = [None]

            def kv_evac(j, h, bk, rb):
                if j < 2:
                    return qk_evac(kn_sb, k_store)(j, h, bk, rb)
                s = acc[j % 2].bitcast(BF16)[:, 0:T]; rs = R_acc[j % 2]
                P.op("act", lambda e: e.activation(s[:, h * 512:(h + 1) * 512], bk[:, :], AF.Copy), reads=[rb], writes=[rs])
                if h == 0:
                    return None
                jv = j - 2

                def later():
                    if own:
                        transpose_store(s, rs, lambda e, stt: e.dma_start(out=v_tm[1:NT + 1, :, jv * 128:(jv + 1) * 128].rearrange("t p c -> p t c"), in_=stt))
                    else:
                        transpose_store(s, rs, lambda e, stt: e.dma_start(out=v_tm[0][:, jv * 128:(jv + 1) * 128], in_=stt[:, NT - 1, :]))
                return later

            if own:
                linear(w_z, 32, 16, hT_rhs, hT_res, act_store(AF.Silu, szT))
                linear(w_gs, 16, 16, hT_rhs, hT_res, act_store(AF.Sigmoid, sgs))
                linear(w_ga, 16, 16, hT_rhs, hT_res, act_store(AF.Sigmoid, sga))
                linear(w_q, 16, 16, hT_rhs, hT_res, qk_evac(qn_sb, q_store), defer=1)
            linear(w_kv, 4, 16, hT_rhs, hT_res, kv_evac, defer=1)
            linear(w_xbc, 48 if own else 40, 16, hT_rhs, hT_res, xbc_evac, defer=2)
            if not own:
                linear(w_xbc[40:48], 8, 16, hT_rhs, hT_res, lambda j, h, bk, rb: xbc_evac(j + 40, h, bk, rb), halves=(1,))
            phase_barrier(wb=True)

        def ssd(own):
            pz = 1 if own else 0
            c = Carver()
            xs_b = [c.bf(4096) for _ in range(2)]; R_xs = [pres("xs0"), pres("xs1")]
            xsc2 = [c.bf(512) for _ in range(2)]; R_xsc2 = [pres("xsc0"), pres("xsc1")]
            BTs = [wbuf[i][:, 0:1024].rearrange("p (g n) -> p g n", g=8) for i in range(2)]
            CTs = [wbuf[i][:, 1024:2048].rearrange("p (g n) -> p g n", g=8) for i in range(2)]
            R_BC = [pres("BC0"), pres("BC1")]
            NMS = ["dtx", "ax", "ee", "dt", "adt", "acum", "lndt", "bias", "expA", "dd", "dtdec", "cdec"]
            sms = [{nm: wbuf[2 + i][:, k * 128:(k + 1) * 128].bitcast(F32) for k, nm in enumerate(NMS)} for i in range(2)]
            R_sms = [pres("sm0"), pres("sm1")]
            AT_his = [wbuf[2 + i][:, 1536:1664] for i in range(2)]
            adt2s = [wbuf[2 + i][:, 1664:1920].bitcast(F32) for i in range(2)]
            for i in range(2):
                sms[i]["adt"] = adt2s[i][:, 0:64]
            R_ATs = [pres("AT0"), pres("AT1")]
            b_t = wbuf[4][:, 0:1024]; R_bt = pres("b_t")
            sz_t = [wbuf[4][:, 1024:1536], wbuf[4][:, 1536:2048]]; R_sz = [pres("sz_t0"), pres("sz_t1")]
            if own:
                xsd = [c.bf(512) for _ in range(2)]; R_xsd = [pres("xsd0"), pres("xsd1")]
                cbT = [c.bf(128) for _ in range(2)]; R_cb = [pres("cbT0"), pres("cbT1")]
                E = [[c.bf(512) for _ in range(2)] for _ in range(2)]; R_E = [[pres("E%d%d" % (a_, b_)) for b_ in range(2)] for a_ in range(2)]
                WT = E; R_WT = R_E
                t1 = [c.f32(512) for _ in range(2)]; R_t1 = [pres("t10"), pres("t11")]
                ytm = [c.bf(512) for _ in range(2)]; R_ytm = [pres("ytm0"), pres("ytm1")]
                yg = [c.f32(512) for _ in range(2)]; R_yg = [pres("yg0"), pres("yg1")]
                sqy = [c.bf(512) for _ in range(2)]; R_sqy = [pres("sqy0"), pres("sqy1")]
                rs_y = c.f32(128); R_rsy = pres("rs_y")
                ysn = [c.bf(512) for _ in range(2)]; R_ysn = [pres("ysn0"), pres("ysn1")]

            def prep_ops(t):
                q_ = t % 2
                sm = sms[q_]; R_sm = R_sms[q_]; xs_t = xs_b[q_]
                ts = slice(t * 128, (t + 1) * 128)
                ops = []
                ops.append(lambda: sdma("ss0%d" % q_, lambda e: e.dma_start(out=xs_t, in_=xs_tm[pz][t]), writes=[R_xs[q_]]))
                if own:
                    ops.append(lambda: sdma("ss2%d" % q_, lambda e: e.dma_start(out=BTs[q_], in_=bcT[0:8, :, ts].rearrange("g p n -> p g n")), writes=[R_BC[q_]]))
                    ops.append(lambda: sdma("ss3%d" % q_, lambda e: e.dma_start(out=CTs[q_], in_=bcT[8:16, :, ts].rearrange("g p n -> p g n")), writes=[R_BC[q_]]))
                st_ = {}

                def o1():
                    st_["bk"], st_["rb"] = nbB()
                    mm_group(st_["bk"], st_["rb"], 64, [(hT[:, k, ts], wdt_sb[:, k, :]) for k in range(16)], [R_hT[t // 4], R_const])
                    P.op("dve", lambda e: e.tensor_tensor(sm["dtx"], st_["bk"][:, 0:64], dtb_sb[:], ALU.add), reads=[st_["rb"], R_const], writes=[R_sm])
                ops.append(o1)
                ops.append(lambda: P.op("act", lambda e: e.activation(sm["ax"], sm["dtx"], AF.Abs), reads=[R_sm], writes=[R_sm]))
                ops.append(lambda: P.op("act", lambda e: e.activation(sm["ee"], sm["ax"], AF.Exp, scale=-1.0), reads=[R_sm], writes=[R_sm]))
                ops.append(lambda: P.op("act", lambda e: e.activation(sm["ee"], sm["ee"], AF.Ln, bias=one_sb[:, 0:1]), reads=[R_sm, R_const], writes=[R_sm]))
                ops.append(lambda: P.op("dve", lambda e: e.scalar_tensor_tensor(sm["dt"], sm["dtx"], 0.0, sm["ee"], ALU.max, ALU.add), reads=[R_sm], writes=[R_sm]))
                ops.append(lambda: P.op("dve", lambda e: e.tensor_tensor(sm["adt"], sm["dt"], a_sb[:], ALU.mult), reads=[R_sm, R_const], writes=[R_sm]))
                if own:
                    ops.append(lambda: P.op("dve", lambda e: e.tensor_tensor(adt2s[q_][:, 64:128], sm["dt"], a_sb[:], ALU.mult), reads=[R_sm, R_const], writes=[R_sm]))

                def o2():
                    bk2, rb2 = nbB()
                    P.op("pe", lambda e: e.matmul(bk2[:, 0:64], tri_f[:], sm["adt"], start=True, stop=True), reads=[R_sm, R_const], writes=[rb2])
                    P.op("act", lambda e: e.activation(sm["acum"], bk2[:, 0:64], AF.Copy), reads=[rb2], writes=[R_sm])
                ops.append(o2)

                def o3():
                    bk4, rb4 = nbB()
                    P.op("pe", lambda e: e.matmul(bk4[:, 0:64], sel127_f[:], sm["acum"], start=True, stop=True), reads=[R_sm, R_const], writes=[rb4])
                    P.op("dve", lambda e: e.tensor_tensor(sm["dd"], bk4[:, 0:64], sm["acum"], ALU.subtract), reads=[rb4, R_sm], writes=[R_sm])
                    P.op("act", lambda e: e.activation(sm["cdec"], bk4[:, 0:64], AF.Exp), reads=[rb4], writes=[R_sm])
                ops.append(o3)
                ops.append(lambda: P.op("act", lambda e: e.activation(sm["dd"], sm["dd"], AF.Exp), reads=[R_sm], writes=[R_sm]))
                ops.append(lambda: P.op("dve", lambda e: e.tensor_tensor(sm["dtdec"], sm["dd"], sm["dt"], ALU.mult), reads=[R_sm], writes=[R_sm]))
                if own:
                    def o4():
                        bk3, rb3 = nbB()
                        P.op("pe", lambda e: e.matmul(bk3[:, 0:128], adt2s[q_], tri_f[:], start=True, stop=True), reads=[R_sm, R_const], writes=[rb3])
                        P.op("dve", lambda e: e.tensor_copy(AT_his[q_], bk3[:, 0:128]), reads=[rb3], writes=[R_ATs[q_]])
                        P.op("dve", lambda e: e.tensor_tensor(AT_his[q_][64:128, :], bk3[64:128, 0:128], AT_his[q_][64:128, :], ALU.subtract),
                             reads=[rb3, R_ATs[q_]], writes=[R_ATs[q_]])
                    ops.append(o4)
                    ops.append(lambda: P.op("act", lambda e: e.activation(sm["lndt"], sm["dt"], AF.Ln), reads=[R_sm], writes=[R_sm]))
                    ops.append(lambda: P.op("dve", lambda e: e.tensor_tensor(sm["bias"], sm["lndt"], sm["acum"], ALU.subtract), reads=[R_sm], writes=[R_sm]))
                    ops.append(lambda: P.op("act", lambda e: e.activation(sm["expA"], sm["acum"], AF.Exp), reads=[R_sm], writes=[R_sm]))
                return ops

            def state_ops(t):
                q_ = t % 2
                sm = sms[q_]; R_sm = R_sms[q_]; xs_t = xs_b[q_]
                ops = [lambda: sdma("ss1", lambda e: e.dma_start(out=b_t, in_=b_tm[pz][t]), writes=[R_bt])]
                ops.append(lambda: P.op("dve", lambda e: e.tensor_tensor(S_run[:].rearrange("p (h d) -> p h d", h=64), S_run[:].rearrange("p (h d) -> p h d", h=64),
                                                                      sm["cdec"].unsqueeze(2).to_broadcast([128, 64, 64]), ALU.mult),
                                        reads=[R_S, R_sm], writes=[R_S]))
                pend = {}

                def og(g):
                    xb = xsc2[g % 2]; rxb = R_xsc2[g % 2]
                    P.op("dve", lambda e: e.tensor_tensor(xb.rearrange("p (h d) -> p h d", h=8), xs_t[:, g * 512:(g + 1) * 512].rearrange("p (h d) -> p h d", h=8),
                                                          sm["dtdec"][:, g * 8:(g + 1) * 8].unsqueeze(2).to_broadcast([128, 8, 64]), ALU.mult),
                         reads=[R_xs[q_], R_sm], writes=[rxb])
                    bS, rbS = nbB()
                    P.op("pe", lambda e: e.matmul(bS[:, :], b_t[:, g * 128:(g + 1) * 128], xb, start=True, stop=True), reads=[R_bt, rxb], writes=[rbS])
                    P.op("dve", lambda e: e.tensor_tensor(S_run[:, g * 512:(g + 1) * 512], S_run[:, g * 512:(g + 1) * 512], bS[:, :], ALU.add),
                         reads=[R_S, rbS], writes=[R_S])
                for g in range(8):
                    ops.append(lambda g=g: og(g))
                return ops

            def y_stages(t):
                q_ = t % 2
                sm = sms[q_]; R_sm = R_sms[q_]; xs_t = xs_b[q_]; BT_t = BTs[q_]; CT_t = CTs[q_]
                AT_hi = AT_his[q_]; R_AT = R_ATs[q_]
                ts = slice(t * 128, (t + 1) * 128)
                ctx = [dict() for _ in range(8)]

                def s0(g):
                    d = ctx[g]
                    d["R"] = []
                    for hh in range(2):
                        bR, rbR = nbA()
                        d["R"].append((bR, rbR))

                        def fnR(e, bR=bR, hh=hh):
                            e.matmul(bR[:, :], ident_bf[:], negm_bf[:], start=True, stop=False)
                            ins = None
                            for i in range(4):
                                hd = g * 8 + hh * 4 + i
                                ins = e.matmul(bR[:, i * 128:(i + 1) * 128], selh_bf[:, hd * 128:(hd + 1) * 128], AT_hi, start=False, stop=(i == 3))
                            return ins
                        P.op("pe", fnR, reads=[R_AT, R_const], writes=[rbR])
                    bkc, rbc = nbA(); d["cb"] = (bkc, rbc)
                    P.op("pe", lambda e: e.matmul(bkc[:, 0:128], BT_t[:, g, :], CT_t[:, g, :], start=True, stop=True), reads=[R_BC[q_]], writes=[rbc])
                    bO, rbO = nbA(); d["O"] = (bO, rbO)
                    P.op("pe", lambda e: e.matmul(bO[:, :], CT_t[:, g, :], S_bf[:, g * 512:(g + 1) * 512], start=True, stop=True),
                         reads=[R_BC[q_], R_Sbf], writes=[rbO])

                def s1(g):
                    d = ctx[g]; st_ = g % 2
                    bkc, rbc = d["cb"]
                    P.op("act", lambda e: e.activation(cbT[st_], bkc[:, 0:128], AF.Copy), reads=[rbc], writes=[R_cb[st_]])
                    for hh in range(2):
                        bR, rbR = d["R"][hh]
                        Eb = E[st_][hh]
                        for i in range(4):
                            hd = g * 8 + hh * 4 + i
                            P.op("act", lambda e, bR=bR, i=i, hd=hd, Eb=Eb: e.activation(Eb[:, i * 128:(i + 1) * 128], bR[:, i * 128:(i + 1) * 128], AF.Exp,
                                                                                       bias=sm["bias"][:, hd:hd + 1]),
                                 reads=[rbR, R_sm], writes=[R_E[st_][hh]])
                    bO, rbO = d["O"]
                    P.op("dve", lambda e: e.tensor_tensor(xsd[st_].rearrange("p (h d) -> p h d", h=8), xs_t[:, g * 512:(g + 1) * 512].rearrange("p (h d) -> p h d", h=8),
                                                          dsk_sb[:, g * 8:(g + 1) * 8].unsqueeze(2).to_broadcast([128, 8, 64]), ALU.mult), reads=[R_xs[q_], R_const], writes=[R_xsd[st_]])
                    P.op("dve", lambda e: e.tensor_tensor(t1[st_].rearrange("p (h d) -> p h d", h=8), bO[:, :].rearrange("p (h d) -> p h d", h=8),
                                                          sm["expA"][:, g * 8:(g + 1) * 8].unsqueeze(2).to_broadcast([128, 8, 64]), ALU.mult),
                         reads=[rbO, R_sm], writes=[R_t1[st_]])

                def s2(g):
                    st_ = g % 2
                    for hh in range(2):
                        Eb = E[st_][hh]; Wb = WT[st_][hh]
                        P.op("dve", lambda e, Eb=Eb, Wb=Wb: e.tensor_tensor(Wb.rearrange("p (i l) -> p i l", i=4), Eb.rearrange("p (i l) -> p i l", i=4),
                                                                              cbT[st_].unsqueeze(1).to_broadcast([128, 4, 128]), ALU.mult),
                             reads=[R_E[st_][hh], R_cb[st_]], writes=[R_WT[st_][hh]])

                def s3(g):
                    st_ = g % 2
                    bY, rbY = nbB()

                    def fnY(e):
                        e.matmul(bY[:, :], ident_bf[:], xsd[st_], start=True, stop=False)
                        ins = None
                        for hl in range(8):
                            hd = g * 8 + hl
                            Wb = WT[st_][hl // 4]
                            ins = e.matmul(bY[:, hl * 64:(hl + 1) * 64], Wb[:, (hl % 4) * 128:(hl % 4 + 1) * 128], xs_t[:, hd * 64:(hd + 1) * 64],
                                           start=False, stop=(hl == 7))
                        return ins
                    P.op("pe", fnY, reads=[R_xsd[st_], R_WT[st_][0], R_WT[st_][1], R_xs[q_], R_const], writes=[rbY])
                    P.op("dve", lambda e: e.tensor_tensor(ytm[st_], bY[:, :], t1[st_], ALU.add), reads=[rbY, R_t1[st_]], writes=[R_ytm[st_]])
                    sdma("ss4%d" % st_, lambda e: e.dma_start(out=sz_t[st_].rearrange("p (i l) -> p i l", i=4),
                                                            in_=szT[g * 4:(g + 1) * 4, :, ts].rearrange("i p l -> p i l")), writes=[R_sz[st_]])

                def s4(g):
                    st_ = g % 2
                    bT, rbT = nbB()

                    bTb = bT[:, :].bitcast(BF16)

                    def fnT(e):
                        ins = None
                        for i in range(4):
                            ins = e.transpose(bTb[:, i * 128:(i + 1) * 128], ytm[st_][:, i * 128:(i + 1) * 128], ident_bf[:])
                        return ins
                    P.op("pe", fnT, reads=[R_ytm[st_], R_const], writes=[rbT])
                    P.op("dve", lambda e: e.tensor_tensor(yg[st_], bTb[:, 0:512], sz_t[st_], ALU.mult), reads=[rbT, R_sz[st_]], writes=[R_yg[st_]])
                    P.op("act", lambda e: e.activation(sqy[st_], yg[st_], AF.Square), reads=[R_yg[st_]], writes=[R_sqy[st_]])

                def s5(g):
                    st_ = g % 2
                    bN, rbN = nbB()
                    mm_group(bN, rbN, 128, [(ones_bf[:], sqy[st_][:, i * 128:(i + 1) * 128]) for i in range(4)], [R_sqy[st_], R_const])
                    P.op("act", lambda e: e.activation(rs_y, bN[:, 0:128], AF.Ln, bias=eps_sb[:, 0:1], scale=1.0 / 512), reads=[rbN, R_const], writes=[R_rsy])
                    P.op("act", lambda e: e.activation(rs_y, rs_y, AF.Exp, scale=-0.5), reads=[R_rsy], writes=[R_rsy])
                    yb = ysn[st_]; ryb = R_ysn[st_]
                    for i in range(4):
                        jj = g * 4 + i
                        P.op("dve", lambda e, i=i, jj=jj: e.scalar_tensor_tensor(yb[:, i * 128:(i + 1) * 128], yg[st_][:, i * 128:(i + 1) * 128],
                                                                               ssmn_sb[:, jj:jj + 1], rs_y, ALU.mult, ALU.mult),
                             reads=[R_yg[st_], R_rsy, R_const], writes=[ryb])
                    sdma("ss5%d" % st_, lambda e: e.dma_start(out=ysT_d[g * 4:(g + 1) * 4, :, ts].rearrange("i p l -> p i l"),
                                                            in_=yb.rearrange("p (i l) -> p i l", i=4)), reads=[ryb])
                return [s0, s1, s2, s3, s4, s5]

            for o in prep_ops(0):
                o()
            for t in range(NT):
                so_ = state_ops(t); po_ = (prep_ops(t + 1) if t + 1 < NT else [])
                extras = []
                while so_ or po_:
                    if so_:
                        extras.append(so_.pop(0))
                    if po_:
                        extras.append(po_.pop(0))
                    if po_:
                        extras.append(po_.pop(0))
                if own:
                    pipeline(list(range(8)), y_stages(t), extras=extras, rev=True)
                else:
                    for o in extras:
                        o()
                if own and t < NT - 1:
                    P.op("act", lambda e: e.activation(S_bf[:], S_run[:], AF.Copy), reads=[R_S], writes=[R_Sbf])
            if not own:
                P.op("dve", lambda e: e.tensor_scalar(S_run[:], S_run[:], mflag_sb[:, 0:1], None, ALU.mult), reads=[R_S, R_const], writes=[R_S])
                P.op("act", lambda e: e.activation(S_bf[:], S_run[:], AF.Copy), reads=[R_S], writes=[R_Sbf])
            phase_barrier(wb=True)

        def attention():
            c = Carver()
            ebo = c.bf(4096); ebp = c.bf(4096); R_eb = pres("eb")
            qT_b = [c.bf(16 * 128).rearrange("p (j q) -> p j q", j=16), wbuf[0][:, :].rearrange("p (j q) -> p j q", j=16)]; R_qb = [pres("qT_t0"), pres("qT_t1")]
            kT_b = [c.bf(4 * 256).rearrange("p (k s) -> p k s", k=4), wbuf[1][:, 0:1024].rearrange("p (k s) -> p k s", k=4)]; R_kb = [pres("kT_t0"), pres("kT_t1")]
            v1 = [c.f32(2 * 4 * 65 // 2 + 2) for _ in range(2)]
            v1 = [v[:, 0:260].bitcast(BF16).rearrange("p (k g d) -> p k g d", k=2, g=4) for v in v1]; R_v = [pres("v0"), pres("v1")]
            Eb = [[c.bf(512) for _ in range(2)] for _ in range(2)]; R_Eb = [[pres("aE%d%d" % (a_, b_)) for b_ in range(2)] for a_ in range(2)]
            PT = [[c.bf(512) for _ in range(2)] for _ in range(2)]; R_PT = [[pres("PT%d%d" % (a, b)) for b in range(2)] for a in range(2)]
            den = c.f32(4); R_den = pres("den")
            ya = c.bf(2048); R_ya = pres("ya")
            yst = c.bf(2048); R_yst = pres("yst")
            P.dma("pool", "at0", lambda e: e.dma_start(out=ebo, in_=c_ebo), writes=[R_eb])
            P.dma("pool", "at1", lambda e: e.dma_start(out=ebp, in_=c_ebp), writes=[R_eb])
            for i in range(2):
                P.op("dve", lambda e, i=i: e.memset(v1[i], 1.0), writes=[R_v[i]])
            def at_loads(t):
                vb = v1[t % 2]; rv = R_v[t % 2]; qT_t = qT_b[t % 2]; kT_t = kT_b[t % 2]
                sdma("at2%d" % (t % 2), lambda e: e.dma_start(out=qT_t, in_=qT_d[:, :, t * 128:(t + 1) * 128].rearrange("j p q -> p j q")), writes=[R_qb[t % 2]])
                for par in range(2):
                    sdma("at3%d%d" % (par, t % 2), lambda e, par=par: e.dma_start(
                        out=kT_t[par * 64:(par + 1) * 64, :, :],
                        in_=kT_d[:, :, t * 128:t * 128 + 256].rearrange("j (u d) s -> d (j u) s", u=2)), writes=[R_kb[t % 2]])
                for k2 in range(2):
                    sdma("at4%d%d" % (t % 2, k2), lambda e, k2=k2: e.dma_start(out=vb[:, k2, :, 0:64],
                                                                          in_=v_tm[t + k2].rearrange("p (g d) -> p g d", g=4)), writes=[rv])
            at_loads(0)
            for t in range(NT):
                vb = v1[t % 2]; rv = R_v[t % 2]; qT_t = qT_b[t % 2]; kT_t = kT_b[t % 2]; R_q = R_qb[t % 2]; R_k = R_kb[t % 2]
                if t + 1 < NT:
                    at_loads(t + 1)
                actx = {}

                def a0(u, t=t, qT_t=qT_t, kT_t=kT_t, R_q=R_q, R_k=R_k):
                    kv, par = u
                    actx[u] = []
                    for kt in range(2):
                        bS, rbS = nbA()
                        actx[u].append((bS, rbS))
                        P.op("pe", lambda e, bS=bS, kt=kt: e.matmul(
                            bS[:, :], kT_t[par * 64:(par + 1) * 64, kv, kt * 128:(kt + 1) * 128],
                            qT_t[par * 64:(par + 1) * 64, kv * 4:(kv + 1) * 4, :], start=True, stop=True), reads=[R_k, R_q], writes=[rbS])

                def a1(u, t=t):
                    kv, par = u
                    for kt in range(2):
                        bS, rbS = actx[u][kt]
                        Ei = Eb[par][kt]
                        P.op("act", lambda e, bS=bS, Ei=Ei: e.activation(Ei, bS[:, :], AF.Exp, scale=0.125), reads=[rbS], writes=[R_Eb[par][kt]])

                def a2(u, t=t):
                    kv, par = u
                    for kt in range(2):
                        Ei = Eb[par][kt]
                        ebsrc = ebo if kt == 1 else ebp
                        ebv = ebsrc.rearrange("p (h q) -> p h q", h=32)[:, kv * 8 + par:kv * 8 + 8:2, :]
                        P.op("dve", lambda e, Ei=Ei, ebv=ebv, kt=kt: e.tensor_tensor(PT[par][kt].rearrange("p (i q) -> p i q", i=4),
                                                                                  Ei.rearrange("p (i q) -> p i q", i=4), ebv, ALU.mult),
                             reads=[R_Eb[par][kt], R_eb], writes=[R_PT[par][kt]])
                        if t == 0 and kt == 0:
                            P.op("dve", lambda e, kt=kt: e.tensor_scalar(PT[par][kt], PT[par][kt], mflag_sb[:, 0:1], None, ALU.mult),
                                 reads=[R_PT[par][kt], R_const], writes=[R_PT[par][kt]])

                def a3(u, t=t, vb=vb, rv=rv):
                    kv, par = u
                    bO, rbO = nbB()

                    def fnO(e, bO=bO):
                        ins = None
                        for i in range(4):
                            for kt in range(2):
                                ins = e.matmul(bO[:, i * 128:i * 128 + 65], PT[par][kt][:, i * 128:(i + 1) * 128], vb[:, kt, kv, :],
                                               start=(kt == 0), stop=(kt == 1))
                        return ins
                    P.op("pe", fnO, reads=[R_PT[par][0], R_PT[par][1], rv], writes=[rbO])
                    bOv = bO[:, :].rearrange("p (i c) -> p i c", i=4)
                    esv = esink_sb[:, kv * 8 + par:kv * 8 + 8:2]
                    P.op("dve", lambda e: e.tensor_tensor(den.unsqueeze(2), bOv[:, :, 64:65], esv.unsqueeze(2), ALU.add),
                         reads=[rbO, R_const], writes=[R_den])
                    P.op("dve", lambda e: e.reciprocal(den, den), reads=[R_den], writes=[R_den])
                    yav = ya.rearrange("p (h d) -> p h d", h=32)[:, kv * 8 + par:kv * 8 + 8:2, :]
                    P.op("dve", lambda e: e.tensor_tensor(yav, bOv[:, :, 0:64], den.unsqueeze(2).to_broadcast([128, 4, 64]), ALU.mult),
                         reads=[rbO, R_den], writes=[R_ya])
                pipeline([(kv, par) for kv in range(4) for par in range(2)], [a0, a1, a2, a3], rev=True)
                for half in range(2):
                    bT, rbT = nbB()
                    bTb = bT[:, :].bitcast(BF16)

                    def fnT(e, bTb=bTb, half=half):
                        ins = None
                        for i in range(8):
                            cch = half * 8 + i
                            ins = e.transpose(bTb[:, i * 128:(i + 1) * 128], ya[:, cch * 128:(cch + 1) * 128], ident_bf[:])
                        return ins
                    P.op("pe", fnT, reads=[R_ya, R_const], writes=[rbT])
                    P.op("act", lambda e, bTb=bTb, half=half: e.activation(yst[:, half * 1024:(half + 1) * 1024], bTb, AF.Copy), reads=[rbT], writes=[R_yst])
                sdma("at5", lambda e, t=t: e.dma_start(out=yaT_d[:, :, t * 128:(t + 1) * 128].rearrange("j p q -> p j q"),
                                                              in_=yst.rearrange("p (j q) -> p j q", j=16)), reads=[R_yst])
            phase_barrier(wb=True)

        def outproj():
            for h in range(2):
                hs = slice(h * 512, (h + 1) * 512)
                c = Carver()
                ysT_h = hT[:, :, :].rearrange("p k t -> p (k t)").rearrange("p (k t) -> p k t", k=32)
                yaT_h = c.bf(16 * 512).rearrange("p (k t) -> p k t", k=16); R_ya = pres("yaT_h")
                mT_h = c.bf(16 * 512).rearrange("p (k t) -> p k t", k=16); R_m = pres("mT_h")
                gsb = [c.bf(512) for _ in range(2)]; gab = [c.bf(512) for _ in range(2)]
                R_g = [pres("g0"), pres("g1")]
                m1 = c.f32(512); R_m1 = pres("m1")
                t2 = c.f32(512); R_t2 = pres("t2")
                R_ys = R_hT[0]
                sdma("op0", lambda e, hs=hs: e.dma_start(out=ysT_h, in_=ysT_d[:, :, hs].rearrange("k p t -> p k t")), writes=[R_hT[0], R_hT[1]])
                sdma("op1", lambda e, hs=hs: e.dma_start(out=yaT_h, in_=yaT_d[:, :, hs].rearrange("k p t -> p k t")), writes=[R_ya])
                for j in range(16):
                    wv1a, rw1a = wload(w_os[j][:, 0:16, :], 16 * 128, lambda b: b.rearrange("p (k c) -> p k c", k=16))
                    wv1b, rw1b = wload(w_os[j][:, 16:32, :], 16 * 128, lambda b: b.rearrange("p (k c) -> p k c", k=16))
                    wv2, rw2 = wload(w_oa[j], 16 * 128, lambda b: b.rearrange("p (k c) -> p k c", k=16))
                    gi = j % 2
                    sdma("op2%d" % gi, lambda e, j=j, hs=hs, gi=gi: e.dma_start(out=gsb[gi], in_=sgs[j][:, hs]), writes=[R_g[gi]])
                    sdma("op3%d" % gi, lambda e, j=j, hs=hs, gi=gi: e.dma_start(out=gab[gi], in_=sga[j][:, hs]), writes=[R_g[gi]])
                    bA, rbA = nb(); bB, rbB = nb()
                    mm_group(bA, rbA, 512, [(wv1a[:, k, :], ysT_h[:, k, :]) for k in range(16)] + [(wv1b[:, k, :], ysT_h[:, 16 + k, :]) for k in range(16)], [rw1a, rw1b, R_hT[0], R_hT[1]])
                    mm_group(bB, rbB, 512, [(wv2[:, k, :], yaT_h[:, k, :]) for k in range(16)], [rw2, R_ya])
                    P.op("dve", lambda e, bA=bA, gi=gi: e.tensor_tensor(m1, bA[:, :], gsb[gi], ALU.mult), reads=[rbA, R_g[gi]], writes=[R_m1])
                    P.op("dve", lambda e, bB=bB, gi=gi: e.tensor_tensor(t2, bB[:, :], gab[gi], ALU.mult), reads=[rbB, R_g[gi]], writes=[R_t2])
                    P.op("dve", lambda e, j=j: e.tensor_tensor(mT_h[:, j, :], m1, t2, ALU.add), reads=[R_m1, R_t2], writes=[R_m])

                def evac(j, hh, bk, rb, h=h):
                    P.op("dve", lambda e: e.tensor_tensor(xT[:, j, h * 512:(h + 1) * 512], bk[:, :], xT[:, j, h * 512:(h + 1) * 512], ALU.add),
                         reads=[rb, R_xT[j][h]], writes=[R_xT[j][h]])
                linear(w_out, 16, 16, lambda k, hh: mT_h[:, k, :], lambda hh: [R_m], evac, halves=(0,))
                phase_barrier()

        def load_x(src):
            for k in range(16):
                for h in range(2):
                    sdma("lx%d" % ((2 * k + h) % 4), lambda e, k=k, h=h: e.dma_start(out=xT[:, k, h * 512:(h + 1) * 512],
                                                                                           in_=src[k * 128:(k + 1) * 128, h * 512:(h + 1) * 512]),
                          writes=[R_xT[k][h]])

        load_x(xT_pre)
        ffn(0)
        rmsnorm(nmix_sb)
        inproj(False)
        ssd(False)
        load_x(xT_own)
        ffn(0)
        rmsnorm(nmix_sb)
        inproj(True)
        ssd(True)
        attention()
        outproj()
        ffn(1)
        outs = []
        for k in range(16):
            ro = P.res("out%d" % k)
            outs.append(ro)
            sdma("ox%d" % (k % 8), lambda e, k=k: e.dma_start(out=oT[k * 128:(k + 1) * 128, :], in_=xT[:, k, :]),
                  reads=[R_xT[k][0], R_xT[k][1]], writes=[ro])
        P.op("sp", lambda e: None, reads=outs)
        P.emit()
    return nc


def _tile_w(W, KCn):
    K, N = W.shape
    return np.ascontiguousarray(W.reshape(KCn, 128, N // 128, 128).transpose(2, 1, 0, 3))


def _vec_pk(v, n):
    return np.ascontiguousarray(v.reshape(n, 128).T)


def _consts():
    c = {}
    c["c_ident"] = np.eye(128, dtype=np.float32)
    s = np.arange(128)
    c["c_tri"] = (s[:, None] <= s[None, :]).astype(np.float32)
    sel = np.zeros((128, 128), np.float32); sel[127, :] = 1.0
    c["c_sel127"] = sel
    negm = np.where(s[None, :] < s[:, None], NEG, 0.0).astype(np.float32)
    c["c_negm"] = np.ascontiguousarray(np.tile(negm, (1, 4)))
    selh = np.zeros((128, 64, 128), np.float32)
    for h in range(64):
        selh[h, h, :] = 1.0
        selh[64 + h, h, :] = 1.0
    c["c_selh"] = selh.reshape(128, 64 * 128)
    c["c_ones"] = np.ones((128, 128), np.float32)
    bd = np.zeros((128, 128), np.float32); bd[:64, :64] = 1.0; bd[64:, 64:] = 1.0
    c["c_bd"] = bd
    slopes = np.array([2.0 ** (-8.0 * (h + 1) / 32) for h in range(32)], dtype=np.float64)
    key = np.arange(128)[:, None, None]; q = np.arange(128)[None, None, :]
    dist_o = (q - key).astype(np.float64)
    ebo = np.where(dist_o >= 0, np.exp(-slopes[None, :, None] * dist_o), 0.0)
    dist_p = (q + 128 - key).astype(np.float64)
    ebp = np.where(dist_p < 128, np.exp(-slopes[None, :, None] * dist_p), 0.0)
    c["c_ebo"] = np.ascontiguousarray(ebo.reshape(128, 4096).astype(np.float32))
    c["c_ebp"] = np.ascontiguousarray(ebp.reshape(128, 4096).astype(np.float32))
    return c


_NC_CACHE = {}


def kernel(x, ffn1_norm, ffn1_w_gate, ffn1_w_up, ffn1_w_down, mix_norm, w_in, conv_w, conv_b,
           dt_bias, a_log, d_skip, ssm_norm, q_norm, k_norm, sinks, w_o_ssm, w_o_attn, w_out,
           ffn2_norm, ffn2_w_gate, ffn2_w_up, ffn2_w_down):
    f = lambda a: np.asarray(a, dtype=np.float32)
    x = f(x)
    shared = dict(_consts())

    def ffn_w(wg, wu, wd, pre):
        g = _tile_w(f(wg)[0], 16); u = _tile_w(f(wu)[0], 16)
        shared[pre + "_gu"] = np.ascontiguousarray(np.stack([g, u], axis=2))
        d = f(wd)[0].reshape(11, 4, 128, 4, 4, 128).transpose(0, 3, 2, 4, 1, 5)
        shared[pre + "_d"] = np.ascontiguousarray(d)
    ffn_w(ffn1_w_gate, ffn1_w_up, ffn1_w_down, "f1")
    ffn_w(ffn2_w_gate, ffn2_w_up, ffn2_w_down, "f2")
    shared["n_f1"] = _vec_pk(f(ffn1_norm)[0], 16); shared["n_f2"] = _vec_pk(f(ffn2_norm)[0], 16); shared["n_mix"] = _vec_pk(f(mix_norm)[0], 16)
    W = f(w_in)[0]
    o = 0
    seg = {}
    for nm, sz in [("z", 4096), ("xbc", 6144), ("dt", 64), ("q", 2048), ("k", 256), ("v", 256), ("gs", 2048), ("ga", 2048)]:
        seg[nm] = W[:, o:o + sz]; o += sz
    shared["w_z"] = _tile_w(seg["z"], 16); shared["w_xbc"] = _tile_w(seg["xbc"], 16)
    shared["w_dt"] = np.ascontiguousarray(seg["dt"].reshape(16, 128, 64).transpose(1, 0, 2))
    shared["w_q"] = _tile_w(seg["q"], 16)
    shared["w_kv"] = _tile_w(np.concatenate([seg["k"], seg["v"]], axis=1), 16)
    shared["w_gs"] = _tile_w(seg["gs"], 16); shared["w_ga"] = _tile_w(seg["ga"], 16)
    shared["conv_w"] = np.ascontiguousarray(f(conv_w)[0].reshape(4, 48, 128).transpose(2, 1, 0))
    shared["conv_b"] = _vec_pk(f(conv_b)[0], 48)
    bc = lambda v: np.ascontiguousarray(np.broadcast_to(f(v)[0][None, :], (128, f(v).shape[1])))
    shared["dtb_bc"] = bc(dt_bias); shared["alog_bc"] = bc(a_log); shared["dsk_bc"] = bc(d_skip); shared["sinks_bc"] = bc(sinks)
    shared["ssm_n"] = _vec_pk(f(ssm_norm)[0], 32)
    shared["qn"] = np.ascontiguousarray(np.tile(f(q_norm)[0], 2)[:, None]); shared["kn"] = np.ascontiguousarray(np.tile(f(k_norm)[0], 2)[:, None])
    shared["w_os"] = _tile_w(f(w_o_ssm)[0], 32); shared["w_oa"] = _tile_w(f(w_o_attn)[0], 16); shared["w_out"] = _tile_w(f(w_out)[0], 16)
    in_maps = []
    for c in range(8):
        b, hf = c // 2, c % 2
        m = dict(shared)
        m["xT_own"] = np.ascontiguousarray(x[b, hf * T:(hf + 1) * T, :].T)
        m["xT_pre"] = np.ascontiguousarray(x[b, 0:T, :].T) if hf == 1 else np.zeros((2048, T), np.float32)
        m["mflag"] = np.full((128, 1), float(hf), np.float32)
        in_maps.append(m)
    if "nc" not in _NC_CACHE:
        _NC_CACHE["nc"] = build_nc()
    res = run_bass_kernel_spmd(_NC_CACHE["nc"], in_maps, core_ids=list(range(8)))
    out = np.empty((4, 2048, 2048), np.float32)
    for c in range(8):
        b, hf = c // 2, c % 2
        out[b, hf * T:(hf + 1) * T, :] = res.results[c]["oT"].T
    return out
```
